# Optimizing a Trainium2 kernel written in Bass

```python
import jax, jax.numpy as jnp
from jax import lax
import numpy as np

D_MODEL = 1024
BATCH = 16
SEQ = 4096
DEPTH = 1
DEC_BATCH = 4
DEC_SEQ = 8192
PAST_LEN = 128

N_MEM = 256
MIX_WIDTH = D_MODEL
RET_WIDTH = MIX_WIDTH // 2
RET_HEADS = 4
RET_HEAD_DIM = RET_WIDTH // RET_HEADS
RET_CHUNK = 128
SGU_WIDTH = MIX_WIDTH - RET_WIDTH
SGU_GROUPS = 4
SGU_GROUP_DIM = SGU_WIDTH // SGU_GROUPS
SGU_CHUNK = 128
XA_HEADS = 4
XA_HEAD_DIM = D_MODEL // XA_HEADS
D_FF = ((-(-8 * D_MODEL // 3)) + 255) // 256 * 256
W_IN_COLS = 4 * RET_WIDTH + 2 * SGU_WIDTH
ROPE_BASE = 10000.0
EPS = 1e-6
N_NORMS = 7
NORM_PRE_MIX, NORM_POST_MIX, NORM_PRE_XA, NORM_POST_XA, NORM_MEM, NORM_PRE_FFN, NORM_POST_FFN = range(N_NORMS)

kernel_name = "hybrid_retention_sgu_encoder"


def rms_norm(x, w):
    xf = x.astype(jnp.float32)
    y = xf * lax.rsqrt(jnp.mean(xf * xf, axis=-1, keepdims=True) + EPS)
    return (y * w.astype(jnp.float32)).astype(x.dtype)


def layer_norm_nobias(x, w):
    xf = x.astype(jnp.float32)
    mu = jnp.mean(xf, axis=-1, keepdims=True)
    var = jnp.mean(jnp.square(xf - mu), axis=-1, keepdims=True)
    return ((xf - mu) * lax.rsqrt(var + EPS) * w.astype(jnp.float32)).astype(x.dtype)


def rotary(x):
    s, dh = x.shape[1], x.shape[3]
    half = dh // 2
    inv = ROPE_BASE ** (-jnp.arange(half, dtype=jnp.float32) / half)
    ang = jnp.arange(s, dtype=jnp.float32)[:, None] * inv[None, :]
    cos = jnp.cos(ang)[None, :, None, :]
    sin = jnp.sin(ang)[None, :, None, :]
    xf = x.astype(jnp.float32)
    x1, x2 = xf[..., :half], xf[..., half:]
    return jnp.concatenate([x1 * cos - x2 * sin, x1 * sin + x2 * cos], axis=-1)


def decay_scan(contrib, decay_chunk, reverse):
    def step(state, c):
        return decay_chunk[None, :, None, None] * state + c, state
    _, states = lax.scan(step, jnp.zeros_like(contrib[0]), contrib, reverse=reverse)
    return states


def bidir_retention(q, k, v, log_gamma):
    b, s, h, d = q.shape
    c = RET_CHUNK
    nc = s // c
    lg = log_gamma.astype(jnp.float32)
    lf, lb = lg[0], lg[1]
    idx = jnp.arange(c, dtype=jnp.float32)
    qc = q.reshape(b, nc, c, h, d)
    kc = k.reshape(b, nc, c, h, d)
    vc = v.reshape(b, nc, c, h, d)
    dist = idx[:, None] - idx[None, :]
    adist = jnp.abs(dist)
    d_intra = jnp.where(dist[None] >= 0, jnp.exp(lf[:, None, None] * adist[None]),
                        jnp.exp(lb[:, None, None] * adist[None]))
    scores = jnp.einsum('bcihd,bcjhd->bchij', qc, kc) * d_intra[None, None]
    y = jnp.einsum('bchij,bcjhe->bcihe', scores, vc)
    kf = kc * jnp.exp(lf[None, :] * (c - 1 - idx)[:, None])[None, None, :, :, None]
    w_f = jnp.einsum('bcjhd,bcjhe->cbhde', kf, vc)
    r_f = decay_scan(w_f, jnp.exp(lf * c), reverse=False)
    qf = qc * jnp.exp(lf[None, :] * (idx + 1.0)[:, None])[None, None, :, :, None]
    y = y + jnp.einsum('bcihd,cbhde->bcihe', qf, r_f)
    kb = kc * jnp.exp(lb[None, :] * idx[:, None])[None, None, :, :, None]
    w_b = jnp.einsum('bcjhd,bcjhe->cbhde', kb, vc)
    r_b = decay_scan(w_b, jnp.exp(lb * c), reverse=True)
    qb = qc * jnp.exp(lb[None, :] * (c - idx)[:, None])[None, None, :, :, None]
    y = y + jnp.einsum('bcihd,cbhde->bcihe', qb, r_b)
    return y.reshape(b, s, h, d)


def token_mixer(h, w_in, ret_log_gamma, ret_gn_w, sgu_norm_w, sgu_w, sgu_b, w_out):
    b, s, _ = h.shape
    z = h @ w_in
    q, k, v, g, u, vs = jnp.split(z, [RET_WIDTH, 2 * RET_WIDTH, 3 * RET_WIDTH, 4 * RET_WIDTH,
                                      4 * RET_WIDTH + SGU_WIDTH], axis=-1)
    q = rotary(q.reshape(b, s, RET_HEADS, RET_HEAD_DIM))
    k = rotary(k.reshape(b, s, RET_HEADS, RET_HEAD_DIM)) * (RET_HEAD_DIM ** -0.5)
    vh = v.reshape(b, s, RET_HEADS, RET_HEAD_DIM).astype(jnp.float32)
    y = bidir_retention(q, k, vh, ret_log_gamma)
    mu = jnp.mean(y, axis=-1, keepdims=True)
    var = jnp.mean(jnp.square(y - mu), axis=-1, keepdims=True)
    y = ((y - mu) * lax.rsqrt(var + EPS)).reshape(b, s, RET_WIDTH) * ret_gn_w.astype(jnp.float32)
    ret_out = (jax.nn.silu(g.astype(jnp.float32)) * y).astype(h.dtype)
    u = jax.nn.gelu(u)
    vs = layer_norm_nobias(jax.nn.gelu(vs), sgu_norm_w)
    nc = s // SGU_CHUNK
    vs = vs.reshape(b, nc, SGU_CHUNK, SGU_GROUPS, SGU_GROUP_DIM)
    sp = jnp.einsum('gpq,bcqgd->bcpgd', sgu_w, vs) + jnp.transpose(sgu_b)[None, None, :, :, None]
    sgu_out = u * sp.reshape(b, s, SGU_WIDTH)
    mixed = jnp.concatenate([ret_out, sgu_out.astype(h.dtype)], axis=-1)
    return mixed @ w_out


def memory_cross_attention(h, mem_n, wq, wkv, wo):
    b, s, _ = h.shape
    m = mem_n.shape[1]
    q = (h @ wq).reshape(b, s, XA_HEADS, XA_HEAD_DIM)
    kv = mem_n @ wkv
    k = kv[..., :D_MODEL].reshape(b, m, XA_HEADS, XA_HEAD_DIM)
    v = kv[..., D_MODEL:].reshape(b, m, XA_HEADS, XA_HEAD_DIM)
    sc = jnp.einsum('bshd,bmhd->bhsm', q, k).astype(jnp.float32) * (XA_HEAD_DIM ** -0.5)
    p = jax.nn.softmax(sc, axis=-1).astype(v.dtype)
    o = jnp.einsum('bhsm,bmhd->bshd', p, v).reshape(b, s, D_MODEL)
    return o @ wo


def swiglu(h, w_gu, w_down):
    gu = h @ w_gu
    gate, up = gu[..., :D_FF], gu[..., D_FF:]
    return (jax.nn.silu(gate) * up) @ w_down


def encoder_layer(x, mem, nw, w_in, ret_log_gamma, ret_gn_w, sgu_norm_w, sgu_w, sgu_b, w_out,
                  xa_wq, xa_wkv, xa_wo, ffn_w_gu, ffn_w_down):
    mix = token_mixer(rms_norm(x, nw[NORM_PRE_MIX]), w_in, ret_log_gamma, ret_gn_w,
                      sgu_norm_w, sgu_w, sgu_b, w_out)
    x = x + rms_norm(mix, nw[NORM_POST_MIX])
    mem_n = rms_norm(mem, nw[NORM_MEM])
    xa = memory_cross_attention(rms_norm(x, nw[NORM_PRE_XA]), mem_n, xa_wq, xa_wkv, xa_wo)
    x = x + rms_norm(xa, nw[NORM_POST_XA])
    ff = swiglu(rms_norm(x, nw[NORM_PRE_FFN]), ffn_w_gu, ffn_w_down)
    return x + rms_norm(ff, nw[NORM_POST_FFN])


def setup_inputs(seed: int = 0) -> dict:
    key = jax.random.key(seed)
    ks = jax.random.split(key, 20)
    nrm = jax.random.normal
    f32 = jnp.float32
    base_lg = jnp.log(1.0 - 2.0 ** (-5.0 - jnp.arange(RET_HEADS, dtype=f32)))
    return {
        "x_prompt": nrm(ks[0], (BATCH, SEQ, D_MODEL), f32),
        "x_sample": nrm(ks[1], (DEC_BATCH, DEC_SEQ, D_MODEL), f32),
        "mem_prompt": nrm(ks[2], (BATCH, N_MEM, D_MODEL), f32),
        "mem_sample": nrm(ks[3], (DEC_BATCH, N_MEM, D_MODEL), f32),
        "norm_w": 1.0 + 0.05 * nrm(ks[4], (DEPTH, N_NORMS, D_MODEL), f32),
        "w_in": nrm(ks[5], (DEPTH, D_MODEL, W_IN_COLS), f32) * D_MODEL ** -0.5,
        "ret_log_gamma": base_lg[None, None, :] * jnp.exp(0.1 * nrm(ks[6], (DEPTH, 2, RET_HEADS), f32)),
        "ret_gn_w": 1.0 + 0.05 * nrm(ks[7], (DEPTH, RET_WIDTH), f32),
        "sgu_norm_w": 1.0 + 0.05 * nrm(ks[8], (DEPTH, SGU_WIDTH), f32),
        "sgu_w": nrm(ks[9], (DEPTH, SGU_GROUPS, SGU_CHUNK, SGU_CHUNK), f32) * SGU_CHUNK ** -0.5,
        "sgu_b": 1.0 + 0.1 * nrm(ks[10], (DEPTH, SGU_GROUPS, SGU_CHUNK), f32),
        "w_out": nrm(ks[11], (DEPTH, MIX_WIDTH, D_MODEL), f32) * MIX_WIDTH ** -0.5,
        "xa_wq": nrm(ks[12], (DEPTH, D_MODEL, D_MODEL), f32) * D_MODEL ** -0.5,
        "xa_wkv": nrm(ks[13], (DEPTH, D_MODEL, 2 * D_MODEL), f32) * D_MODEL ** -0.5,
        "xa_wo": nrm(ks[14], (DEPTH, D_MODEL, D_MODEL), f32) * D_MODEL ** -0.5,
        "ffn_w_gu": nrm(ks[15], (DEPTH, D_MODEL, 2 * D_FF), f32) * D_MODEL ** -0.5,
        "ffn_w_down": nrm(ks[16], (DEPTH, D_FF, D_MODEL), f32) * D_FF ** -0.5,
    }


def reference(x_prompt, x_sample, mem_prompt, mem_sample, norm_w, w_in, ret_log_gamma, ret_gn_w,
              sgu_norm_w, sgu_w, sgu_b, w_out, xa_wq, xa_wkv, xa_wo, ffn_w_gu, ffn_w_down):
    y_prompt = x_prompt
    y_sample = x_sample
    for l in range(DEPTH):
        y_prompt = encoder_layer(y_prompt, mem_prompt, norm_w[l], w_in[l], ret_log_gamma[l], ret_gn_w[l],
                                 sgu_norm_w[l], sgu_w[l], sgu_b[l], w_out[l], xa_wq[l], xa_wkv[l], xa_wo[l],
                                 ffn_w_gu[l], ffn_w_down[l])
        y_sample = encoder_layer(y_sample, mem_sample, norm_w[l], w_in[l], ret_log_gamma[l], ret_gn_w[l],
                                 sgu_norm_w[l], sgu_w[l], sgu_b[l], w_out[l], xa_wq[l], xa_wkv[l], xa_wo[l],
                                 ffn_w_gu[l], ffn_w_down[l])
    return (y_prompt, y_sample)
```

```python
import math
from contextlib import ExitStack

import numpy as np
import concourse.bass as bass
import concourse.mybir as mybir
from concourse.bass_utils import run_bass_kernel_spmd

F32 = mybir.dt.float32
BF16 = mybir.dt.bfloat16
I32 = mybir.dt.int32
AF = mybir.ActivationFunctionType
ALU = mybir.AluOpType
AX = mybir.AxisListType

D = 1024
NU = 3
UT = 4096
NCH = UT // 128
DFF = 2816
NF = DFF // 128
EPS = 1e-6
KSCALE = 128 ** -0.5
LN_KSCALE = math.log(KSCALE)
GELU_C = 0.7978845608028654
SEQ_PLAY = [False]
SKIPY = [False]
STOPAT = [9]
XCUT = [100000]
QKTR = [0]
STRICT = [1]
DUMMY = [2]
FSTOP = [9]
ECUT = [1000]


class Res:
    __slots__ = ("name", "last_w", "readers")

    def __init__(self, name):
        self.name = name
        self.last_w = None
        self.readers = {}


class Node:
    __slots__ = ("eng", "fn", "deps", "signal", "token", "dma_key", "idx")


class Prog:
    ENGS = ("pe", "act", "dve", "pool", "sp")

    def __init__(self):
        self.nodes = []
        self.rec = None

    def _grp(self, n):
        return n.dma_key if n.dma_key is not None else "E_" + n.eng

    def _add(self, eng, fn, reads, writes, dma_key=None, after=()):
        n = Node()
        n.eng = eng
        n.fn = fn
        n.signal = dma_key is not None
        n.token = None
        n.dma_key = dma_key
        n.idx = len(self.nodes)
        deps = {}

        def need(d, raw=False):
            dn = self.nodes[d]
            if dn.eng == eng and dn.dma_key is None and dma_key is None:
                if STRICT[0] == 0 and (not raw or eng == "pe"):
                    return
                if STRICT[0] == 1 and eng == "pe":
                    return
            g = self._grp(dn)
            if g not in deps or deps[g] < d:
                deps[g] = d

        for a in after:
            need(a.idx, True)
        for r in reads:
            if r.last_w is not None:
                need(r.last_w, True)
        for w in writes:
            if w.last_w is not None:
                need(w.last_w)
            for rd in w.readers.values():
                need(rd)
        n.deps = sorted(deps.values())
        for d in n.deps:
            self.nodes[d].signal = True
        self.nodes.append(n)
        g = self._grp(n)
        for r in reads:
            r.readers[g] = n.idx
        for w in writes:
            w.last_w = n.idx
            w.readers = {}
        return n

    def op(self, eng, fn, reads=(), writes=(), after=()):
        if self.rec is not None:
            self.rec.append((eng, fn, list(reads), list(writes), None, tuple(after)))
            return None
        return self._add(eng, fn, reads, writes, after=after)

    def dma(self, eng, fn, reads=(), writes=(), key=None, after=()):
        if self.rec is not None:
            self.rec.append((eng, fn, list(reads), list(writes), key, tuple(after)))
            return None
        return self._add(eng, fn, reads, writes, dma_key=key, after=after)

    def record(self, body):
        self.rec = []
        body()
        lst, self.rec = self.rec, None
        return lst

    def play(self, a, b=()):
        i = j = 0
        if SEQ_PLAY[0]:
            for t in list(a) + list(b):
                self._add(*t)
            return
        while i < len(a) or j < len(b):
            if j >= len(b) or (i < len(a) and i * len(b) <= j * len(a)):
                self._add(*a[i]); i += 1
            else:
                self._add(*b[j]); j += 1

    def emit(self, nc):
        cnt = {}
        keys = []
        for n in self.nodes:
            if not n.signal:
                continue
            k = self._grp(n)
            if k not in cnt:
                cnt[k] = 0
                keys.append(k)
            cnt[k] += 16 if n.dma_key is not None else 1
            n.token = (k, cnt[k])
        final = [(k, cnt[k]) for k in keys if not k.startswith("E_")]
        per_eng = {e: [n for n in self.nodes if n.eng == e] for e in self.ENGS}
        nodes = self.nodes
        with ExitStack() as st:
            sems = {k: st.enter_context(nc.semaphore("s_" + k)) for k in keys}
            block = st.enter_context(nc.Block())

            def run(lst, do_final):
                def body(eng):
                    waited = {}
                    for n in lst:
                        for d in n.deps:
                            k, v = nodes[d].token
                            if waited.get(k, 0) < v:
                                eng.wait_ge(sems[k], v)
                                waited[k] = v
                        ins = n.fn(eng)
                        if n.signal:
                            ins.then_inc(sems[n.token[0]], 16 if n.dma_key is not None else 1)
                    if do_final:
                        for k, v in final:
                            if waited.get(k, 0) < v:
                                eng.wait_ge(sems[k], v)
                return body

            block.tensor(run(per_eng["pe"], False))
            block.scalar(run(per_eng["act"], False))
            block.vector(run(per_eng["dve"], False))
            block.gpsimd(run(per_eng["pool"], False))
            block.sync(run(per_eng["sp"], True))


def build_program(debug=False):
    nc = bass.Bass("TRN2", target_bir_lowering=False)
    P = Prog()

    def din(name, shape, dt=F32):
        return nc.dram_tensor(name, shape, dt, kind="ExternalInput").ap()

    def dscr(name, shape, dt):
        return nc.dram_tensor(name, shape, dt, kind="ExternalOutput" if debug else "Internal").ap()

    xs = din("xs", [NU, UT, D])
    mems = din("mems", [NU, 256, D])
    meta = din("meta", [128, 4])
    norm_w = din("norm_w", [7, D])
    w_in = din("w_in", [D, 3072])
    lg_d = din("ret_log_gamma", [8])
    gnw_d = din("ret_gn_w", [512])
    sgunw_d = din("sgu_norm_w", [512])
    sguw_d = din("sgu_w", [4, 128, 128])
    sgub_d = din("sgu_b", [4, 128])
    w_out = din("w_out", [D, D])
    wq_d = din("xa_wq", [D, D])
    wkv_d = din("xa_wkv", [D, 2 * D])
    wo_d = din("xa_wo", [D, D])
    wgu_d = din("ffn_w_gu", [D, 2 * DFF])
    wdn_d = din("ffn_w_down", [DFF, D])
    outp = nc.dram_tensor("out", [NU, UT, D], F32, kind="ExternalOutput").ap()

    s_win = dscr("s_win", [D, 3072], BF16)
    s_wout = dscr("s_wout", [D, D], BF16)
    s_wq = dscr("s_wq", [D, D], BF16)
    s_wkv = dscr("s_wkv", [D, 2 * D], BF16)
    s_wo = dscr("s_wo", [D, D], BF16)
    s_wgu = dscr("s_wgu", [NF, 128, 8, 2, 128], BF16)
    s_wdn = dscr("s_wdn", [DFF, D], BF16)
    s_rb = dscr("s_rb", [NU, NCH, 128, 512], BF16)
    s_kv = dscr("s_kv", [NU, NCH, 128, 1024], BF16)
    s_hT = dscr("s_hT", [NU, NCH, 128, 1024], BF16)
    if debug:
        s_x2 = nc.dram_tensor("dbg_x2", [NU, UT, D], F32, kind="ExternalOutput").ap()
        dbg_x1 = nc.dram_tensor("dbg_x1", [NU, UT, D], F32, kind="ExternalOutput").ap()
    else:
        s_x2 = dscr("s_x2", [NU, UT, D], F32)

    st = ExitStack()
    with st:
        def sb(name, shape, dt):
            return st.enter_context(nc.sbuf_tensor(name, shape, dt))

        WA = sb("WA", [128, 49152], BF16)
        win_v = WA[:, 0:24576].rearrange("p (k c) -> p k c", k=8)
        wout_v = WA[:, 24576:32768].rearrange("p (k c) -> p k c", k=8)
        wq_v = WA[:, 32768:40960].rearrange("p (k c) -> p k c", k=8)
        wo_v = WA[:, 40960:49152].rearrange("p (k c) -> p k c", k=8)
        r_win = [Res("win%d" % i) for i in range(2)]
        r_wout, r_wq, r_wo = Res("wout"), Res("wq"), Res("wo")
        act_v = WA[:, 0:11264].rearrange("p (f t) -> p f t", f=NF)
        h3T_vs = [WA[:, 11264 + i * 4096:15360 + i * 4096].rearrange("p (k t) -> p k t", k=8) for i in range(2)]
        wgu_ring = [WA[:, 19456 + i * 2048: 19456 + (i + 1) * 2048].rearrange("p (k g c) -> p k g c", k=8, g=2) for i in range(3)]
        wdn_ring = [WA[:, 25600 + i * 2048: 25600 + (i + 1) * 2048].rearrange("p (f c) -> p f c", f=2) for i in range(3)]
        r_act = [Res("act%d" % f) for f in range(NF)]
        r_h3Ts = [[Res("h3T%d_%d" % (i, c)) for c in range(4)] for i in range(2)]
        r_wgur = [Res("wgur%d" % i) for i in range(3)]
        r_wdnr = [Res("wdnr%d" % i) for i in range(3)]

        xc = [sb("xc%d" % i, [128, D], F32) for i in range(3)]
        r_xc = [Res("xc0"), Res("xc1"), Res("xc2")]
        xress = [WA[:, 31744 + i * 8192:39936 + i * 8192].bitcast(F32).rearrange("p (c d) -> p c d", c=4) for i in range(2)]
        r_xress = [[Res("xres%d_%d" % (i, c)) for c in range(4)] for i in range(2)]
        TA = sb("TA", [128, D], F32); r_TA = Res("TA")
        TB = sb("TB", [128, D], F32); r_TB = Res("TB")
        TC = sb("TC", [128, D], F32); r_TC = Res("TC")
        TD = sb("TD", [128, D], F32); r_TD = Res("TD")
        hb = sb("hb", [128, D], BF16); r_hb = Res("hb")
        hT = sb("hT", [128, 8, 128], BF16); r_hT = Res("hT")
        qk = sb("qk", [128, D], BF16); r_qk = Res("qk")
        qkT = sb("qkT", [128, 8, 128], BF16); r_qkT = Res("qkT")
        qfT = sb("qfT", [128, 4, 128], BF16); r_qfT = Res("qfT")
        qbT = sb("qbT", [128, 4, 128], BF16); r_qbT = Res("qbT")
        vb = sb("vb", [128, 512], BF16); r_vb = Res("vb")
        kfb = sb("kfb", [128, 512], BF16); r_kfb = Res("kfb")
        sTm = sb("sTm", [128, 4, 128], BF16); r_sTm = Res("sTm")
        Rf = sb("Rf", [128, 512], BF16); r_Rf = Res("Rf")
        Sst = sb("Sst", [128, 512], F32); r_S = Res("S")
        Rb = [sb("Rb%d" % i, [128, 512], BF16) for i in range(2)]
        r_Rb = [Res("Rb0"), Res("Rb1")]
        sg = sb("sg", [128, 512], F32); r_sg = Res("sg")
        vsn = sb("vsn", [128, 512], BF16); r_vsn = Res("vsn")
        mixed = sb("mixed", [128, D], BF16); r_mixed = Res("mixed")
        mixedT = sb("mixedT", [128, 8, 128], BF16); r_mixedT = Res("mixedT")
        YB = sb("YB", [128, 2048], BF16)
        Pm = YB[:, 0:1024]; r_Pm = Res("Pm")
        pT = YB[:, 1024:2048].rearrange("p (k t) -> p k t", k=8); r_pT = Res("pT")
        memT = YB[:, :].rearrange("p (k m) -> p k m", k=8)
        hb2 = sb("hb2", [128, D], BF16); r_hb2 = Res("hb2")
        hT2 = sb("hT2", [128, 8, 128], BF16); r_hT2 = Res("hT2")
        qx = sb("qx", [128, D], BF16); r_qx = Res("qx")
        qxT = sb("qxT", [128, 8, 128], BF16); r_qxT = Res("qxT")
        smY = sb("smallY", [128, 64], F32); r_smY = Res("smallY")
        memkT = sb("memkT", [128, 8, 256], BF16); r_memkT = Res("memkT")
        memv = sb("memv", [128, 2, D], BF16); r_memv = Res("memv")
        cosT = sb("cosT", [128, NCH, 64], F32)
        sinT = sb("sinT", [128, NCH, 64], F32)
        r_rope = Res("rope")
        NWp = [sb("NWp%d" % i, [128, D], F32) for i in range(2)]
        NWp.append(NWp[0])
        r_NWp = [Res("NWp0"), Res("NWp1")]
        r_NWp.append(r_NWp[0])
        NWs = sb("NWs", [128, 512], F32)
        DT = sb("DT", [128, 4, 128], F32)
        Ab = sb("Ab", [128, 4, 128], F32)
        Bb = sb("Bb", [128, 4, 128], F32)
        KF = sb("KF", [128, 4, 128], F32)
        r_tab = Res("tab")
        sguwT = sb("sguwT", [128, 4, 128], BF16)
        ident = sb("ident", [128, 128], BF16)
        identf = sb("identf", [128, 128], F32)
        sm = sb("small", [128, 64], F32)
        r_sm = Res("small")
        cs = sb("const", [128, 96], F32)
        nwT = sb("nwT", [128, 64], F32)
        invf = sb("invf", [128, 64], F32)
        posb = sb("posb", [128, NCH], F32)

        PS = st.enter_context(nc.psum_tensor("PS", [128, 4096], F32))
        r_bank = [Res("bank%d" % b) for b in range(8)]

        def bank(b, n=1):
            return PS[:, b * 512:(b + n) * 512]

        def bankbf(b):
            return PS[:, b * 512:(b + 1) * 512].bitcast(BF16)

        C_PIDX, C_M05, C_LG, C_KFS, C_KBS, C_GF, C_GB, C_T, C_SGUB, C_META, C_HALF = 0, 1, 8, 16, 20, 24, 28, 32, 40, 48, 56

        ctr = [0]

        def key(prefix):
            ctr[0] += 1
            return "%s%d" % (prefix, ctr[0])

        def rstd_ops(ms_col, out_col, eps, smt=None, r_smt=None):
            smt = sm if smt is None else smt
            r_smt = r_sm if r_smt is None else r_smt
            P.op("pool", lambda e: e.tensor_scalar(smt[:, out_col:out_col + 1], smt[:, ms_col:ms_col + 1], eps, None, ALU.add),
                 reads=[r_smt], writes=[r_smt])
            P.op("pool", lambda e: e.tensor_tensor(smt[:, out_col:out_col + 1], smt[:, out_col:out_col + 1], cs[:, C_M05:C_M05 + 1], ALU.pow),
                 reads=[r_smt], writes=[r_smt])

        def transposes(src, r_src, dst_views, r_dst, tb, nblk=8, evac="act"):
            tr = bankbf(tb)
            for j in range(nblk):
                P.op("pe", lambda e, j=j: e.transpose(tr[:, j * 128:(j + 1) * 128], src[:, j * 128:(j + 1) * 128], ident[:]),
                     reads=[r_src], writes=[r_bank[tb]])
            for (dst, lo, hi, eng) in dst_views:
                if eng == "act":
                    P.op("act", lambda e, dst=dst, lo=lo, hi=hi: e.activation(dst, tr[:, lo * 128:hi * 128].rearrange("p (k t) -> p k t", t=128), AF.Copy),
                         reads=[r_bank[tb]], writes=r_dst)
                else:
                    P.op("dve", lambda e, dst=dst, lo=lo, hi=hi: e.tensor_copy(dst, tr[:, lo * 128:hi * 128].rearrange("p (k t) -> p k t", t=128)),
                         reads=[r_bank[tb]], writes=r_dst)

        def prenorm(src, r_src, dstb, r_dstb, slot, junk=None, r_junk=None, smt=None, r_smt=None):
            junk = TD[:] if junk is None else junk
            r_junk = r_TD if r_junk is None else r_junk
            smt_ = sm if smt is None else smt
            r_smt_ = r_sm if r_smt is None else r_smt
            P.op("act", lambda e: e.activation(junk, src, AF.Square, scale=1.0 / 32.0, accum_out=smt_[:, slot:slot + 1]),
                 reads=r_src, writes=[r_junk, r_smt_])
            rstd_ops(slot, slot + 1, EPS, smt_, r_smt_)
            P.op("act", lambda e: e.activation(dstb, src, AF.Copy, scale=smt_[:, slot + 1:slot + 2]),
                 reads=r_src + [r_smt_], writes=r_dstb)

        def postnorm_residual(b0, nwp, r_nwp, xtile, r_x, slot, out_tile=None, r_out=None, junk=None, r_junk=None,
                              smt=None, r_smt=None, tt=None, r_tt=None):
            junk = TD[:] if junk is None else junk
            r_junk = r_TD if r_junk is None else r_junk
            smt_ = sm if smt is None else smt
            r_smt_ = r_sm if r_smt is None else r_smt
            tt_ = TA if tt is None else tt
            r_tt_ = r_TA if r_tt is None else r_tt
            o = bank(b0, 2)
            P.op("act", lambda e: e.activation(junk, o, AF.Square, scale=1.0 / 32.0, accum_out=smt_[:, slot:slot + 1]),
                 reads=[r_bank[b0], r_bank[b0 + 1]], writes=[r_junk, r_smt_])
            rstd_ops(slot, slot + 1, EPS, smt_, r_smt_)
            P.op("dve", lambda e: e.scalar_tensor_tensor(tt_[:], o, smt_[:, slot + 1:slot + 2], nwp[:], ALU.mult, ALU.mult),
                 reads=[r_bank[b0], r_bank[b0 + 1], r_smt_, r_nwp], writes=[r_tt_])
            ot = xtile if out_tile is None else out_tile
            ro = r_x if r_out is None else r_out
            P.op("dve", lambda e: e.tensor_tensor(ot, tt_[:], xtile, ALU.add),
                 reads=[r_tt_] + r_x, writes=ro)

        with nc.allow_non_contiguous_dma(reason="tiny setup loads"):
            P.dma("sp", lambda e: e.dma_start(out=cs[:, C_LG:C_LG + 8], in_=lg_d.partition_broadcast(128)), writes=[r_tab], key="c_lg")
            P.dma("sp", lambda e: e.dma_start(out=cs[:, C_META:C_META + 4], in_=meta), writes=[r_tab], key="c_meta")
            P.dma("sp", lambda e: e.dma_start(out=cs[:, C_SGUB:C_SGUB + 4], in_=sgub_d.rearrange("g p -> p g")), writes=[r_tab], key="c_sgub")
            P.dma("sp", lambda e: e.dma_start(out=NWs[:], in_=sgunw_d.partition_broadcast(128)), writes=[r_tab], key="c_nws")
            for i, row in enumerate((1, 3)):
                P.dma("sp", lambda e, i=i, row=row: e.dma_start(out=NWp[i][:], in_=norm_w[row].partition_broadcast(128)), writes=[r_NWp[i]], key="c_nwp%d" % i)
            P.dma("sp", lambda e: e.dma_start(out=TA[0:56, 0:128], in_=norm_w.rearrange("n (k p) -> (n k) p", p=128)), writes=[r_TA], key="c_nw")
            P.dma("sp", lambda e: e.dma_start(out=TA[56:60, 0:128], in_=gnw_d.rearrange("(k p) -> k p", p=128)), writes=[r_TA], key="c_gnw")
            P.dma("sp", lambda e: e.dma_start(out=TB[:, 0:512].rearrange("p (g q) -> p g q", g=4), in_=sguw_d.rearrange("g p q -> p g q")), writes=[r_TB], key="c_sguw")

        P.op("pool", lambda e: e.iota(identf[:], pattern=[[1, 128]], base=0, channel_multiplier=-1, allow_small_or_imprecise_dtypes=True), writes=[r_tab])
        P.op("pool", lambda e: e.iota(cs[:, C_PIDX:C_PIDX + 1], pattern=[[0, 1]], base=0, channel_multiplier=1, allow_small_or_imprecise_dtypes=True), writes=[r_tab])
        P.op("pool", lambda e: e.iota(posb[:], pattern=[[128, NCH]], base=0, channel_multiplier=1, allow_small_or_imprecise_dtypes=True), writes=[r_tab])
        P.op("pool", lambda e: e.iota(TC[:, 0:128], pattern=[[1, 128]], base=0, channel_multiplier=0, allow_small_or_imprecise_dtypes=True), writes=[r_TC])
        P.op("dve", lambda e: e.memset(cs[:, C_M05:C_M05 + 1], -0.5), writes=[r_tab])
        P.op("dve", lambda e: e.memset(cs[:, C_HALF:C_HALF + 1], 0.5), writes=[r_tab])
        P.op("dve", lambda e: e.tensor_scalar(TC[:, 128:256], identf[:], 0.0, None, ALU.max), reads=[r_tab], writes=[r_TC])
        P.op("dve", lambda e: e.tensor_scalar(TC[:, 256:384], identf[:], -1.0, 0.0, ALU.mult, ALU.max), reads=[r_tab], writes=[r_TC])
        P.op("dve", lambda e: e.tensor_scalar(ident[:], identf[:], 0.0, None, ALU.is_equal), reads=[r_tab], writes=[r_tab])
        P.op("dve", lambda e: e.tensor_scalar(identf[:], identf[:], 0.0, None, ALU.is_equal), reads=[r_tab], writes=[r_tab])
        for h in range(4):
            lf = cs[:, C_LG + h:C_LG + h + 1]
            lb = cs[:, C_LG + 4 + h:C_LG + 5 + h]
            P.op("dve", lambda e, lf=lf: e.tensor_scalar(TD[:, 0:128], TC[:, 128:256], lf, None, ALU.mult), reads=[r_TC, r_tab], writes=[r_TD])
            P.op("dve", lambda e, lb=lb: e.scalar_tensor_tensor(TD[:, 0:128], TC[:, 256:384], lb, TD[:, 0:128], ALU.mult, ALU.add), reads=[r_TC, r_tab, r_TD], writes=[r_TD])
            P.op("act", lambda e, h=h: e.activation(DT[:, h, :], TD[:, 0:128], AF.Exp), reads=[r_TD], writes=[r_tab])
            P.op("dve", lambda e, h=h: e.tensor_scalar(DT[:, h, :], DT[:, h, :], KSCALE, None, ALU.mult), reads=[r_tab], writes=[r_tab])
            P.op("dve", lambda e, lf=lf: e.tensor_scalar(TD[:, 128:256], TC[:, 0:128], 1.0, lf, ALU.add, ALU.mult), reads=[r_TC, r_tab], writes=[r_TD])
            P.op("act", lambda e, h=h: e.activation(Ab[:, h, :], TD[:, 128:256], AF.Exp), reads=[r_TD], writes=[r_tab])
            P.op("dve", lambda e: e.tensor_scalar(TD[:, 256:384], TC[:, 0:128], -1.0, 128.0, ALU.mult, ALU.add), reads=[r_TC], writes=[r_TD])
            P.op("dve", lambda e, lb=lb: e.tensor_scalar(TD[:, 256:384], TD[:, 256:384], lb, None, ALU.mult), reads=[r_TD, r_tab], writes=[r_TD])
            P.op("act", lambda e, h=h: e.activation(Bb[:, h, :], TD[:, 256:384], AF.Exp), reads=[r_TD], writes=[r_tab])
        P.op("dve", lambda e: e.tensor_scalar(cs[:, C_T:C_T + 1], cs[:, C_PIDX:C_PIDX + 1], -1.0, 127.0, ALU.mult, ALU.add), reads=[r_tab], writes=[r_tab])
        P.op("dve", lambda e: e.tensor_scalar(cs[:, C_KFS:C_KFS + 4], cs[:, C_LG:C_LG + 4], cs[:, C_T:C_T + 1], None, ALU.mult), reads=[r_tab], writes=[r_tab])
        P.op("dve", lambda e: e.tensor_scalar(cs[:, C_KBS:C_KBS + 4], cs[:, C_LG + 4:C_LG + 8], cs[:, C_PIDX:C_PIDX + 1], None, ALU.mult), reads=[r_tab], writes=[r_tab])
        P.op("dve", lambda e: e.tensor_scalar(cs[:, C_GF:C_GF + 8], cs[:, C_LG:C_LG + 8], 128.0, None, ALU.mult), reads=[r_tab], writes=[r_tab])
        P.op("act", lambda e: e.activation(cs[:, C_KFS:C_KFS + 16], cs[:, C_KFS:C_KFS + 16], AF.Exp), reads=[r_tab], writes=[r_tab])
        P.op("dve", lambda e: e.tensor_scalar(cs[:, C_KFS:C_KFS + 8], cs[:, C_KFS:C_KFS + 8], KSCALE, None, ALU.mult), reads=[r_tab], writes=[r_tab])
        P.op("dve", lambda e: e.tensor_copy(KF[:], cs[:, C_KFS:C_KFS + 4].unsqueeze(2).to_broadcast([128, 4, 128])), reads=[r_tab], writes=[r_tab])
        P.op("dve", lambda e: e.memset(invf[:, 0:1], 1.0), writes=[r_tab])
        for kk in range(6):
            w = 1 << kk
            r = float(np.float32(10000.0 ** (-w / 64.0)))
            P.op("dve", lambda e, w=w, r=r: e.tensor_scalar(invf[:, w:2 * w], invf[:, 0:w], r, None, ALU.mult), reads=[r_tab], writes=[r_tab])
        P.op("pe", lambda e: e.transpose(bank(0)[:, 0:60], TA[0:60, 0:128], identf[0:60, 0:60]), reads=[r_TA, r_tab], writes=[r_bank[0]])
        P.op("dve", lambda e: e.tensor_copy(nwT[:, 0:60], bank(0)[:, 0:60]), reads=[r_bank[0]], writes=[r_tab])
        P.op("dve", lambda e: e.memset(nwT[:, 60:64], 0.5), reads=[r_tab], writes=[r_tab])
        P.op("dve", lambda e: e.tensor_scalar(nwT[:, 56:60], nwT[:, 56:60], 0.5, None, ALU.mult), reads=[r_tab], writes=[r_tab])
        P.op("dve", lambda e: e.tensor_copy(hb[:, 0:512], TB[:, 0:512]), reads=[r_TB], writes=[r_hb])
        for g in range(4):
            P.op("pe", lambda e, g=g: e.transpose(bankbf(1)[:, g * 128:(g + 1) * 128], hb[:, g * 128:(g + 1) * 128], ident[:]), reads=[r_hb, r_tab], writes=[r_bank[1]])
        P.op("dve", lambda e: e.tensor_copy(sguwT[:], bankbf(1)[:, 0:512].rearrange("p (g t) -> p g t", g=4)), reads=[r_bank[1]], writes=[r_tab])

        stage_f = [TA, TB, TC, TD]
        r_stage_f = [r_TA, r_TB, r_TC, r_TD]
        stage_b = [qk, mixed, hb, qx]
        r_stage_b = [r_qk, r_mixed, r_hb, r_qx]
        r_scr = {}
        last_store = {}
        pieces = []

        def nwcol(n):
            return lambda k: nwT[:, n * 8 + k:n * 8 + k + 1]

        def add_plain(src, scr, rows, cols, scale_col, rname):
            for k in range(rows // 128):
                for cb in range(cols // 1024):
                    last = (k == rows // 128 - 1) and (cb == cols // 1024 - 1)
                    pieces.append((src[k * 128:(k + 1) * 128, cb * 1024:(cb + 1) * 1024], 1024, scale_col(k),
                                   (lambda sbb, k=k, cb=cb, scr=scr: [(scr[k * 128:(k + 1) * 128, cb * 1024:(cb + 1) * 1024], sbb[:])]), rname, last))

        add_plain(w_in, s_win, D, 3072, nwcol(0), "s_win")
        add_plain(w_out, s_wout, D, D, lambda k: nwT[:, 56 + k:57 + k], "s_wout")
        add_plain(wq_d, s_wq, D, D, nwcol(2), "s_wq")
        add_plain(wo_d, s_wo, D, D, lambda k: 1.0, "s_wo")
        add_plain(wkv_d, s_wkv, D, 2 * D, nwcol(4), "s_wkv")
        for k in range(8):
            for cb in range(6):
                wcols = 1024 if cb < 5 else 512

                def st(sbb, k=k, cb=cb, wcols=wcols):
                    res = []
                    for j in range(wcols // 128):
                        col = cb * 1024 + j * 128
                        gu, f = (0, col // 128) if col < DFF else (1, (col - DFF) // 128)
                        res.append((s_wgu[f, :, k, gu, :], sbb[:, j * 128:(j + 1) * 128]))
                    return res
                pieces.append((wgu_d[k * 128:(k + 1) * 128, cb * 1024:cb * 1024 + wcols], wcols, nwT[:, 5 * 8 + k:5 * 8 + k + 1], st, "s_wgu", k == 7 and cb == 5))
        add_plain(wdn_d, s_wdn, DFF, D, lambda k: 0.5, "s_wdn")

        NSL, LA = 4, 3
        for t in range(len(pieces) + LA):
            if t < len(pieces):
                src, wcols, sc, stf, rname, last = pieces[t]
                i = t % NSL
                P.dma("sp", lambda e, src=src, wcols=wcols, i=i: e.dma_start(out=stage_f[i][:, 0:wcols], in_=src), writes=[r_stage_f[i]], key="pl%d" % i)
            t2 = t - LA
            if t2 >= 0:
                src, wcols, sc, stf, rname, last = pieces[t2]
                i = t2 % NSL
                sf, rf, sbb, rb = stage_f[i], r_stage_f[i], stage_b[i], r_stage_b[i]
                if t2 % 2 == 0:
                    P.op("act", lambda e, sf=sf, sbb=sbb, sc=sc, wcols=wcols: e.activation(sbb[:, 0:wcols], sf[:, 0:wcols], AF.Copy, scale=sc), reads=[rf, r_tab], writes=[rb])
                else:
                    P.op("dve", lambda e, sf=sf, sbb=sbb, sc=sc, wcols=wcols: e.tensor_scalar(sbb[:, 0:wcols], sf[:, 0:wcols], sc, None, ALU.mult), reads=[rf, r_tab], writes=[rb])
                for (dst, srcv) in stf(sbb):
                    last_store[i] = P.dma("pool", lambda e, dst=dst, srcv=srcv: e.dma_start(out=dst, in_=srcv), reads=[rb], key="ps%d" % i)
                if last:
                    r_scr[rname] = list(last_store.values())

        for i in range(2):
            P.dma("sp", lambda e, i=i: e.dma_start(out=win_v[:, i * 4:(i + 1) * 4, :], in_=s_win[i * 512:(i + 1) * 512, :].rearrange("(k p) c -> p k c", p=128)),
                  after=r_scr["s_win"], writes=[r_win[i]], key="ld_win%d" % i)
        P.dma("sp", lambda e: e.dma_start(out=wout_v, in_=s_wout.rearrange("(k p) c -> p k c", p=128)), after=r_scr["s_wout"], writes=[r_wout], key="ld_wout")
        P.dma("sp", lambda e: e.dma_start(out=wq_v, in_=s_wq.rearrange("(k p) c -> p k c", p=128)), after=r_scr["s_wq"], writes=[r_wq], key="ld_wq")
        P.dma("sp", lambda e: e.dma_start(out=wo_v, in_=s_wo.rearrange("(k p) c -> p k c", p=128)), after=r_scr["s_wo"], writes=[r_wo], key="ld_wo")

        def rope_tables(u):
            off = cs[:, C_META + 1 + u:C_META + 2 + u]
            assert NCH % 16 == 0
            for half in range(NCH // 16):
                c0 = half * 16
                P.op("dve", lambda e, c0=c0, off=off: e.tensor_scalar(sm[:, 32:48], posb[:, c0:c0 + 16], off, None, ALU.add), reads=[r_tab, r_sm], writes=[r_sm])
                angv = TC[:].rearrange("p (c j) -> p c j", j=64)
                P.op("dve", lambda e, angv=angv: e.tensor_tensor(angv, sm[:, 32:48].unsqueeze(2).to_broadcast([128, 16, 64]),
                                                                invf[:].unsqueeze(1).to_broadcast([128, 16, 64]), ALU.mult),
                     reads=[r_sm, r_tab], writes=[r_TC])
                for which, tbl in ((0, sinT), (1, cosT)):
                    shift = 0.0 if which == 0 else 0.25
                    P.op("dve", lambda e, shift=shift: e.tensor_scalar(TD[:], TC[:], 1.0 / (2.0 * math.pi), shift, ALU.mult, ALU.add), reads=[r_TC], writes=[r_TD])
                    P.op("dve", lambda e: e.tensor_copy(TA[:].bitcast(I32), TD[:]), reads=[r_TD], writes=[r_TA])
                    P.op("dve", lambda e: e.tensor_copy(TB[:], TA[:].bitcast(I32)), reads=[r_TA], writes=[r_TB])
                    P.op("dve", lambda e: e.tensor_tensor(TD[:], TD[:], TB[:], ALU.subtract), reads=[r_TD, r_TB], writes=[r_TD])
                    P.op("dve", lambda e: e.tensor_scalar(TB[:], TD[:], 0.5, None, ALU.is_gt), reads=[r_TD], writes=[r_TB])
                    P.op("dve", lambda e: e.tensor_tensor(TD[:], TD[:], TB[:], ALU.subtract), reads=[r_TD, r_TB], writes=[r_TD])
                    P.op("dve", lambda e: e.tensor_scalar(TB[:], TD[:], -0.5, None, ALU.is_lt), reads=[r_TD], writes=[r_TB])
                    P.op("dve", lambda e: e.tensor_tensor(TD[:], TD[:], TB[:], ALU.add), reads=[r_TD, r_TB], writes=[r_TD])
                    P.op("act", lambda e, tbl=tbl, c0=c0: e.activation(tbl[:, c0:c0 + 16, :], TD[:].rearrange("p (c j) -> p c j", j=64), AF.Sin, scale=6.283185),
                         reads=[r_TD], writes=[r_rope])

        def load_x(src_ap, i, q="pool"):
            P.dma(q, lambda e: e.dma_start(out=xc[i][:], in_=src_ap), writes=[r_xc[i]], key="ldx%d_%s" % (i, q))

        def zproj(ncols_blocks):
            for n in ncols_blocks:
                b = 2 + n
                for k in range(8):
                    P.op("pe", lambda e, n=n, k=k, b=b: e.matmul(bank(b), hT[:, k, :], win_v[:, k, n * 512:(n + 1) * 512], start=(k == 0), stop=(k == 7)),
                         reads=[r_hT, r_win[k // 4]], writes=[r_bank[b]])

        def rotary(b0, nblk, c, dst, tA=None, r_tA=None, tB=None, r_tB=None, r_dst=None):
            nh = nblk * 4
            w = nblk * 512
            tA = TA[:, 0:w] if tA is None else tA
            tB = TB[:, 0:w] if tB is None else tB
            r_tA = r_TA if r_tA is None else r_tA
            r_tB = r_TB if r_tB is None else r_tB
            r_dst = r_qk if r_dst is None else r_dst
            z3 = bank(b0, nblk).rearrange("p (g j) -> p g j", j=64)
            z4 = bank(b0, nblk).rearrange("p (h t j) -> p h t j", t=2, j=64)
            A3 = tA.rearrange("p (g j) -> p g j", j=64)
            A4 = tA.rearrange("p (h t j) -> p h t j", t=2, j=64)
            B4 = tB.rearrange("p (h t j) -> p h t j", t=2, j=64)
            d4 = dst.rearrange("p (h t j) -> p h t j", t=2, j=64)
            cosb = cosT[:, c, :].unsqueeze(1).to_broadcast([128, nh * 2, 64])
            sinb = sinT[:, c, :].unsqueeze(1).to_broadcast([128, nh, 64])
            rb = [r_bank[b0 + i] for i in range(nblk)]
            P.op("dve", lambda e: e.tensor_tensor(A3, z3, cosb, ALU.mult), reads=rb + [r_rope], writes=[r_tA])
            P.op("dve", lambda e: e.tensor_tensor(B4[:, :, 0, :], z4[:, :, 1, :], sinb, ALU.mult), reads=rb + [r_rope], writes=[r_tB])
            P.op("dve", lambda e: e.tensor_tensor(B4[:, :, 1, :], z4[:, :, 0, :], sinb, ALU.mult), reads=rb + [r_rope], writes=[r_tB])
            P.op("dve", lambda e: e.tensor_tensor(d4[:, :, 0, :], A4[:, :, 0, :], B4[:, :, 0, :], ALU.subtract), reads=[r_tA, r_tB], writes=[r_dst])
            P.op("dve", lambda e: e.tensor_tensor(d4[:, :, 1, :], A4[:, :, 1, :], B4[:, :, 1, :], ALU.add), reads=[r_tA, r_tB], writes=[r_dst])

        def phase_a(u, first_zero):
            if first_zero:
                P.op("dve", lambda e: e.memset(Sst[:], 0.0), writes=[r_S])
            else:
                P.op("dve", lambda e: e.tensor_scalar(Sst[:], Sst[:], cs[:, C_META:C_META + 1], None, ALU.mult), reads=[r_tab, r_S], writes=[r_S])
            SB = [dict(hb=hb, r_hb=r_hb, hT=hT, r_hT=r_hT, kr=qk[:, 512:1024], r_kr=r_qk, vb=vb[:], r_vb=r_vb, kf=kfb[:], r_kf=r_kfb,
                       tA=TA[:, 0:512], r_tA=r_TA, tB=TB[:, 0:512], r_tB=r_TB, junk=TD[:], r_junk=r_TD, sm=sm, r_sm=r_sm, tr=0, bk=3, bv=4, bw=5),
                  dict(hb=hb2, r_hb=r_hb2, hT=hT2, r_hT=r_hT2, kr=qx[:, 512:1024], r_kr=r_qx, vb=Pm[:, 0:512], r_vb=r_Pm, kf=mixed[:, 0:512], r_kf=r_mixed,
                       tA=TC[:, 0:512], r_tA=r_TC, tB=TC[:, 512:1024], r_tB=r_TC, junk=YB[:, 1024:2048], r_junk=r_pT, sm=smY, r_sm=r_smY, tr=1, bk=6, bv=7, bw=2)]
            load_x(xs[u, (NCH - 1) * 128:NCH * 128, :], 0)
            load_x(xs[u, (NCH - 2) * 128:(NCH - 1) * 128, :], 1)

            def front(c, st):
                B = SB[st]
                prenorm(xc[st][:], [r_xc[st]], B["hb"][:], [B["r_hb"]], 0, junk=B["junk"], r_junk=B["r_junk"], smt=B["sm"], r_smt=B["r_sm"])
                if c - 2 >= 0:
                    load_x(xs[u, (c - 2) * 128:(c - 1) * 128, :], st)
                transposes(B["hb"], B["r_hb"], [(B["hT"][:, 0:4, :], 0, 4, "act"), (B["hT"][:, 4:8, :], 4, 8, "dve")], [B["r_hT"]], B["tr"])
                P.dma("sp", lambda e: e.dma_start(out=s_hT[u, c].rearrange("p (k t) -> p k t", k=8), in_=B["hT"][:]), reads=[B["r_hT"]], writes=[r_scr_h[u][c]], key="sth%d" % st)
                for n, b in ((1, B["bk"]), (2, B["bv"])):
                    for k in range(8):
                        P.op("pe", lambda e, n=n, k=k, b=b: e.matmul(bank(b), B["hT"][:, k, :], win_v[:, k, n * 512:(n + 1) * 512], start=(k == 0), stop=(k == 7)),
                             reads=[B["r_hT"], r_win[k // 4]], writes=[r_bank[b]])
                rotary(B["bk"], 1, c, B["kr"], tA=B["tA"], r_tA=B["r_tA"], tB=B["tB"], r_tB=B["r_tB"], r_dst=B["r_kr"])
                P.op("act", lambda e: e.activation(B["vb"], bank(B["bv"]), AF.Copy), reads=[r_bank[B["bv"]]], writes=[B["r_vb"]])
                P.dma("sp", lambda e: e.dma_start(out=s_kv[u, c][:, 0:512], in_=B["kr"]), reads=[B["r_kr"]], writes=[r_scr_k[u][c]], key="stk%d" % st)
                P.dma("sp", lambda e: e.dma_start(out=s_kv[u, c][:, 512:1024], in_=B["vb"]), reads=[B["r_vb"]], writes=[r_scr_v[u][c]], key="stv%d" % st)
                P.op("dve", lambda e: e.tensor_tensor(B["kf"].rearrange("p (h t) -> p h t", t=128), B["kr"].rearrange("p (h t) -> p h t", t=128),
                                                      cs[:, C_KBS:C_KBS + 4].unsqueeze(2).to_broadcast([128, 4, 128]), ALU.mult),
                     reads=[B["r_kr"], r_tab], writes=[B["r_kf"]])
                for h in range(4):
                    hs = slice(h * 128, (h + 1) * 128)
                    P.op("pe", lambda e, hs=hs: e.matmul(bank(B["bw"])[:, hs], B["kf"][:, hs], B["vb"][:, hs], start=True, stop=True),
                         reads=[B["r_kf"], B["r_vb"]], writes=[r_bank[B["bw"]]])

            def tail(c, st):
                B = SB[st]
                P.op("act", lambda e: e.activation(Rb[st][:], Sst[:], AF.Copy), reads=[r_S], writes=[r_Rb[st]])
                P.dma("sp", lambda e: e.dma_start(out=s_rb[u, c], in_=Rb[st][:]), reads=[r_Rb[st]], writes=[r_scr_rb[u][c]], key="strb%d" % st)
                for h in range(4):
                    hs = slice(h * 128, (h + 1) * 128)
                    P.op("dve", lambda e, h=h, hs=hs: e.scalar_tensor_tensor(Sst[:, hs], Sst[:, hs], cs[:, C_GB + h:C_GB + h + 1], bank(B["bw"])[:, hs], ALU.mult, ALU.add),
                         reads=[r_S, r_tab, r_bank[B["bw"]]], writes=[r_S])

            for p in range(NCH // 2):
                c0 = NCH - 1 - 2 * p
                F0 = P.record(lambda: front(c0, 0))
                F1 = P.record(lambda: front(c0 - 1, 1))
                P.play(F0, F1)
                tail(c0, 0)
                tail(c0 - 1, 1)

        r_scr_rb = [[Res("s_rb%d_%d" % (u, c)) for c in range(NCH)] for u in range(NU)]
        r_scr_k = [[Res("s_k%d_%d" % (u, c)) for c in range(NCH)] for u in range(NU)]
        r_scr_h = [[Res("s_h%d_%d" % (u, c)) for c in range(NCH)] for u in range(NU)]
        r_scr_v = [[Res("s_v%d_%d" % (u, c)) for c in range(NCH)] for u in range(NU)]
        r_scr_x2 = [[Res("s_x2_%d_%d" % (u, c)) for c in range(NCH)] for u in range(NU)]

        def mem_kv(u):
            for mc in range(2):
                i = mc
                load_x(mems[u, mc * 128:(mc + 1) * 128, :], i)
                prenorm(xc[i][:], [r_xc[i]], hb[:], [r_hb], 0)
                tr = bankbf(0)
                for j in range(8):
                    P.op("pe", lambda e, j=j: e.transpose(tr[:, j * 128:(j + 1) * 128], hb[:, j * 128:(j + 1) * 128], ident[:]), reads=[r_hb], writes=[r_bank[0]])
                P.op("act", lambda e, mc=mc: e.activation(memT[:, :, mc * 128:(mc + 1) * 128], tr.rearrange("p (k t) -> p k t", t=128), AF.Copy),
                     reads=[r_bank[0]], writes=[r_Pm, r_pT])
            stg = [TA[:].bitcast(BF16), TB[:].bitcast(BF16)]
            rstg = [r_TA, r_TB]
            for rnd in range(2):
                for k in range(8):
                    i = k % 2
                    P.dma("sp", lambda e, k=k, i=i: e.dma_start(out=stg[i], in_=s_wkv[k * 128:(k + 1) * 128, :]), after=r_scr["s_wkv"], writes=[rstg[i]], key="ldkv%d" % i)
                    for dq in range(4):
                        dc = rnd * 4 + dq
                        P.op("pe", lambda e, k=k, i=i, dq=dq, dc=dc: e.matmul(bank(dq)[:, 0:256], stg[i][:, dc * 128:(dc + 1) * 128], memT[:, k, :], start=(k == 0), stop=(k == 7)),
                             reads=[rstg[i], r_Pm, r_pT], writes=[r_bank[dq]])
                    if rnd == 0:
                        for mc in range(2):
                            for hf in range(2):
                                b = 4 + mc * 2 + hf
                                P.op("pe", lambda e, k=k, i=i, mc=mc, hf=hf, b=b: e.matmul(bank(b), memT[:, k, mc * 128:(mc + 1) * 128], stg[i][:, 1024 + hf * 512:1024 + (hf + 1) * 512],
                                                                                   start=(k == 0), stop=(k == 7)),
                                     reads=[rstg[i], r_Pm, r_pT], writes=[r_bank[b]])
                for dq in range(4):
                    dc = rnd * 4 + dq
                    P.op("act", lambda e, dq=dq, dc=dc: e.activation(memkT[:, dc, :], bank(dq)[:, 0:256], AF.Copy), reads=[r_bank[dq]], writes=[r_memkT])
                if rnd == 0:
                    for mc in range(2):
                        P.op("dve", lambda e, mc=mc: e.tensor_copy(memv[:, mc, :], bank(4 + mc * 2, 2)), reads=[r_bank[4 + mc * 2], r_bank[5 + mc * 2]], writes=[r_memv])

        XB_Q, XB_K, XB_V, XB_G = 2, 3, 4, 5
        YB0 = 6

        def chunk_x(u, c):
            i = c % 3
            j = c % 2
            x = xc[i]
            rx = [r_xc[i]]
            if c + 1 < NCH:
                P.dma("sp", lambda e: e.dma_start(out=Rb[1 - j][:], in_=s_rb[u, c + 1]), reads=[r_scr_rb[u][c + 1]], writes=[r_Rb[1 - j]], key="ldrb%d" % (1 - j))

            def zp(n, b):
                for k in range(8):
                    P.op("pe", lambda e, k=k: e.matmul(bank(b), hT[:, k, :], win_v[:, k, n * 512:(n + 1) * 512], start=(k == 0), stop=(k == 7)),
                         reads=[r_hT, r_win[k // 4]], writes=[r_bank[b]])
            P.dma("sp", lambda e: e.dma_start(out=qk[:, 512:1024], in_=s_kv[u, c][:, 0:512]), reads=[r_scr_k[u][c]], writes=[r_qk], key="ldk")
            P.dma("sp", lambda e: e.dma_start(out=vb[:], in_=s_kv[u, c][:, 512:1024]), reads=[r_scr_v[u][c]], writes=[r_vb], key="ldv")
            zp(0, 2); zp(3, 5); zp(4, 3); zp(5, 4)
            if c + 1 < NCH:
                P.dma("sp", lambda e: e.dma_start(out=hT[:], in_=s_hT[u, c + 1].rearrange("p (k t) -> p k t", k=8)), reads=[r_scr_h[u][c + 1]], writes=[r_hT], key="ldh")
            rotary(2, 1, c, qk[:, 0:512])
            P.op("act", lambda e: e.activation(TB[:, 0:512], bank(5), AF.Tanh, scale=0.5), reads=[r_bank[5]], writes=[r_TB])
            P.op("dve", lambda e: e.scalar_tensor_tensor(sg[:], TB[:, 0:512], 1.0, bank(5), ALU.add, ALU.mult), reads=[r_TB, r_bank[5]], writes=[r_sg])
            P.op("dve", lambda e: e.tensor_tensor(kfb[:], qk[:, 512:1024], KF[:].rearrange("p h e -> p (h e)"), ALU.mult), reads=[r_qk, r_tab], writes=[r_kfb])
            tr = bankbf(QKTR[0])
            for jj in range(8):
                P.op("pe", lambda e, jj=jj: e.transpose(tr[:, jj * 128:(jj + 1) * 128], qk[:, jj * 128:(jj + 1) * 128], ident[:]), reads=[r_qk], writes=[r_bank[QKTR[0]]])
            tq = tr[:, 0:512].rearrange("p (h t) -> p h t", t=128)
            for hh in range(2):
                if DUMMY[0] == 2:
                    P.op("dve", lambda e, hh=hh: e.tensor_copy(qkT[:, hh * 4:(hh + 1) * 4, :], tr[:, hh * 512:(hh + 1) * 512].rearrange("p (k t) -> p k t", t=128)),
                         reads=[r_bank[QKTR[0]]], writes=[r_qkT])
                else:
                    P.op("act", lambda e, hh=hh: e.activation((hT2 if DUMMY[0] else qkT)[:, hh * 4:(hh + 1) * 4, :], tr[:, hh * 512:(hh + 1) * 512].rearrange("p (k t) -> p k t", t=128), AF.Copy),
                         reads=[r_bank[QKTR[0]]], writes=[r_qkT])
            P.op("dve", lambda e: e.tensor_tensor(qfT[:], tq, Ab[:], ALU.mult), reads=[r_bank[QKTR[0]], r_tab], writes=[r_qfT])
            P.op("dve", lambda e: e.tensor_tensor(qbT[:], tq, Bb[:], ALU.mult), reads=[r_bank[QKTR[0]], r_tab], writes=[r_qbT])
            for h in range(4):
                P.op("pe", lambda e, h=h: e.matmul(bank(2)[:, h * 128:(h + 1) * 128], qkT[:, 4 + h, :], qkT[:, h, :], start=True, stop=True),
                     reads=[r_qkT], writes=[r_bank[2]])
            P.op("dve", lambda e: e.tensor_tensor(sTm[:], bank(2).rearrange("p (h t) -> p h t", t=128), DT[:], ALU.mult), reads=[r_bank[2], r_tab], writes=[r_sTm])
            zuv = bank(3, 2)
            ruv = [r_bank[3], r_bank[4]]
            P.op("act", lambda e: e.activation(TB[:], zuv, AF.Square, scale=float(np.sqrt(0.044715))), reads=ruv, writes=[r_TB])
            P.op("dve", lambda e: e.scalar_tensor_tensor(TD[:], TB[:], 1.0, zuv, ALU.add, ALU.mult), reads=[r_TB] + ruv, writes=[r_TD])
            P.op("act", lambda e: e.activation(TB[:], TD[:], AF.Tanh, scale=GELU_C), reads=[r_TD], writes=[r_TB])
            P.op("dve", lambda e: e.scalar_tensor_tensor(TD[:], TB[:], 1.0, zuv, ALU.add, ALU.mult), reads=[r_TB] + ruv, writes=[r_TD])
            for h in range(4):
                hs = slice(h * 128, (h + 1) * 128)
                P.op("pe", lambda e, h=h, hs=hs: e.matmul(bank(5)[:, hs], sTm[:, h, :], vb[:, hs], start=True, stop=False), reads=[r_sTm, r_vb], writes=[r_bank[5]])
                P.op("pe", lambda e, h=h, hs=hs: e.matmul(bank(5)[:, hs], qfT[:, h, :], Rf[:, hs], start=False, stop=False), reads=[r_qfT, r_Rf], writes=[r_bank[5]])
                P.op("pe", lambda e, h=h, hs=hs: e.matmul(bank(5)[:, hs], qbT[:, h, :], Rb[j][:, hs], start=False, stop=True), reads=[r_qbT, r_Rb[j]], writes=[r_bank[5]])
            for h in range(4):
                hs = slice(h * 128, (h + 1) * 128)
                P.op("pe", lambda e, hs=hs: e.matmul(bank(2)[:, hs], kfb[:, hs], vb[:, hs], start=True, stop=True), reads=[r_kfb, r_vb], writes=[r_bank[2]])
            for h in range(4):
                hs = slice(h * 128, (h + 1) * 128)
                P.op("dve", lambda e, h=h, hs=hs: e.scalar_tensor_tensor(Sst[:, hs], Sst[:, hs], cs[:, C_GF + h:C_GF + h + 1], bank(2)[:, hs], ALU.mult, ALU.add),
                     reads=[r_S, r_tab, r_bank[2]], writes=[r_S])
            P.op("act", lambda e: e.activation(Rf[:], Sst[:], AF.Copy), reads=[r_S], writes=[r_Rf])
            P.op("dve", lambda e: e.tensor_reduce(sm[:, 2:3], TD[:, 512:1024], AX.X, ALU.add), reads=[r_TD, r_sm], writes=[r_sm])
            P.op("act", lambda e: e.activation(TB[:, 512:1024], TD[:, 512:1024], AF.Square, accum_out=sm[:, 3:4]), reads=[r_TD, r_sm], writes=[r_TB, r_sm])
            P.op("pool", lambda e: e.tensor_scalar(sm[:, 2:4], sm[:, 2:4], 1.0 / 512.0, None, ALU.mult), reads=[r_sm], writes=[r_sm])
            P.op("pool", lambda e: e.tensor_tensor(sm[:, 4:5], sm[:, 2:3], sm[:, 2:3], ALU.mult), reads=[r_sm], writes=[r_sm])
            P.op("pool", lambda e: e.tensor_tensor(sm[:, 5:6], sm[:, 3:4], sm[:, 4:5], ALU.subtract), reads=[r_sm], writes=[r_sm])
            rstd_ops(5, 6, 4.0 * EPS)
            P.op("dve", lambda e: e.tensor_scalar(TB[:, 512:1024], TD[:, 512:1024], sm[:, 2:3], sm[:, 6:7], ALU.subtract, ALU.mult), reads=[r_TD, r_sm], writes=[r_TB])
            P.op("dve", lambda e: e.tensor_tensor(vsn[:], TB[:, 512:1024], NWs[:], ALU.mult), reads=[r_TB, r_tab], writes=[r_vsn])
            y3 = bank(5).rearrange("p (h t) -> p h t", t=128)
            P.op("dve", lambda e: e.tensor_reduce(sm[:, 8:12], y3, AX.X, ALU.add), reads=[r_bank[5], r_sm], writes=[r_sm])
            P.op("act", lambda e: e.activation(TA[:, 512:1024], bank(5), AF.Square), reads=[r_bank[5]], writes=[r_TA])
            P.op("dve", lambda e: e.tensor_reduce(sm[:, 12:16], TA[:, 512:1024].rearrange("p (h t) -> p h t", t=128), AX.X, ALU.add), reads=[r_TA, r_sm], writes=[r_sm])
            P.op("pool", lambda e: e.tensor_scalar(sm[:, 8:16], sm[:, 8:16], 1.0 / 128.0, None, ALU.mult), reads=[r_sm], writes=[r_sm])
            P.op("pool", lambda e: e.tensor_tensor(sm[:, 16:20], sm[:, 8:12], sm[:, 8:12], ALU.mult), reads=[r_sm], writes=[r_sm])
            P.op("pool", lambda e: e.tensor_tensor(sm[:, 20:24], sm[:, 12:16], sm[:, 16:20], ALU.subtract), reads=[r_sm], writes=[r_sm])
            P.op("pool", lambda e: e.tensor_scalar(sm[:, 20:24], sm[:, 20:24], EPS, None, ALU.add), reads=[r_sm], writes=[r_sm])
            P.op("pool", lambda e: e.tensor_tensor(sm[:, 24:28], sm[:, 20:24], cs[:, C_M05:C_M05 + 1].to_broadcast([128, 4]), ALU.pow), reads=[r_sm, r_tab], writes=[r_sm])
            for h in range(4):
                hs = slice(h * 128, (h + 1) * 128)
                P.op("dve", lambda e, h=h, hs=hs: e.tensor_scalar(TA[:, hs], bank(5)[:, hs], sm[:, 8 + h:9 + h], sm[:, 24 + h:25 + h], ALU.subtract, ALU.mult),
                     reads=[r_bank[5], r_sm], writes=[r_TA])
            P.op("dve", lambda e: e.tensor_tensor(mixed[:, 0:512], TA[:, 0:512], sg[:], ALU.mult), reads=[r_TA, r_sg], writes=[r_mixed])
            for g in range(4):
                gs = slice(g * 128, (g + 1) * 128)
                P.op("pe", lambda e, g=g, gs=gs: e.matmul(bank(2)[:, gs], sguwT[:, g, :], vsn[:, gs], start=True, stop=True), reads=[r_vsn, r_tab], writes=[r_bank[2]])
            for g in range(4):
                gs = slice(g * 128, (g + 1) * 128)
                P.op("dve", lambda e, g=g, gs=gs: e.scalar_tensor_tensor(mixed[:, 512 + g * 128:640 + g * 128], bank(2)[:, gs], cs[:, C_SGUB + g:C_SGUB + g + 1],
                                                                        TD[:, gs], ALU.add, ALU.mult),
                     reads=[r_bank[2], r_tab, r_TD], writes=[r_mixed])
            transposes(mixed, r_mixed, [(mixedT[:, 0:4, :], 0, 4, "act"), (mixedT[:, 4:8, :], 4, 8, "dve")], [r_mixedT], 0)
            for hf in range(2):
                for k in range(8):
                    P.op("pe", lambda e, hf=hf, k=k: e.matmul(bank(3 + hf), mixedT[:, k, :], wout_v[:, k, hf * 512:(hf + 1) * 512], start=(k == 0), stop=(k == 7)),
                         reads=[r_mixedT, r_wout], writes=[r_bank[3 + hf]])
            postnorm_residual(3, NWp[0], r_NWp[0], x[:], rx, 28)
            if debug:
                P.dma("pool", lambda e: e.dma_start(out=dbg_x1[u, c * 128:(c + 1) * 128, :], in_=x[:]), reads=rx, key="dbgx1")

        def chunk_y(u, c):
            i = c % 3
            x = xc[i]
            rx = [r_xc[i]]
            ykw = dict(junk=Pm[:], r_junk=r_Pm, smt=smY, r_smt=r_smY)
            prenorm(x[:], rx, hb2[:], [r_hb2], 30, **ykw)
            transposes(hb2, r_hb2, [(hT2[:, 0:4, :], 0, 4, "act"), (hT2[:, 4:8, :], 4, 8, "dve")], [r_hT2], 1)
            for hf in range(2):
                for k in range(8):
                    P.op("pe", lambda e, hf=hf, k=k: e.matmul(bank(6 + hf), hT2[:, k, :], wq_v[:, k, hf * 512:(hf + 1) * 512], start=(k == 0), stop=(k == 7)),
                         reads=[r_hT2, r_wq], writes=[r_bank[6 + hf]])
            P.op("act", lambda e: e.activation(qx[:], bank(6, 2), AF.Copy), reads=[r_bank[6], r_bank[7]], writes=[r_qx])
            transposes(qx, r_qx, [(qxT[:, 0:4, :], 0, 4, "act"), (qxT[:, 4:8, :], 4, 8, "dve")], [r_qxT], 1)
            for h in range(4):
                for dc in range(2):
                    P.op("pe", lambda e, h=h, dc=dc: e.matmul(bank(6 + h // 2)[:, (h % 2) * 256:(h % 2 + 1) * 256], qxT[:, 2 * h + dc, :], memkT[:, 2 * h + dc, :],
                                                         start=(dc == 0), stop=(dc == 1)),
                         reads=[r_qxT, r_memkT], writes=[r_bank[6 + h // 2]])
            sc4 = bank(6, 2).rearrange("p (h m) -> p h m", m=256)
            P.op("dve", lambda e: e.tensor_reduce(smY[:, 48:52], sc4, AX.X, ALU.max), reads=[r_bank[6], r_bank[7], r_smY], writes=[r_smY])
            P.op("pool", lambda e: e.tensor_scalar(smY[:, 52:56], smY[:, 48:52], -1.0 / 16.0, None, ALU.mult), reads=[r_smY], writes=[r_smY])
            for h in range(4):
                P.op("act", lambda e, h=h: e.activation(Pm[:, h * 256:(h + 1) * 256], bank(6, 2)[:, h * 256:(h + 1) * 256], AF.Exp, bias=smY[:, 52 + h:53 + h],
                                                       scale=1.0 / 16.0, accum_out=smY[:, 56 + h:57 + h]),
                     reads=[r_bank[6 + h // 2], r_smY], writes=[r_Pm, r_smY])
            P.op("dve", lambda e: e.reciprocal(smY[:, 60:64], smY[:, 56:60]), reads=[r_smY], writes=[r_smY])
            transposes(Pm, r_Pm, [(pT[:, 0:4, :], 0, 4, "act"), (pT[:, 4:8, :], 4, 8, "dve")], [r_pT], 1)
            for h in range(4):
                for mc in range(2):
                    P.op("pe", lambda e, h=h, mc=mc: e.matmul(bank(6 + h // 2)[:, (h % 2) * 256:(h % 2 + 1) * 256], pT[:, 2 * h + mc, :], memv[:, mc, h * 256:(h + 1) * 256],
                                                         start=(mc == 0), stop=(mc == 1)),
                         reads=[r_pT, r_memv], writes=[r_bank[6 + h // 2]])
            for h in range(4):
                P.op("dve", lambda e, h=h: e.tensor_scalar(qx[:, h * 256:(h + 1) * 256], bank(6, 2)[:, h * 256:(h + 1) * 256], smY[:, 60 + h:61 + h], None, ALU.mult),
                     reads=[r_bank[6 + h // 2], r_smY], writes=[r_qx])
            transposes(qx, r_qx, [(qxT[:, 0:4, :], 0, 4, "act"), (qxT[:, 4:8, :], 4, 8, "dve")], [r_qxT], 1)
            for hf in range(2):
                for k in range(8):
                    P.op("pe", lambda e, hf=hf, k=k: e.matmul(bank(6 + hf), qxT[:, k, :], wo_v[:, k, hf * 512:(hf + 1) * 512], start=(k == 0), stop=(k == 7)),
                         reads=[r_qxT, r_wo], writes=[r_bank[6 + hf]])
            postnorm_residual(6, NWp[1], r_NWp[1], x[:], rx, 34, tt=TC, r_tt=r_TC, **ykw)
            P.dma("sp", lambda e: e.dma_start(out=s_x2[u, c * 128:(c + 1) * 128, :], in_=x[:]), reads=rx, writes=[r_scr_x2[u][c]], key="stx%d" % i)

        def phase_m(u, carry):
            mem_kv(u)
            if carry:
                P.op("dve", lambda e: e.tensor_scalar(Sst[:], Sst[:], cs[:, C_META:C_META + 1], None, ALU.mult), reads=[r_tab, r_S], writes=[r_S])
            else:
                P.op("dve", lambda e: e.memset(Sst[:], 0.0), writes=[r_S])
            P.op("act", lambda e: e.activation(Rf[:], Sst[:], AF.Copy), reads=[r_S], writes=[r_Rf])
            load_x(xs[u, 0:128, :], 0, "sp")
            load_x(xs[u, 128:256, :], 1, "sp")
            P.dma("sp", lambda e: e.dma_start(out=Rb[0][:], in_=s_rb[u, 0]), reads=[r_scr_rb[u][0]], writes=[r_Rb[0]], key="ldrb0")
            P.dma("sp", lambda e: e.dma_start(out=hT[:], in_=s_hT[u, 0].rearrange("p (k t) -> p k t", k=8)), reads=[r_scr_h[u][0]], writes=[r_hT], key="ldh")
            XL = P.record(lambda: chunk_x(u, 0))[:XCUT[0]]
            P.play(XL)
            for c in range(NCH):
                if c + 2 < NCH:
                    load_x(xs[u, (c + 2) * 128:(c + 3) * 128, :], (c + 2) % 3, "sp")
                YL = P.record(lambda: chunk_y(u, c)) if not SKIPY[0] else P.record(lambda: P.dma("pool", lambda e, c=c: e.dma_start(out=s_x2[u, c * 128:(c + 1) * 128, :], in_=xc[c % 2][:]), reads=[r_xc[c % 2]], writes=[r_scr_x2[u][c]], key="stx%d" % (c % 2)))
                XL = []
                if c + 1 < NCH:
                    def bx(c=c):
                        if c + 2 < NCH:
                            pass
                        chunk_x(u, c + 1)
                    XL = P.record(bx)[:XCUT[0]]
                P.play(XL, YL)

        wgu_cnt = [0]
        wdn_cnt = [0]
        alias_done = set()

        def alias(eng, *rs):
            out = []
            for r in rs:
                if (eng, r.name) not in alias_done:
                    alias_done.add((eng, r.name))
                    out.append(r)
            return out

        r_allw = [r_win[0], r_win[1], r_wout, r_wq, r_wo]

        def f_prologue(u, bi, par):
            t0 = bi * 512
            res = []
            for c in range(4):
                def body(c=c):
                    xr = xress[par][:, c, :]
                    P.dma("pool", lambda e: e.dma_start(out=xr, in_=s_x2[u, t0 + c * 128:t0 + (c + 1) * 128, :]), reads=[r_scr_x2[u][bi * 4 + c]],
                          writes=[r_xress[par][c]] + alias("pool", *r_allw), key="ldx2_%d" % c)
                    prenorm(xr, [r_xress[par][c]], hb[:], [r_hb], 0, junk=mixed[:], r_junk=r_mixed)
                    tb = 6 + (c % 2)
                    tr = bankbf(tb)
                    for jj in range(8):
                        P.op("pe", lambda e, jj=jj: e.transpose(tr[:, jj * 128:(jj + 1) * 128], hb[:, jj * 128:(jj + 1) * 128], ident[:]), reads=[r_hb], writes=[r_bank[tb]])
                    P.op("dve", lambda e: e.tensor_copy(h3T_vs[par][:, :, c * 128:(c + 1) * 128], tr.rearrange("p (k t) -> p k t", t=128)),
                         reads=[r_bank[tb]], writes=[r_h3Ts[par][c]] + alias("dve", *r_allw))
                res += P.record(body)
            return res

        def f_gateup(u, bi, par):
            lists = []
            for f in range(NF):
                def body(f=f):
                    s_ = wgu_cnt[0] % 3
                    wgu_cnt[0] += 1
                    P.dma("sp", lambda e: e.dma_start(out=wgu_ring[s_], in_=s_wgu[f]), after=r_scr["s_wgu"], writes=[r_wgur[s_]] + alias("sp", *r_allw), key="ldgu%d" % s_)
                    pb = 2 * (f % 2)
                    for gu in range(2):
                        for k in range(8):
                            P.op("pe", lambda e, gu=gu, k=k: e.matmul(bank(pb + gu), wgu_ring[s_][:, k, gu, :], h3T_vs[par][:, k, :], start=(k == 0), stop=(k == 7)),
                                 reads=[r_wgur[s_]] + r_h3Ts[par], writes=[r_bank[pb + gu]])
                    tt = TC if f % 2 == 0 else TD
                    rtt = r_TC if f % 2 == 0 else r_TD
                    P.op("act", lambda e: e.activation(tt[:, 0:512], bank(pb), AF.Tanh, scale=0.5), reads=[r_bank[pb]], writes=[rtt])
                    P.op("dve", lambda e: e.scalar_tensor_tensor(tt[:, 512:1024], tt[:, 0:512], 1.0, bank(pb), ALU.add, ALU.mult), reads=[rtt, r_bank[pb]], writes=[rtt])
                    P.op("dve", lambda e: e.tensor_tensor(act_v[:, f, :], tt[:, 512:1024], bank(pb + 1), ALU.mult), reads=[rtt, r_bank[pb + 1]],
                         writes=[r_act[f]] + alias("dve", *r_allw))
                lists.append(P.record(body))
            return lists

        def f_down(u, bi):
            for fp in range(NF // 2):
                s_ = wdn_cnt[0] % 3
                wdn_cnt[0] += 1
                P.dma("sp", lambda e, fp=fp, s_=s_: e.dma_start(out=wdn_ring[s_], in_=s_wdn[fp * 256:(fp + 1) * 256, :].rearrange("(f p) c -> p f c", p=128)),
                      after=r_scr["s_wdn"], writes=[r_wdnr[s_]] + alias("sp", *r_allw), key="lddn%d" % s_)
                for fi in range(2):
                    f = fp * 2 + fi
                    for c in range(4):
                        for hf in range(2):
                            P.op("pe", lambda e, f=f, fi=fi, c=c, hf=hf, s_=s_: e.matmul(bank(2 * c + hf), act_v[:, f, c * 128:(c + 1) * 128], wdn_ring[s_][:, fi, hf * 512:(hf + 1) * 512],
                                                                               start=(f == 0), stop=(f == NF - 1)),
                                 reads=[r_act[f], r_wdnr[s_]], writes=[r_bank[2 * c + hf]])

        def f_epilogue(u, bi, par):
            t0 = bi * 512
            lists = []
            for c in range(4):
                def body(c=c):
                    i = c % 2
                    postnorm_residual(2 * c, NWp[2], r_NWp[2], xress[par][:, c, :], [r_xress[par][c]], 0, out_tile=xc[i][:], r_out=[r_xc[i]],
                                      junk=qx[:], r_junk=r_qx, smt=smY, r_smt=r_smY)
                    P.dma("pool", lambda e: e.dma_start(out=outp[u, t0 + c * 128:t0 + (c + 1) * 128, :], in_=xc[i][:]), reads=[r_xc[i]], key="sto%d" % i)
                lists.append(P.record(body))
            return lists

        def phase_f_all():
            blocks = [(u, bi) for u in range(NU) for bi in range(UT // 512)]
            P.play(f_prologue(blocks[0][0], blocks[0][1], 0))
            if FSTOP[0] == 0:
                return
            for n, (u, bi) in enumerate(blocks):
                par = n % 2
                if FSTOP[0] == 3 and n >= 1:
                    return
                gl = f_gateup(u, bi, par)
                epi = f_epilogue(*blocks[n - 1], 1 - par) if n > 0 else [[], [], [], []]
                pro = f_prologue(*blocks[n + 1], 1 - par) if n + 1 < len(blocks) else []
                P.play(epi[0][:ECUT[0]] if n == 1 else epi[0])
                if FSTOP[0] == 4 and n == 1:
                    return
                P.play(gl[0])
                if FSTOP[0] == 5 and n == 1:
                    return
                P.play(epi[1])
                if FSTOP[0] == 6 and n == 1:
                    return
                P.play(gl[1])
                if FSTOP[0] == 7 and n == 1:
                    return
                main = [t for l in gl[2:] for t in l]
                side = epi[2] + epi[3] + pro
                P.play(main, side)
                if FSTOP[0] == 1 or (FSTOP[0] == 8 and n == 1):
                    return
                P.play(P.record(lambda: f_down(u, bi)))
                if FSTOP[0] == 2 or (FSTOP[0] >= 10 and n == FSTOP[0] - 9):
                    return
            for li, l in enumerate(f_epilogue(*blocks[-1], (len(blocks) - 1) % 2)):
                if FSTOP[0] >= 30 and li >= FSTOP[0] - 30:
                    break
                P.play(l)

        STOP = STOPAT[0]
        if STOP >= 1:
            rope_tables(1)
            phase_a(1, True)
        if STOP >= 2:
            rope_tables(0)
            phase_a(0, False)
        if STOP == 3:
            mem_kv(0)
        if STOP >= 4:
            phase_m(0, False)
        if STOP >= 5:
            rope_tables(1)
            phase_m(1, True)
            rope_tables(0)
            phase_a(2, True)
            phase_m(2, False)
        if STOP >= 6:
            P.dma("sp", lambda e: e.dma_start(out=NWp[0][:], in_=norm_w[6].partition_broadcast(128)), writes=[r_NWp[0]], key="c_nwp0")
            phase_f_all()
        with nc.allow_non_contiguous_dma(reason="setup loads / weight scratch layout"):
            P.emit(nc)
    return nc


_NC_CACHE = {}
_DEBUG = [False]


def kernel(x_prompt, x_sample, mem_prompt, mem_sample, norm_w, w_in, ret_log_gamma, ret_gn_w,
           sgu_norm_w, sgu_w, sgu_b, w_out, xa_wq, xa_wkv, xa_wo, ffn_w_gu, ffn_w_down):
    f = lambda a: np.ascontiguousarray(np.asarray(a, dtype=np.float32))
    x_prompt, x_sample, mem_prompt, mem_sample = f(x_prompt), f(x_sample), f(mem_prompt), f(mem_sample)
    shared = {
        "norm_w": f(norm_w)[0], "w_in": f(w_in)[0], "ret_log_gamma": f(ret_log_gamma)[0].reshape(8),
        "ret_gn_w": f(ret_gn_w)[0], "sgu_norm_w": f(sgu_norm_w)[0], "sgu_w": f(sgu_w)[0], "sgu_b": f(sgu_b)[0],
        "w_out": f(w_out)[0], "xa_wq": f(xa_wq)[0], "xa_wkv": f(xa_wkv)[0], "xa_wo": f(xa_wo)[0],
        "ffn_w_gu": f(ffn_w_gu)[0], "ffn_w_down": f(ffn_w_down)[0],
    }
    in_maps = []
    for core in range(8):
        if core < 4:
            xs = np.stack([x_sample[core, :UT], x_sample[core, UT:], x_prompt[core]])
            mm = np.stack([mem_sample[core], mem_sample[core], mem_prompt[core]])
            link = 1.0
        else:
            p0 = 4 + 3 * (core - 4)
            xs = np.stack([x_prompt[p0], x_prompt[p0 + 1], x_prompt[p0 + 2]])
            mm = np.stack([mem_prompt[p0], mem_prompt[p0 + 1], mem_prompt[p0 + 2]])
            link = 0.0
        meta = np.zeros((128, 4), np.float32)
        meta[:, 0] = link
        meta[:, 2] = link * UT
        d = dict(shared)
        d.update({"xs": np.ascontiguousarray(xs), "mems": np.ascontiguousarray(mm), "meta": meta})
        in_maps.append(d)
    if "nc" not in _NC_CACHE:
        _NC_CACHE["nc"] = build_program(debug=_DEBUG[0])
    res = run_bass_kernel_spmd(_NC_CACHE["nc"], in_maps, core_ids=list(range(8)))
    if _DEBUG[0]:
        _DEBUG.append(res)
    y_prompt = np.empty_like(x_prompt)
    y_sample = np.empty_like(x_sample)
    for core in range(8):
        o = np.asarray(res.results[core]["out"], dtype=np.float32)
        if core < 4:
            y_sample[core, :UT] = o[0]
            y_sample[core, UT:] = o[1]
            y_prompt[core] = o[2]
        else:
            p0 = 4 + 3 * (core - 4)
            y_prompt[p0], y_prompt[p0 + 1], y_prompt[p0 + 2] = o[0], o[1], o[2]
    return (y_prompt, y_sample)
```

```python
import math
from contextlib import ExitStack

import numpy as np
import concourse.bass as bass
import concourse.mybir as mybir
from concourse.bass_utils import run_bass_kernel_spmd

F32 = mybir.dt.float32
BF16 = mybir.dt.bfloat16
I32 = mybir.dt.int32
AF = mybir.ActivationFunctionType
ALU = mybir.AluOpType
AX = mybir.AxisListType

D = 1024
NU = 3
UT = 4096
NCH = UT // 128
DFF = 2816
NF = DFF // 128
EPS = 1e-6
KSCALE = 128 ** -0.5
LN_KSCALE = math.log(KSCALE)
GELU_C = 0.7978845608028654
SEQ_PLAY = [False]
SKIPY = [False]
STOPAT = [9]
XCUT = [100000]
QKTR = [0]
STRICT = [1]
DUMMY = [2]
FSTOP = [9]
ECUT = [1000]


class Res:
    __slots__ = ("name", "last_w", "readers")

    def __init__(self, name):
        self.name = name
        self.last_w = None
        self.readers = {}


class Node:
    __slots__ = ("eng", "fn", "deps", "signal", "token", "dma_key", "idx")


class Prog:
    ENGS = ("pe", "act", "dve", "pool", "sp")

    def __init__(self):
        self.nodes = []
        self.rec = None

    def _grp(self, n):
        return n.dma_key if n.dma_key is not None else "E_" + n.eng

    def _add(self, eng, fn, reads, writes, dma_key=None, after=()):
        n = Node()
        n.eng = eng
        n.fn = fn
        n.signal = dma_key is not None
        n.token = None
        n.dma_key = dma_key
        n.idx = len(self.nodes)
        deps = {}

        def need(d, raw=False):
            dn = self.nodes[d]
            if dn.eng == eng and dn.dma_key is None and dma_key is None:
                if STRICT[0] == 0 and (not raw or eng == "pe"):
                    return
                if STRICT[0] == 1 and eng == "pe":
                    return
            g = self._grp(dn)
            if g not in deps or deps[g] < d:
                deps[g] = d

        for a in after:
            need(a.idx, True)
        for r in reads:
            if r.last_w is not None:
                need(r.last_w, True)
        for w in writes:
            if w.last_w is not None:
                need(w.last_w)
            for rd in w.readers.values():
                need(rd)
        n.deps = sorted(deps.values())
        for d in n.deps:
            self.nodes[d].signal = True
        self.nodes.append(n)
        g = self._grp(n)
        for r in reads:
            r.readers[g] = n.idx
        for w in writes:
            w.last_w = n.idx
            w.readers = {}
        return n

    def op(self, eng, fn, reads=(), writes=(), after=()):
        if self.rec is not None:
            self.rec.append((eng, fn, list(reads), list(writes), None, tuple(after)))
            return None
        return self._add(eng, fn, reads, writes, after=after)

    def dma(self, eng, fn, reads=(), writes=(), key=None, after=()):
        if self.rec is not None:
            self.rec.append((eng, fn, list(reads), list(writes), key, tuple(after)))
            return None
        return self._add(eng, fn, reads, writes, dma_key=key, after=after)

    def record(self, body):
        self.rec = []
        body()
        lst, self.rec = self.rec, None
        return lst

    def play(self, a, b=()):
        i = j = 0
        if SEQ_PLAY[0]:
            for t in list(a) + list(b):
                self._add(*t)
            return
        while i < len(a) or j < len(b):
            if j >= len(b) or (i < len(a) and i * len(b) <= j * len(a)):
                self._add(*a[i]); i += 1
            else:
                self._add(*b[j]); j += 1

    def emit(self, nc):
        cnt = {}
        keys = []
        for n in self.nodes:
            if not n.signal:
                continue
            k = self._grp(n)
            if k not in cnt:
                cnt[k] = 0
                keys.append(k)
            cnt[k] += 16 if n.dma_key is not None else 1
            n.token = (k, cnt[k])
        final = [(k, cnt[k]) for k in keys if not k.startswith("E_")]
        per_eng = {e: [n for n in self.nodes if n.eng == e] for e in self.ENGS}
        nodes = self.nodes
        with ExitStack() as st:
            sems = {k: st.enter_context(nc.semaphore("s_" + k)) for k in keys}
            block = st.enter_context(nc.Block())

            def run(lst, do_final):
                def body(eng):
                    waited = {}
                    for n in lst:
                        for d in n.deps:
                            k, v = nodes[d].token
                            if waited.get(k, 0) < v:
                                eng.wait_ge(sems[k], v)
                                waited[k] = v
                        ins = n.fn(eng)
                        if n.signal:
                            ins.then_inc(sems[n.token[0]], 16 if n.dma_key is not None else 1)
                    if do_final:
                        for k, v in final:
                            if waited.get(k, 0) < v:
                                eng.wait_ge(sems[k], v)
                return body

            block.tensor(run(per_eng["pe"], False))
            block.scalar(run(per_eng["act"], False))
            block.vector(run(per_eng["dve"], False))
            block.gpsimd(run(per_eng["pool"], False))
            block.sync(run(per_eng["sp"], True))


def build_program(debug=False):
    nc = bass.Bass("TRN2", target_bir_lowering=False)
    P = Prog()

    def din(name, shape, dt=F32):
        return nc.dram_tensor(name, shape, dt, kind="ExternalInput").ap()

    def dscr(name, shape, dt):
        return nc.dram_tensor(name, shape, dt, kind="ExternalOutput" if debug else "Internal").ap()

    xs = din("xs", [NU, UT, D])
    mems = din("mems", [NU, 256, D])
    meta = din("meta", [128, 4])
    norm_w = din("norm_w", [7, D])
    w_in = din("w_in", [D, 3072])
    lg_d = din("ret_log_gamma", [8])
    gnw_d = din("ret_gn_w", [512])
    sgunw_d = din("sgu_norm_w", [512])
    sguw_d = din("sgu_w", [4, 128, 128])
    sgub_d = din("sgu_b", [4, 128])
    w_out = din("w_out", [D, D])
    wq_d = din("xa_wq", [D, D])
    wkv_d = din("xa_wkv", [D, 2 * D])
    wo_d = din("xa_wo", [D, D])
    wgu_d = din("ffn_w_gu", [D, 2 * DFF])
    wdn_d = din("ffn_w_down", [DFF, D])
    outp = nc.dram_tensor("out", [NU, UT, D], F32, kind="ExternalOutput").ap()

    s_win = dscr("s_win", [D, 3072], BF16)
    s_wout = dscr("s_wout", [D, D], BF16)
    s_wq = dscr("s_wq", [D, D], BF16)
    s_wkv = dscr("s_wkv", [D, 2 * D], BF16)
    s_wo = dscr("s_wo", [D, D], BF16)
    s_wgu = dscr("s_wgu", [NF, 128, 8, 2, 128], BF16)
    s_wdn = dscr("s_wdn", [DFF, D], BF16)
    s_rb = dscr("s_rb", [NU, NCH, 128, 512], BF16)
    s_kv = dscr("s_kv", [NU, NCH, 128, 1024], BF16)
    s_hT = dscr("s_hT", [NU, NCH, 128, 1024], BF16)
    if debug:
        s_x2 = nc.dram_tensor("dbg_x2", [NU, UT, D], F32, kind="ExternalOutput").ap()
        dbg_x1 = nc.dram_tensor("dbg_x1", [NU, UT, D], F32, kind="ExternalOutput").ap()
    else:
        s_x2 = dscr("s_x2", [NU, UT, D], F32)

    st = ExitStack()
    with st:
        def sb(name, shape, dt):
            return st.enter_context(nc.sbuf_tensor(name, shape, dt))

        WA = sb("WA", [128, 49152], BF16)
        win_v = WA[:, 0:24576].rearrange("p (k c) -> p k c", k=8)
        wout_v = WA[:, 24576:32768].rearrange("p (k c) -> p k c", k=8)
        wq_v = WA[:, 32768:40960].rearrange("p (k c) -> p k c", k=8)
        wo_v = WA[:, 40960:49152].rearrange("p (k c) -> p k c", k=8)
        r_win = [Res("win%d" % i) for i in range(2)]
        r_wout, r_wq, r_wo = Res("wout"), Res("wq"), Res("wo")
        act_v = WA[:, 0:11264].rearrange("p (f t) -> p f t", f=NF)
        h3T_vs = [WA[:, 11264 + i * 4096:15360 + i * 4096].rearrange("p (k t) -> p k t", k=8) for i in range(2)]
        wgu_ring = [WA[:, 19456 + i * 2048: 19456 + (i + 1) * 2048].rearrange("p (k g c) -> p k g c", k=8, g=2) for i in range(3)]
        wdn_ring = [WA[:, 25600 + i * 2048: 25600 + (i + 1) * 2048].rearrange("p (f c) -> p f c", f=2) for i in range(3)]
        r_act = [Res("act%d" % f) for f in range(NF)]
        r_h3Ts = [[Res("h3T%d_%d" % (i, c)) for c in range(4)] for i in range(2)]
        r_wgur = [Res("wgur%d" % i) for i in range(3)]
        r_wdnr = [Res("wdnr%d" % i) for i in range(3)]

        xc = [sb("xc%d" % i, [128, D], F32) for i in range(3)]
        r_xc = [Res("xc0"), Res("xc1"), Res("xc2")]
        xress = [WA[:, 31744 + i * 8192:39936 + i * 8192].bitcast(F32).rearrange("p (c d) -> p c d", c=4) for i in range(2)]
        r_xress = [[Res("xres%d_%d" % (i, c)) for c in range(4)] for i in range(2)]
        TA = sb("TA", [128, D], F32); r_TA = Res("TA")
        TB = sb("TB", [128, D], F32); r_TB = Res("TB")
        TC = sb("TC", [128, D], F32); r_TC = Res("TC")
        TD = sb("TD", [128, D], F32); r_TD = Res("TD")
        hb = sb("hb", [128, D], BF16); r_hb = Res("hb")
        hT = sb("hT", [128, 8, 128], BF16); r_hT = Res("hT")
        qk = sb("qk", [128, D], BF16); r_qk = Res("qk")
        qkT = sb("qkT", [128, 8, 128], BF16); r_qkT = Res("qkT")
        qfT = sb("qfT", [128, 4, 128], BF16); r_qfT = Res("qfT")
        qbT = sb("qbT", [128, 4, 128], BF16); r_qbT = Res("qbT")
        vb = sb("vb", [128, 512], BF16); r_vb = Res("vb")
        kfb = sb("kfb", [128, 512], BF16); r_kfb = Res("kfb")
        sTm = sb("sTm", [128, 4, 128], BF16); r_sTm = Res("sTm")
        Rf = sb("Rf", [128, 512], BF16); r_Rf = Res("Rf")
        Sst = sb("Sst", [128, 512], F32); r_S = Res("S")
        Rb = [sb("Rb%d" % i, [128, 512], BF16) for i in range(2)]
        r_Rb = [Res("Rb0"), Res("Rb1")]
        sg = sb("sg", [128, 512], F32); r_sg = Res("sg")
        vsn = sb("vsn", [128, 512], BF16); r_vsn = Res("vsn")
        mixed = sb("mixed", [128, D], BF16); r_mixed = Res("mixed")
        mixedT = sb("mixedT", [128, 8, 128], BF16); r_mixedT = Res("mixedT")
        YB = sb("YB", [128, 2048], BF16)
        Pm = YB[:, 0:1024]; r_Pm = Res("Pm")
        pT = YB[:, 1024:2048].rearrange("p (k t) -> p k t", k=8); r_pT = Res("pT")
        memT = YB[:, :].rearrange("p (k m) -> p k m", k=8)
        hb2 = sb("hb2", [128, D], BF16); r_hb2 = Res("hb2")
        hT2 = sb("hT2", [128, 8, 128], BF16); r_hT2 = Res("hT2")
        qx = sb("qx", [128, D], BF16); r_qx = Res("qx")
        qxT = sb("qxT", [128, 8, 128], BF16); r_qxT = Res("qxT")
        smY = sb("smallY", [128, 64], F32); r_smY = Res("smallY")
        memkT = sb("memkT", [128, 8, 256], BF16); r_memkT = Res("memkT")
        memv = sb("memv", [128, 2, D], BF16); r_memv = Res("memv")
        cosT = sb("cosT", [128, NCH, 64], F32)
        sinT = sb("sinT", [128, NCH, 64], F32)
        r_rope = Res("rope")
        NWp = [sb("NWp%d" % i, [128, D], F32) for i in range(2)]
        NWp.append(NWp[0])
        r_NWp = [Res("NWp0"), Res("NWp1")]
        r_NWp.append(r_NWp[0])
        NWs = sb("NWs", [128, 512], F32)
        DT = sb("DT", [128, 4, 128], F32)
        Ab = sb("Ab", [128, 4, 128], F32)
        Bb = sb("Bb", [128, 4, 128], F32)
        KF = sb("KF", [128, 4, 128], F32)
        r_tab = Res("tab")
        sguwT = sb("sguwT", [128, 4, 128], BF16)
        ident = sb("ident", [128, 128], BF16)
        identf = sb("identf", [128, 128], F32)
        sm = sb("small", [128, 64], F32)
        r_sm = Res("small")
        cs = sb("const", [128, 96], F32)
        nwT = sb("nwT", [128, 64], F32)
        invf = sb("invf", [128, 64], F32)
        posb = sb("posb", [128, NCH], F32)

        PS = st.enter_context(nc.psum_tensor("PS", [128, 4096], F32))
        r_bank = [Res("bank%d" % b) for b in range(8)]

        def bank(b, n=1):
            return PS[:, b * 512:(b + n) * 512]

        def bankbf(b):
            return PS[:, b * 512:(b + 1) * 512].bitcast(BF16)

        C_PIDX, C_M05, C_LG, C_KFS, C_KBS, C_GF, C_GB, C_T, C_SGUB, C_META, C_HALF = 0, 1, 8, 16, 20, 24, 28, 32, 40, 48, 56

        ctr = [0]

        def key(prefix):
            ctr[0] += 1
            return "%s%d" % (prefix, ctr[0])

        def rstd_ops(ms_col, out_col, eps, smt=None, r_smt=None):
            smt = sm if smt is None else smt
            r_smt = r_sm if r_smt is None else r_smt
            P.op("pool", lambda e: e.tensor_scalar(smt[:, out_col:out_col + 1], smt[:, ms_col:ms_col + 1], eps, None, ALU.add),
                 reads=[r_smt], writes=[r_smt])
            P.op("pool", lambda e: e.tensor_tensor(smt[:, out_col:out_col + 1], smt[:, out_col:out_col + 1], cs[:, C_M05:C_M05 + 1], ALU.pow),
                 reads=[r_smt], writes=[r_smt])

        def transposes(src, r_src, dst_views, r_dst, tb, nblk=8, evac="act"):
            tr = bankbf(tb)
            for j in range(nblk):
                P.op("pe", lambda e, j=j: e.transpose(tr[:, j * 128:(j + 1) * 128], src[:, j * 128:(j + 1) * 128], ident[:]),
                     reads=[r_src], writes=[r_bank[tb]])
            for (dst, lo, hi, eng) in dst_views:
                if eng == "act":
                    P.op("act", lambda e, dst=dst, lo=lo, hi=hi: e.activation(dst, tr[:, lo * 128:hi * 128].rearrange("p (k t) -> p k t", t=128), AF.Copy),
                         reads=[r_bank[tb]], writes=r_dst)
                else:
                    P.op("dve", lambda e, dst=dst, lo=lo, hi=hi: e.tensor_copy(dst, tr[:, lo * 128:hi * 128].rearrange("p (k t) -> p k t", t=128)),
                         reads=[r_bank[tb]], writes=r_dst)

        def prenorm(src, r_src, dstb, r_dstb, slot, junk=None, r_junk=None, smt=None, r_smt=None):
            junk = TD[:] if junk is None else junk
            r_junk = r_TD if r_junk is None else r_junk
            smt_ = sm if smt is None else smt
            r_smt_ = r_sm if r_smt is None else r_smt
            P.op("act", lambda e: e.activation(junk, src, AF.Square, scale=1.0 / 32.0, accum_out=smt_[:, slot:slot + 1]),
                 reads=r_src, writes=[r_junk, r_smt_])
            rstd_ops(slot, slot + 1, EPS, smt_, r_smt_)
            P.op("act", lambda e: e.activation(dstb, src, AF.Copy, scale=smt_[:, slot + 1:slot + 2]),
                 reads=r_src + [r_smt_], writes=r_dstb)

        def postnorm_residual(b0, nwp, r_nwp, xtile, r_x, slot, out_tile=None, r_out=None, junk=None, r_junk=None,
                              smt=None, r_smt=None, tt=None, r_tt=None):
            junk = TD[:] if junk is None else junk
            r_junk = r_TD if r_junk is None else r_junk
            smt_ = sm if smt is None else smt
            r_smt_ = r_sm if r_smt is None else r_smt
            tt_ = TA if tt is None else tt
            r_tt_ = r_TA if r_tt is None else r_tt
            o = bank(b0, 2)
            P.op("act", lambda e: e.activation(junk, o, AF.Square, scale=1.0 / 32.0, accum_out=smt_[:, slot:slot + 1]),
                 reads=[r_bank[b0], r_bank[b0 + 1]], writes=[r_junk, r_smt_])
            rstd_ops(slot, slot + 1, EPS, smt_, r_smt_)
            P.op("dve", lambda e: e.scalar_tensor_tensor(tt_[:], o, smt_[:, slot + 1:slot + 2], nwp[:], ALU.mult, ALU.mult),
                 reads=[r_bank[b0], r_bank[b0 + 1], r_smt_, r_nwp], writes=[r_tt_])
            ot = xtile if out_tile is None else out_tile
            ro = r_x if r_out is None else r_out
            P.op("dve", lambda e: e.tensor_tensor(ot, tt_[:], xtile, ALU.add),
                 reads=[r_tt_] + r_x, writes=ro)

        with nc.allow_non_contiguous_dma(reason="tiny setup loads"):
            P.dma("sp", lambda e: e.dma_start(out=cs[:, C_LG:C_LG + 8], in_=lg_d.partition_broadcast(128)), writes=[r_tab], key="c_lg")
            P.dma("sp", lambda e: e.dma_start(out=cs[:, C_META:C_META + 4], in_=meta), writes=[r_tab], key="c_meta")
            P.dma("sp", lambda e: e.dma_start(out=cs[:, C_SGUB:C_SGUB + 4], in_=sgub_d.rearrange("g p -> p g")), writes=[r_tab], key="c_sgub")
            P.dma("sp", lambda e: e.dma_start(out=NWs[:], in_=sgunw_d.partition_broadcast(128)), writes=[r_tab], key="c_nws")
            for i, row in enumerate((1, 3)):
                P.dma("sp", lambda e, i=i, row=row: e.dma_start(out=NWp[i][:], in_=norm_w[row].partition_broadcast(128)), writes=[r_NWp[i]], key="c_nwp%d" % i)
            P.dma("sp", lambda e: e.dma_start(out=TA[0:56, 0:128], in_=norm_w.rearrange("n (k p) -> (n k) p", p=128)), writes=[r_TA], key="c_nw")
            P.dma("sp", lambda e: e.dma_start(out=TA[56:60, 0:128], in_=gnw_d.rearrange("(k p) -> k p", p=128)), writes=[r_TA], key="c_gnw")
            P.dma("sp", lambda e: e.dma_start(out=TB[:, 0:512].rearrange("p (g q) -> p g q", g=4), in_=sguw_d.rearrange("g p q -> p g q")), writes=[r_TB], key="c_sguw")

        P.op("pool", lambda e: e.iota(identf[:], pattern=[[1, 128]], base=0, channel_multiplier=-1, allow_small_or_imprecise_dtypes=True), writes=[r_tab])
        P.op("pool", lambda e: e.iota(cs[:, C_PIDX:C_PIDX + 1], pattern=[[0, 1]], base=0, channel_multiplier=1, allow_small_or_imprecise_dtypes=True), writes=[r_tab])
        P.op("pool", lambda e: e.iota(posb[:], pattern=[[128, NCH]], base=0, channel_multiplier=1, allow_small_or_imprecise_dtypes=True), writes=[r_tab])
        P.op("pool", lambda e: e.iota(TC[:, 0:128], pattern=[[1, 128]], base=0, channel_multiplier=0, allow_small_or_imprecise_dtypes=True), writes=[r_TC])
        P.op("dve", lambda e: e.memset(cs[:, C_M05:C_M05 + 1], -0.5), writes=[r_tab])
        P.op("dve", lambda e: e.memset(cs[:, C_HALF:C_HALF + 1], 0.5), writes=[r_tab])
        P.op("dve", lambda e: e.tensor_scalar(TC[:, 128:256], identf[:], 0.0, None, ALU.max), reads=[r_tab], writes=[r_TC])
        P.op("dve", lambda e: e.tensor_scalar(TC[:, 256:384], identf[:], -1.0, 0.0, ALU.mult, ALU.max), reads=[r_tab], writes=[r_TC])
        P.op("dve", lambda e: e.tensor_scalar(ident[:], identf[:], 0.0, None, ALU.is_equal), reads=[r_tab], writes=[r_tab])
        P.op("dve", lambda e: e.tensor_scalar(identf[:], identf[:], 0.0, None, ALU.is_equal), reads=[r_tab], writes=[r_tab])
        for h in range(4):
            lf = cs[:, C_LG + h:C_LG + h + 1]
            lb = cs[:, C_LG + 4 + h:C_LG + 5 + h]
            P.op("dve", lambda e, lf=lf: e.tensor_scalar(TD[:, 0:128], TC[:, 128:256], lf, None, ALU.mult), reads=[r_TC, r_tab], writes=[r_TD])
            P.op("dve", lambda e, lb=lb: e.scalar_tensor_tensor(TD[:, 0:128], TC[:, 256:384], lb, TD[:, 0:128], ALU.mult, ALU.add), reads=[r_TC, r_tab, r_TD], writes=[r_TD])
            P.op("act", lambda e, h=h: e.activation(DT[:, h, :], TD[:, 0:128], AF.Exp), reads=[r_TD], writes=[r_tab])
            P.op("dve", lambda e, h=h: e.tensor_scalar(DT[:, h, :], DT[:, h, :], KSCALE, None, ALU.mult), reads=[r_tab], writes=[r_tab])
            P.op("dve", lambda e, lf=lf: e.tensor_scalar(TD[:, 128:256], TC[:, 0:128], 1.0, lf, ALU.add, ALU.mult), reads=[r_TC, r_tab], writes=[r_TD])
            P.op("act", lambda e, h=h: e.activation(Ab[:, h, :], TD[:, 128:256], AF.Exp), reads=[r_TD], writes=[r_tab])
            P.op("dve", lambda e: e.tensor_scalar(TD[:, 256:384], TC[:, 0:128], -1.0, 128.0, ALU.mult, ALU.add), reads=[r_TC], writes=[r_TD])
            P.op("dve", lambda e, lb=lb: e.tensor_scalar(TD[:, 256:384], TD[:, 256:384], lb, None, ALU.mult), reads=[r_TD, r_tab], writes=[r_TD])
            P.op("act", lambda e, h=h: e.activation(Bb[:, h, :], TD[:, 256:384], AF.Exp), reads=[r_TD], writes=[r_tab])
        P.op("dve", lambda e: e.tensor_scalar(cs[:, C_T:C_T + 1], cs[:, C_PIDX:C_PIDX + 1], -1.0, 127.0, ALU.mult, ALU.add), reads=[r_tab], writes=[r_tab])
        P.op("dve", lambda e: e.tensor_scalar(cs[:, C_KFS:C_KFS + 4], cs[:, C_LG:C_LG + 4], cs[:, C_T:C_T + 1], None, ALU.mult), reads=[r_tab], writes=[r_tab])
        P.op("dve", lambda e: e.tensor_scalar(cs[:, C_KBS:C_KBS + 4], cs[:, C_LG + 4:C_LG + 8], cs[:, C_PIDX:C_PIDX + 1], None, ALU.mult), reads=[r_tab], writes=[r_tab])
        P.op("dve", lambda e: e.tensor_scalar(cs[:, C_GF:C_GF + 8], cs[:, C_LG:C_LG + 8], 128.0, None, ALU.mult), reads=[r_tab], writes=[r_tab])
        P.op("act", lambda e: e.activation(cs[:, C_KFS:C_KFS + 16], cs[:, C_KFS:C_KFS + 16], AF.Exp), reads=[r_tab], writes=[r_tab])
        P.op("dve", lambda e: e.tensor_scalar(cs[:, C_KFS:C_KFS + 8], cs[:, C_KFS:C_KFS + 8], KSCALE, None, ALU.mult), reads=[r_tab], writes=[r_tab])
        P.op("dve", lambda e: e.tensor_copy(KF[:], cs[:, C_KFS:C_KFS + 4].unsqueeze(2).to_broadcast([128, 4, 128])), reads=[r_tab], writes=[r_tab])
        P.op("dve", lambda e: e.memset(invf[:, 0:1], 1.0), writes=[r_tab])
        for kk in range(6):
            w = 1 << kk
            r = float(np.float32(10000.0 ** (-w / 64.0)))
            P.op("dve", lambda e, w=w, r=r: e.tensor_scalar(invf[:, w:2 * w], invf[:, 0:w], r, None, ALU.mult), reads=[r_tab], writes=[r_tab])
        P.op("pe", lambda e: e.transpose(bank(0)[:, 0:60], TA[0:60, 0:128], identf[0:60, 0:60]), reads=[r_TA, r_tab], writes=[r_bank[0]])
        P.op("dve", lambda e: e.tensor_copy(nwT[:, 0:60], bank(0)[:, 0:60]), reads=[r_bank[0]], writes=[r_tab])
        P.op("dve", lambda e: e.memset(nwT[:, 60:64], 0.5), reads=[r_tab], writes=[r_tab])
        P.op("dve", lambda e: e.tensor_scalar(nwT[:, 56:60], nwT[:, 56:60], 0.5, None, ALU.mult), reads=[r_tab], writes=[r_tab])
        P.op("dve", lambda e: e.tensor_copy(hb[:, 0:512], TB[:, 0:512]), reads=[r_TB], writes=[r_hb])
        for g in range(4):
            P.op("pe", lambda e, g=g: e.transpose(bankbf(1)[:, g * 128:(g + 1) * 128], hb[:, g * 128:(g + 1) * 128], ident[:]), reads=[r_hb, r_tab], writes=[r_bank[1]])
        P.op("dve", lambda e: e.tensor_copy(sguwT[:], bankbf(1)[:, 0:512].rearrange("p (g t) -> p g t", g=4)), reads=[r_bank[1]], writes=[r_tab])

        stage_f = [TA, TB, TC, TD]
        r_stage_f = [r_TA, r_TB, r_TC, r_TD]
        stage_b = [qk, mixed, hb, qx]
        r_stage_b = [r_qk, r_mixed, r_hb, r_qx]
        r_scr = {}
        last_store = {}
        pieces = []

        def nwcol(n):
            return lambda k: nwT[:, n * 8 + k:n * 8 + k + 1]

        def add_plain(src, scr, rows, cols, scale_col, rname):
            for k in range(rows // 128):
                for cb in range(cols // 1024):
                    last = (k == rows // 128 - 1) and (cb == cols // 1024 - 1)
                    pieces.append((src[k * 128:(k + 1) * 128, cb * 1024:(cb + 1) * 1024], 1024, scale_col(k),
                                   (lambda sbb, k=k, cb=cb, scr=scr: [(scr[k * 128:(k + 1) * 128, cb * 1024:(cb + 1) * 1024], sbb[:])]), rname, last))

        add_plain(w_in, s_win, D, 3072, nwcol(0), "s_win")
        add_plain(w_out, s_wout, D, D, lambda k: nwT[:, 56 + k:57 + k], "s_wout")
        add_plain(wq_d, s_wq, D, D, nwcol(2), "s_wq")
        add_plain(wo_d, s_wo, D, D, lambda k: 1.0, "s_wo")
        add_plain(wkv_d, s_wkv, D, 2 * D, nwcol(4), "s_wkv")
        for k in range(8):
            for cb in range(6):
                wcols = 1024 if cb < 5 else 512

                def st(sbb, k=k, cb=cb, wcols=wcols):
                    res = []
                    for j in range(wcols // 128):
                        col = cb * 1024 + j * 128
                        gu, f = (0, col // 128) if col < DFF else (1, (col - DFF) // 128)
                        res.append((s_wgu[f, :, k, gu, :], sbb[:, j * 128:(j + 1) * 128]))
                    return res
                pieces.append((wgu_d[k * 128:(k + 1) * 128, cb * 1024:cb * 1024 + wcols], wcols, nwT[:, 5 * 8 + k:5 * 8 + k + 1], st, "s_wgu", k == 7 and cb == 5))
        add_plain(wdn_d, s_wdn, DFF, D, lambda k: 0.5, "s_wdn")

        NSL, LA = 4, 3
        for t in range(len(pieces) + LA):
            if t < len(pieces):
                src, wcols, sc, stf, rname, last = pieces[t]
                i = t % NSL
                P.dma("sp", lambda e, src=src, wcols=wcols, i=i: e.dma_start(out=stage_f[i][:, 0:wcols], in_=src), writes=[r_stage_f[i]], key="pl%d" % i)
            t2 = t - LA
            if t2 >= 0:
                src, wcols, sc, stf, rname, last = pieces[t2]
                i = t2 % NSL
                sf, rf, sbb, rb = stage_f[i], r_stage_f[i], stage_b[i], r_stage_b[i]
                if t2 % 2 == 0:
                    P.op("act", lambda e, sf=sf, sbb=sbb, sc=sc, wcols=wcols: e.activation(sbb[:, 0:wcols], sf[:, 0:wcols], AF.Copy, scale=sc), reads=[rf, r_tab], writes=[rb])
                else:
                    P.op("dve", lambda e, sf=sf, sbb=sbb, sc=sc, wcols=wcols: e.tensor_scalar(sbb[:, 0:wcols], sf[:, 0:wcols], sc, None, ALU.mult), reads=[rf, r_tab], writes=[rb])
                for (dst, srcv) in stf(sbb):
                    last_store[i] = P.dma("pool", lambda e, dst=dst, srcv=srcv: e.dma_start(out=dst, in_=srcv), reads=[rb], key="ps%d" % i)
                if last:
                    r_scr[rname] = list(last_store.values())

        for i in range(2):
            P.dma("sp", lambda e, i=i: e.dma_start(out=win_v[:, i * 4:(i + 1) * 4, :], in_=s_win[i * 512:(i + 1) * 512, :].rearrange("(k p) c -> p k c", p=128)),
                  after=r_scr["s_win"], writes=[r_win[i]], key="ld_win%d" % i)
        P.dma("sp", lambda e: e.dma_start(out=wout_v, in_=s_wout.rearrange("(k p) c -> p k c", p=128)), after=r_scr["s_wout"], writes=[r_wout], key="ld_wout")
        P.dma("sp", lambda e: e.dma_start(out=wq_v, in_=s_wq.rearrange("(k p) c -> p k c", p=128)), after=r_scr["s_wq"], writes=[r_wq], key="ld_wq")
        P.dma("sp", lambda e: e.dma_start(out=wo_v, in_=s_wo.rearrange("(k p) c -> p k c", p=128)), after=r_scr["s_wo"], writes=[r_wo], key="ld_wo")

        def rope_tables(u):
            off = cs[:, C_META + 1 + u:C_META + 2 + u]
            assert NCH % 16 == 0
            for half in range(NCH // 16):
                c0 = half * 16
                P.op("dve", lambda e, c0=c0, off=off: e.tensor_scalar(sm[:, 32:48], posb[:, c0:c0 + 16], off, None, ALU.add), reads=[r_tab, r_sm], writes=[r_sm])
                angv = TC[:].rearrange("p (c j) -> p c j", j=64)
                P.op("dve", lambda e, angv=angv: e.tensor_tensor(angv, sm[:, 32:48].unsqueeze(2).to_broadcast([128, 16, 64]),
                                                                invf[:].unsqueeze(1).to_broadcast([128, 16, 64]), ALU.mult),
                     reads=[r_sm, r_tab], writes=[r_TC])
                for which, tbl in ((0, sinT), (1, cosT)):
                    shift = 0.0 if which == 0 else 0.25
                    P.op("dve", lambda e, shift=shift: e.tensor_scalar(TD[:], TC[:], 1.0 / (2.0 * math.pi), shift, ALU.mult, ALU.add), reads=[r_TC], writes=[r_TD])
                    P.op("dve", lambda e: e.tensor_copy(TA[:].bitcast(I32), TD[:]), reads=[r_TD], writes=[r_TA])
                    P.op("dve", lambda e: e.tensor_copy(TB[:], TA[:].bitcast(I32)), reads=[r_TA], writes=[r_TB])
                    P.op("dve", lambda e: e.tensor_tensor(TD[:], TD[:], TB[:], ALU.subtract), reads=[r_TD, r_TB], writes=[r_TD])
                    P.op("dve", lambda e: e.tensor_scalar(TB[:], TD[:], 0.5, None, ALU.is_gt), reads=[r_TD], writes=[r_TB])
                    P.op("dve", lambda e: e.tensor_tensor(TD[:], TD[:], TB[:], ALU.subtract), reads=[r_TD, r_TB], writes=[r_TD])
                    P.op("dve", lambda e: e.tensor_scalar(TB[:], TD[:], -0.5, None, ALU.is_lt), reads=[r_TD], writes=[r_TB])
                    P.op("dve", lambda e: e.tensor_tensor(TD[:], TD[:], TB[:], ALU.add), reads=[r_TD, r_TB], writes=[r_TD])
                    P.op("act", lambda e, tbl=tbl, c0=c0: e.activation(tbl[:, c0:c0 + 16, :], TD[:].rearrange("p (c j) -> p c j", j=64), AF.Sin, scale=6.283185),
                         reads=[r_TD], writes=[r_rope])

        def load_x(src_ap, i, q="pool"):
            P.dma(q, lambda e: e.dma_start(out=xc[i][:], in_=src_ap), writes=[r_xc[i]], key="ldx%d_%s" % (i, q))

        def zproj(ncols_blocks):
            for n in ncols_blocks:
                b = 2 + n
                for k in range(8):
                    P.op("pe", lambda e, n=n, k=k, b=b: e.matmul(bank(b), hT[:, k, :], win_v[:, k, n * 512:(n + 1) * 512], start=(k == 0), stop=(k == 7)),
                         reads=[r_hT, r_win[k // 4]], writes=[r_bank[b]])

        def rotary(b0, nblk, c, dst, tA=None, r_tA=None, tB=None, r_tB=None, r_dst=None):
            nh = nblk * 4
            w = nblk * 512
            tA = TA[:, 0:w] if tA is None else tA
            tB = TB[:, 0:w] if tB is None else tB
            r_tA = r_TA if r_tA is None else r_tA
            r_tB = r_TB if r_tB is None else r_tB
            r_dst = r_qk if r_dst is None else r_dst
            z3 = bank(b0, nblk).rearrange("p (g j) -> p g j", j=64)
            z4 = bank(b0, nblk).rearrange("p (h t j) -> p h t j", t=2, j=64)
            A3 = tA.rearrange("p (g j) -> p g j", j=64)
            A4 = tA.rearrange("p (h t j) -> p h t j", t=2, j=64)
            B4 = tB.rearrange("p (h t j) -> p h t j", t=2, j=64)
            d4 = dst.rearrange("p (h t j) -> p h t j", t=2, j=64)
            cosb = cosT[:, c, :].unsqueeze(1).to_broadcast([128, nh * 2, 64])
            sinb = sinT[:, c, :].unsqueeze(1).to_broadcast([128, nh, 64])
            rb = [r_bank[b0 + i] for i in range(nblk)]
            P.op("dve", lambda e: e.tensor_tensor(A3, z3, cosb, ALU.mult), reads=rb + [r_rope], writes=[r_tA])
            P.op("dve", lambda e: e.tensor_tensor(B4[:, :, 0, :], z4[:, :, 1, :], sinb, ALU.mult), reads=rb + [r_rope], writes=[r_tB])
            P.op("dve", lambda e: e.tensor_tensor(B4[:, :, 1, :], z4[:, :, 0, :], sinb, ALU.mult), reads=rb + [r_rope], writes=[r_tB])
            P.op("dve", lambda e: e.tensor_tensor(d4[:, :, 0, :], A4[:, :, 0, :], B4[:, :, 0, :], ALU.subtract), reads=[r_tA, r_tB], writes=[r_dst])
            P.op("dve", lambda e: e.tensor_tensor(d4[:, :, 1, :], A4[:, :, 1, :], B4[:, :, 1, :], ALU.add), reads=[r_tA, r_tB], writes=[r_dst])

        def phase_a(u, first_zero):
            if first_zero:
                P.op("dve", lambda e: e.memset(Sst[:], 0.0), writes=[r_S])
            else:
                P.op("dve", lambda e: e.tensor_scalar(Sst[:], Sst[:], cs[:, C_META:C_META + 1], None, ALU.mult), reads=[r_tab, r_S], writes=[r_S])
            SB = [dict(hb=hb, r_hb=r_hb, hT=hT, r_hT=r_hT, kr=qk[:, 512:1024], r_kr=r_qk, vb=vb[:], r_vb=r_vb, kf=kfb[:], r_kf=r_kfb,
                       tA=TA[:, 0:512], r_tA=r_TA, tB=TB[:, 0:512], r_tB=r_TB, junk=TD[:], r_junk=r_TD, sm=sm, r_sm=r_sm, tr=0, bk=3, bv=4, bw=5),
                  dict(hb=hb2, r_hb=r_hb2, hT=hT2, r_hT=r_hT2, kr=qx[:, 512:1024], r_kr=r_qx, vb=Pm[:, 0:512], r_vb=r_Pm, kf=mixed[:, 0:512], r_kf=r_mixed,
                       tA=TC[:, 0:512], r_tA=r_TC, tB=TC[:, 512:1024], r_tB=r_TC, junk=YB[:, 1024:2048], r_junk=r_pT, sm=smY, r_sm=r_smY, tr=1, bk=6, bv=7, bw=2)]
            load_x(xs[u, (NCH - 1) * 128:NCH * 128, :], 0)
            load_x(xs[u, (NCH - 2) * 128:(NCH - 1) * 128, :], 1)

            def front(c, st):
                B = SB[st]
                prenorm(xc[st][:], [r_xc[st]], B["hb"][:], [B["r_hb"]], 0, junk=B["junk"], r_junk=B["r_junk"], smt=B["sm"], r_smt=B["r_sm"])
                if c - 2 >= 0:
                    load_x(xs[u, (c - 2) * 128:(c - 1) * 128, :], st)
                transposes(B["hb"], B["r_hb"], [(B["hT"][:, 0:4, :], 0, 4, "act"), (B["hT"][:, 4:8, :], 4, 8, "dve")], [B["r_hT"]], B["tr"])
                P.dma("sp", lambda e: e.dma_start(out=s_hT[u, c].rearrange("p (k t) -> p k t", k=8), in_=B["hT"][:]), reads=[B["r_hT"]], writes=[r_scr_h[u][c]], key="sth%d" % st)
                for n, b in ((1, B["bk"]), (2, B["bv"])):
                    for k in range(8):
                        P.op("pe", lambda e, n=n, k=k, b=b: e.matmul(bank(b), B["hT"][:, k, :], win_v[:, k, n * 512:(n + 1) * 512], start=(k == 0), stop=(k == 7)),
                             reads=[B["r_hT"], r_win[k // 4]], writes=[r_bank[b]])
                rotary(B["bk"], 1, c, B["kr"], tA=B["tA"], r_tA=B["r_tA"], tB=B["tB"], r_tB=B["r_tB"], r_dst=B["r_kr"])
                P.op("act", lambda e: e.activation(B["vb"], bank(B["bv"]), AF.Copy), reads=[r_bank[B["bv"]]], writes=[B["r_vb"]])
                P.dma("sp", lambda e: e.dma_start(out=s_kv[u, c][:, 0:512], in_=B["kr"]), reads=[B["r_kr"]], writes=[r_scr_k[u][c]], key="stk%d" % st)
                P.dma("sp", lambda e: e.dma_start(out=s_kv[u, c][:, 512:1024], in_=B["vb"]), reads=[B["r_vb"]], writes=[r_scr_v[u][c]], key="stv%d" % st)
                P.op("dve", lambda e: e.tensor_tensor(B["kf"].rearrange("p (h t) -> p h t", t=128), B["kr"].rearrange("p (h t) -> p h t", t=128),
                                                      cs[:, C_KBS:C_KBS + 4].unsqueeze(2).to_broadcast([128, 4, 128]), ALU.mult),
                     reads=[B["r_kr"], r_tab], writes=[B["r_kf"]])
                for h in range(4):
                    hs = slice(h * 128, (h + 1) * 128)
                    P.op("pe", lambda e, hs=hs: e.matmul(bank(B["bw"])[:, hs], B["kf"][:, hs], B["vb"][:, hs], start=True, stop=True),
                         reads=[B["r_kf"], B["r_vb"]], writes=[r_bank[B["bw"]]])

            def tail(c, st):
                B = SB[st]
                P.op("act", lambda e: e.activation(Rb[st][:], Sst[:], AF.Copy), reads=[r_S], writes=[r_Rb[st]])
                P.dma("sp", lambda e: e.dma_start(out=s_rb[u, c], in_=Rb[st][:]), reads=[r_Rb[st]], writes=[r_scr_rb[u][c]], key="strb%d" % st)
                for h in range(4):
                    hs = slice(h * 128, (h + 1) * 128)
                    P.op("dve", lambda e, h=h, hs=hs: e.scalar_tensor_tensor(Sst[:, hs], Sst[:, hs], cs[:, C_GB + h:C_GB + h + 1], bank(B["bw"])[:, hs], ALU.mult, ALU.add),
                         reads=[r_S, r_tab, r_bank[B["bw"]]], writes=[r_S])

            for p in range(NCH // 2):
                c0 = NCH - 1 - 2 * p
                F0 = P.record(lambda: front(c0, 0))
                F1 = P.record(lambda: front(c0 - 1, 1))
                P.play(F0, F1)
                tail(c0, 0)
                tail(c0 - 1, 1)

        r_scr_rb = [[Res("s_rb%d_%d" % (u, c)) for c in range(NCH)] for u in range(NU)]
        r_scr_k = [[Res("s_k%d_%d" % (u, c)) for c in range(NCH)] for u in range(NU)]
        r_scr_h = [[Res("s_h%d_%d" % (u, c)) for c in range(NCH)] for u in range(NU)]
        r_scr_v = [[Res("s_v%d_%d" % (u, c)) for c in range(NCH)] for u in range(NU)]
        r_scr_x2 = [[Res("s_x2_%d_%d" % (u, c)) for c in range(NCH)] for u in range(NU)]

        def mem_kv(u):
            for mc in range(2):
                i = mc
                load_x(mems[u, mc * 128:(mc + 1) * 128, :], i)
                prenorm(xc[i][:], [r_xc[i]], hb[:], [r_hb], 0)
                tr = bankbf(0)
                for j in range(8):
                    P.op("pe", lambda e, j=j: e.transpose(tr[:, j * 128:(j + 1) * 128], hb[:, j * 128:(j + 1) * 128], ident[:]), reads=[r_hb], writes=[r_bank[0]])
                P.op("act", lambda e, mc=mc: e.activation(memT[:, :, mc * 128:(mc + 1) * 128], tr.rearrange("p (k t) -> p k t", t=128), AF.Copy),
                     reads=[r_bank[0]], writes=[r_Pm, r_pT])
            stg = [TA[:].bitcast(BF16), TB[:].bitcast(BF16)]
            rstg = [r_TA, r_TB]
            for rnd in range(2):
                for k in range(8):
                    i = k % 2
                    P.dma("sp", lambda e, k=k, i=i: e.dma_start(out=stg[i], in_=s_wkv[k * 128:(k + 1) * 128, :]), after=r_scr["s_wkv"], writes=[rstg[i]], key="ldkv%d" % i)
                    for dq in range(4):
                        dc = rnd * 4 + dq
                        P.op("pe", lambda e, k=k, i=i, dq=dq, dc=dc: e.matmul(bank(dq)[:, 0:256], stg[i][:, dc * 128:(dc + 1) * 128], memT[:, k, :], start=(k == 0), stop=(k == 7)),
                             reads=[rstg[i], r_Pm, r_pT], writes=[r_bank[dq]])
                    if rnd == 0:
                        for mc in range(2):
                            for hf in range(2):
                                b = 4 + mc * 2 + hf
                                P.op("pe", lambda e, k=k, i=i, mc=mc, hf=hf, b=b: e.matmul(bank(b), memT[:, k, mc * 128:(mc + 1) * 128], stg[i][:, 1024 + hf * 512:1024 + (hf + 1) * 512],
                                                                                   start=(k == 0), stop=(k == 7)),
                                     reads=[rstg[i], r_Pm, r_pT], writes=[r_bank[b]])
                for dq in range(4):
                    dc = rnd * 4 + dq
                    P.op("act", lambda e, dq=dq, dc=dc: e.activation(memkT[:, dc, :], bank(dq)[:, 0:256], AF.Copy), reads=[r_bank[dq]], writes=[r_memkT])
                if rnd == 0:
                    for mc in range(2):
                        P.op("dve", lambda e, mc=mc: e.tensor_copy(memv[:, mc, :], bank(4 + mc * 2, 2)), reads=[r_bank[4 + mc * 2], r_bank[5 + mc * 2]], writes=[r_memv])

        XB_Q, XB_K, XB_V, XB_G = 2, 3, 4, 5
        YB0 = 6

        def chunk_x(u, c):
            i = c % 3
            j = c % 2
            x = xc[i]
            rx = [r_xc[i]]
            if c + 1 < NCH:
                P.dma("sp", lambda e: e.dma_start(out=Rb[1 - j][:], in_=s_rb[u, c + 1]), reads=[r_scr_rb[u][c + 1]], writes=[r_Rb[1 - j]], key="ldrb%d" % (1 - j))

            def zp(n, b):
                for k in range(8):
                    P.op("pe", lambda e, k=k: e.matmul(bank(b), hT[:, k, :], win_v[:, k, n * 512:(n + 1) * 512], start=(k == 0), stop=(k == 7)),
                         reads=[r_hT, r_win[k // 4]], writes=[r_bank[b]])
            P.dma("sp", lambda e: e.dma_start(out=qk[:, 512:1024], in_=s_kv[u, c][:, 0:512]), reads=[r_scr_k[u][c]], writes=[r_qk], key="ldk")
            P.dma("sp", lambda e: e.dma_start(out=vb[:], in_=s_kv[u, c][:, 512:1024]), reads=[r_scr_v[u][c]], writes=[r_vb], key="ldv")
            zp(0, 2); zp(3, 5); zp(4, 3); zp(5, 4)
            if c + 1 < NCH:
                P.dma("sp", lambda e: e.dma_start(out=hT[:], in_=s_hT[u, c + 1].rearrange("p (k t) -> p k t", k=8)), reads=[r_scr_h[u][c + 1]], writes=[r_hT], key="ldh")
            rotary(2, 1, c, qk[:, 0:512])
            P.op("act", lambda e: e.activation(TB[:, 0:512], bank(5), AF.Tanh, scale=0.5), reads=[r_bank[5]], writes=[r_TB])
            P.op("dve", lambda e: e.scalar_tensor_tensor(sg[:], TB[:, 0:512], 1.0, bank(5), ALU.add, ALU.mult), reads=[r_TB, r_bank[5]], writes=[r_sg])
            P.op("dve", lambda e: e.tensor_tensor(kfb[:], qk[:, 512:1024], KF[:].rearrange("p h e -> p (h e)"), ALU.mult), reads=[r_qk, r_tab], writes=[r_kfb])
            tr = bankbf(QKTR[0])
            for jj in range(8):
                P.op("pe", lambda e, jj=jj: e.transpose(tr[:, jj * 128:(jj + 1) * 128], qk[:, jj * 128:(jj + 1) * 128], ident[:]), reads=[r_qk], writes=[r_bank[QKTR[0]]])
            tq = tr[:, 0:512].rearrange("p (h t) -> p h t", t=128)
            for hh in range(2):
                if DUMMY[0] == 2:
                    P.op("dve", lambda e, hh=hh: e.tensor_copy(qkT[:, hh * 4:(hh + 1) * 4, :], tr[:, hh * 512:(hh + 1) * 512].rearrange("p (k t) -> p k t", t=128)),
                         reads=[r_bank[QKTR[0]]], writes=[r_qkT])
                else:
                    P.op("act", lambda e, hh=hh: e.activation((hT2 if DUMMY[0] else qkT)[:, hh * 4:(hh + 1) * 4, :], tr[:, hh * 512:(hh + 1) * 512].rearrange("p (k t) -> p k t", t=128), AF.Copy),
                         reads=[r_bank[QKTR[0]]], writes=[r_qkT])
            P.op("dve", lambda e: e.tensor_tensor(qfT[:], tq, Ab[:], ALU.mult), reads=[r_bank[QKTR[0]], r_tab], writes=[r_qfT])
            P.op("dve", lambda e: e.tensor_tensor(qbT[:], tq, Bb[:], ALU.mult), reads=[r_bank[QKTR[0]], r_tab], writes=[r_qbT])
            for h in range(4):
                P.op("pe", lambda e, h=h: e.matmul(bank(2)[:, h * 128:(h + 1) * 128], qkT[:, 4 + h, :], qkT[:, h, :], start=True, stop=True),
                     reads=[r_qkT], writes=[r_bank[2]])
            P.op("dve", lambda e: e.tensor_tensor(sTm[:], bank(2).rearrange("p (h t) -> p h t", t=128), DT[:], ALU.mult), reads=[r_bank[2], r_tab], writes=[r_sTm])
            zuv = bank(3, 2)
            ruv = [r_bank[3], r_bank[4]]
            P.op("act", lambda e: e.activation(TB[:], zuv, AF.Square, scale=float(np.sqrt(0.044715))), reads=ruv, writes=[r_TB])
            P.op("dve", lambda e: e.scalar_tensor_tensor(TD[:], TB[:], 1.0, zuv, ALU.add, ALU.mult), reads=[r_TB] + ruv, writes=[r_TD])
            P.op("act", lambda e: e.activation(TB[:], TD[:], AF.Tanh, scale=GELU_C), reads=[r_TD], writes=[r_TB])
            P.op("dve", lambda e: e.scalar_tensor_tensor(TD[:], TB[:], 1.0, zuv, ALU.add, ALU.mult), reads=[r_TB] + ruv, writes=[r_TD])
            for h in range(4):
                hs = slice(h * 128, (h + 1) * 128)
                P.op("pe", lambda e, h=h, hs=hs: e.matmul(bank(5)[:, hs], sTm[:, h, :], vb[:, hs], start=True, stop=False), reads=[r_sTm, r_vb], writes=[r_bank[5]])
                P.op("pe", lambda e, h=h, hs=hs: e.matmul(bank(5)[:, hs], qfT[:, h, :], Rf[:, hs], start=False, stop=False), reads=[r_qfT, r_Rf], writes=[r_bank[5]])
                P.op("pe", lambda e, h=h, hs=hs: e.matmul(bank(5)[:, hs], qbT[:, h, :], Rb[j][:, hs], start=False, stop=True), reads=[r_qbT, r_Rb[j]], writes=[r_bank[5]])
            for h in range(4):
                hs = slice(h * 128, (h + 1) * 128)
                P.op("pe", lambda e, hs=hs: e.matmul(bank(2)[:, hs], kfb[:, hs], vb[:, hs], start=True, stop=True), reads=[r_kfb, r_vb], writes=[r_bank[2]])
            for h in range(4):
                hs = slice(h * 128, (h + 1) * 128)
                P.op("dve", lambda e, h=h, hs=hs: e.scalar_tensor_tensor(Sst[:, hs], Sst[:, hs], cs[:, C_GF + h:C_GF + h + 1], bank(2)[:, hs], ALU.mult, ALU.add),
                     reads=[r_S, r_tab, r_bank[2]], writes=[r_S])
            P.op("act", lambda e: e.activation(Rf[:], Sst[:], AF.Copy), reads=[r_S], writes=[r_Rf])
            P.op("dve", lambda e: e.tensor_reduce(sm[:, 2:3], TD[:, 512:1024], AX.X, ALU.add), reads=[r_TD, r_sm], writes=[r_sm])
            P.op("act", lambda e: e.activation(TB[:, 512:1024], TD[:, 512:1024], AF.Square, accum_out=sm[:, 3:4]), reads=[r_TD, r_sm], writes=[r_TB, r_sm])
            P.op("pool", lambda e: e.tensor_scalar(sm[:, 2:4], sm[:, 2:4], 1.0 / 512.0, None, ALU.mult), reads=[r_sm], writes=[r_sm])
            P.op("pool", lambda e: e.tensor_tensor(sm[:, 4:5], sm[:, 2:3], sm[:, 2:3], ALU.mult), reads=[r_sm], writes=[r_sm])
            P.op("pool", lambda e: e.tensor_tensor(sm[:, 5:6], sm[:, 3:4], sm[:, 4:5], ALU.subtract), reads=[r_sm], writes=[r_sm])
            rstd_ops(5, 6, 4.0 * EPS)
            P.op("dve", lambda e: e.tensor_scalar(TB[:, 512:1024], TD[:, 512:1024], sm[:, 2:3], sm[:, 6:7], ALU.subtract, ALU.mult), reads=[r_TD, r_sm], writes=[r_TB])
            P.op("dve", lambda e: e.tensor_tensor(vsn[:], TB[:, 512:1024], NWs[:], ALU.mult), reads=[r_TB, r_tab], writes=[r_vsn])
            y3 = bank(5).rearrange("p (h t) -> p h t", t=128)
            P.op("dve", lambda e: e.tensor_reduce(sm[:, 8:12], y3, AX.X, ALU.add), reads=[r_bank[5], r_sm], writes=[r_sm])
            P.op("act", lambda e: e.activation(TA[:, 512:1024], bank(5), AF.Square), reads=[r_bank[5]], writes=[r_TA])
            P.op("dve", lambda e: e.tensor_reduce(sm[:, 12:16], TA[:, 512:1024].rearrange("p (h t) -> p h t", t=128), AX.X, ALU.add), reads=[r_TA, r_sm], writes=[r_sm])
            P.op("pool", lambda e: e.tensor_scalar(sm[:, 8:16], sm[:, 8:16], 1.0 / 128.0, None, ALU.mult), reads=[r_sm], writes=[r_sm])
            P.op("pool", lambda e: e.tensor_tensor(sm[:, 16:20], sm[:, 8:12], sm[:, 8:12], ALU.mult), reads=[r_sm], writes=[r_sm])
            P.op("pool", lambda e: e.tensor_tensor(sm[:, 20:24], sm[:, 12:16], sm[:, 16:20], ALU.subtract), reads=[r_sm], writes=[r_sm])
            P.op("pool", lambda e: e.tensor_scalar(sm[:, 20:24], sm[:, 20:24], EPS, None, ALU.add), reads=[r_sm], writes=[r_sm])
            P.op("pool", lambda e: e.tensor_tensor(sm[:, 24:28], sm[:, 20:24], cs[:, C_M05:C_M05 + 1].to_broadcast([128, 4]), ALU.pow), reads=[r_sm, r_tab], writes=[r_sm])
            for h in range(4):
                hs = slice(h * 128, (h + 1) * 128)
                P.op("dve", lambda e, h=h, hs=hs: e.tensor_scalar(TA[:, hs], bank(5)[:, hs], sm[:, 8 + h:9 + h], sm[:, 24 + h:25 + h], ALU.subtract, ALU.mult),
                     reads=[r_bank[5], r_sm], writes=[r_TA])
            P.op("dve", lambda e: e.tensor_tensor(mixed[:, 0:512], TA[:, 0:512], sg[:], ALU.mult), reads=[r_TA, r_sg], writes=[r_mixed])
            for g in range(4):
                gs = slice(g * 128, (g + 1) * 128)
                P.op("pe", lambda e, g=g, gs=gs: e.matmul(bank(2)[:, gs], sguwT[:, g, :], vsn[:, gs], start=True, stop=True), reads=[r_vsn, r_tab], writes=[r_bank[2]])
            for g in range(4):
                gs = slice(g * 128, (g + 1) * 128)
                P.op("dve", lambda e, g=g, gs=gs: e.scalar_tensor_tensor(mixed[:, 512 + g * 128:640 + g * 128], bank(2)[:, gs], cs[:, C_SGUB + g:C_SGUB + g + 1],
                                                                        TD[:, gs], ALU.add, ALU.mult),
                     reads=[r_bank[2], r_tab, r_TD], writes=[r_mixed])

        def chunk_y(u, c):
            i = c % 3
            x = xc[i]
            rx = [r_xc[i]]
            ykw = dict(junk=Pm[:], r_junk=r_Pm, smt=smY, r_smt=r_smY)
            transposes(mixed, r_mixed, [(mixedT[:, 0:4, :], 0, 4, "act"), (mixedT[:, 4:8, :], 4, 8, "dve")], [r_mixedT], 1)
            for hf in range(2):
                for k in range(8):
                    P.op("pe", lambda e, hf=hf, k=k: e.matmul(bank(6 + hf), mixedT[:, k, :], wout_v[:, k, hf * 512:(hf + 1) * 512], start=(k == 0), stop=(k == 7)),
                         reads=[r_mixedT, r_wout], writes=[r_bank[6 + hf]])
            postnorm_residual(6, NWp[0], r_NWp[0], x[:], rx, 28, tt=TC, r_tt=r_TC, **ykw)
            if debug:
                P.dma("pool", lambda e: e.dma_start(out=dbg_x1[u, c * 128:(c + 1) * 128, :], in_=x[:]), reads=rx, key="dbgx1")
            prenorm(x[:], rx, hb2[:], [r_hb2], 30, **ykw)
            transposes(hb2, r_hb2, [(hT2[:, 0:4, :], 0, 4, "act"), (hT2[:, 4:8, :], 4, 8, "dve")], [r_hT2], 1)
            for dc in range(8):
                for k in range(8):
                    P.op("pe", lambda e, dc=dc, k=k: e.matmul(bank(6 + dc // 4)[:, (dc % 4) * 128:(dc % 4 + 1) * 128], wq_v[:, k, dc * 128:(dc + 1) * 128], hT2[:, k, :],
                                                         start=(k == 0), stop=(k == 7)),
                         reads=[r_hT2, r_wq], writes=[r_bank[6 + dc // 4]])
            P.op("act", lambda e: e.activation(qxT[:, 0:4, :], bank(6).rearrange("p (k t) -> p k t", t=128), AF.Copy), reads=[r_bank[6]], writes=[r_qxT])
            P.op("dve", lambda e: e.tensor_copy(qxT[:, 4:8, :], bank(7).rearrange("p (k t) -> p k t", t=128)), reads=[r_bank[7]], writes=[r_qxT])
            for h in range(4):
                for dc in range(2):
                    P.op("pe", lambda e, h=h, dc=dc: e.matmul(bank(6 + h // 2)[:, (h % 2) * 256:(h % 2 + 1) * 256], qxT[:, 2 * h + dc, :], memkT[:, 2 * h + dc, :],
                                                         start=(dc == 0), stop=(dc == 1)),
                         reads=[r_qxT, r_memkT], writes=[r_bank[6 + h // 2]])
            sc4 = bank(6, 2).rearrange("p (h m) -> p h m", m=256)
            P.op("dve", lambda e: e.tensor_reduce(smY[:, 48:52], sc4, AX.X, ALU.max), reads=[r_bank[6], r_bank[7], r_smY], writes=[r_smY])
            P.op("pool", lambda e: e.tensor_scalar(smY[:, 52:56], smY[:, 48:52], -1.0 / 16.0, None, ALU.mult), reads=[r_smY], writes=[r_smY])
            for h in range(4):
                P.op("act", lambda e, h=h: e.activation(Pm[:, h * 256:(h + 1) * 256], bank(6, 2)[:, h * 256:(h + 1) * 256], AF.Exp, bias=smY[:, 52 + h:53 + h],
                                                       scale=1.0 / 16.0, accum_out=smY[:, 56 + h:57 + h]),
                     reads=[r_bank[6 + h // 2], r_smY], writes=[r_Pm, r_smY])
            P.op("dve", lambda e: e.reciprocal(smY[:, 60:64], smY[:, 56:60]), reads=[r_smY], writes=[r_smY])
            transposes(Pm, r_Pm, [(pT[:, 0:4, :], 0, 4, "act"), (pT[:, 4:8, :], 4, 8, "dve")], [r_pT], 1)
            for h in range(4):
                for mc in range(2):
                    P.op("pe", lambda e, h=h, mc=mc: e.matmul(bank(6 + h // 2)[:, (h % 2) * 256:(h % 2 + 1) * 256], pT[:, 2 * h + mc, :], memv[:, mc, h * 256:(h + 1) * 256],
                                                         start=(mc == 0), stop=(mc == 1)),
                         reads=[r_pT, r_memv], writes=[r_bank[6 + h // 2]])
            for h in range(4):
                P.op("dve", lambda e, h=h: e.tensor_scalar(qx[:, h * 256:(h + 1) * 256], bank(6, 2)[:, h * 256:(h + 1) * 256], smY[:, 60 + h:61 + h], None, ALU.mult),
                     reads=[r_bank[6 + h // 2], r_smY], writes=[r_qx])
            transposes(qx, r_qx, [(qxT[:, 0:4, :], 0, 4, "act"), (qxT[:, 4:8, :], 4, 8, "dve")], [r_qxT], 1)
            for hf in range(2):
                for k in range(8):
                    P.op("pe", lambda e, hf=hf, k=k: e.matmul(bank(6 + hf), qxT[:, k, :], wo_v[:, k, hf * 512:(hf + 1) * 512], start=(k == 0), stop=(k == 7)),
                         reads=[r_qxT, r_wo], writes=[r_bank[6 + hf]])
            postnorm_residual(6, NWp[1], r_NWp[1], x[:], rx, 34, tt=TC, r_tt=r_TC, **ykw)
            P.dma("sp", lambda e: e.dma_start(out=s_x2[u, c * 128:(c + 1) * 128, :], in_=x[:]), reads=rx, writes=[r_scr_x2[u][c]], key="stx%d" % i)

        def phase_m(u, carry):
            mem_kv(u)
            if carry:
                P.op("dve", lambda e: e.tensor_scalar(Sst[:], Sst[:], cs[:, C_META:C_META + 1], None, ALU.mult), reads=[r_tab, r_S], writes=[r_S])
            else:
                P.op("dve", lambda e: e.memset(Sst[:], 0.0), writes=[r_S])
            P.op("act", lambda e: e.activation(Rf[:], Sst[:], AF.Copy), reads=[r_S], writes=[r_Rf])
            load_x(xs[u, 0:128, :], 0, "sp")
            load_x(xs[u, 128:256, :], 1, "sp")
            P.dma("sp", lambda e: e.dma_start(out=Rb[0][:], in_=s_rb[u, 0]), reads=[r_scr_rb[u][0]], writes=[r_Rb[0]], key="ldrb0")
            P.dma("sp", lambda e: e.dma_start(out=hT[:], in_=s_hT[u, 0].rearrange("p (k t) -> p k t", k=8)), reads=[r_scr_h[u][0]], writes=[r_hT], key="ldh")
            XL = P.record(lambda: chunk_x(u, 0))[:XCUT[0]]
            P.play(XL)
            for c in range(NCH):
                if c + 2 < NCH:
                    load_x(xs[u, (c + 2) * 128:(c + 3) * 128, :], (c + 2) % 3, "sp")
                YL = P.record(lambda: chunk_y(u, c)) if not SKIPY[0] else P.record(lambda: P.dma("pool", lambda e, c=c: e.dma_start(out=s_x2[u, c * 128:(c + 1) * 128, :], in_=xc[c % 2][:]), reads=[r_xc[c % 2]], writes=[r_scr_x2[u][c]], key="stx%d" % (c % 2)))
                XL = []
                if c + 1 < NCH:
                    def bx(c=c):
                        if c + 2 < NCH:
                            pass
                        chunk_x(u, c + 1)
                    XL = P.record(bx)[:XCUT[0]]
                P.play(XL, YL)

        wgu_cnt = [0]
        wdn_cnt = [0]
        alias_done = set()

        def alias(eng, *rs):
            out = []
            for r in rs:
                if (eng, r.name) not in alias_done:
                    alias_done.add((eng, r.name))
                    out.append(r)
            return out

        r_allw = [r_win[0], r_win[1], r_wout, r_wq, r_wo]

        def f_prologue(u, bi, par):
            t0 = bi * 512
            res = []
            for c in range(4):
                def body(c=c):
                    xr = xress[par][:, c, :]
                    P.dma("pool", lambda e: e.dma_start(out=xr, in_=s_x2[u, t0 + c * 128:t0 + (c + 1) * 128, :]), reads=[r_scr_x2[u][bi * 4 + c]],
                          writes=[r_xress[par][c]] + alias("pool", *r_allw), key="ldx2_%d" % c)
                    prenorm(xr, [r_xress[par][c]], hb[:], [r_hb], 0, junk=mixed[:], r_junk=r_mixed)
                    tb = 6 + (c % 2)
                    tr = bankbf(tb)
                    for jj in range(8):
                        P.op("pe", lambda e, jj=jj: e.transpose(tr[:, jj * 128:(jj + 1) * 128], hb[:, jj * 128:(jj + 1) * 128], ident[:]), reads=[r_hb], writes=[r_bank[tb]])
                    P.op("dve", lambda e: e.tensor_copy(h3T_vs[par][:, :, c * 128:(c + 1) * 128], tr.rearrange("p (k t) -> p k t", t=128)),
                         reads=[r_bank[tb]], writes=[r_h3Ts[par][c]] + alias("dve", *r_allw))
                res += P.record(body)
            return res

        def f_gateup(u, bi, par):
            lists = []
            for f in range(NF):
                def body(f=f):
                    s_ = wgu_cnt[0] % 3
                    wgu_cnt[0] += 1
                    P.dma("sp", lambda e: e.dma_start(out=wgu_ring[s_], in_=s_wgu[f]), after=r_scr["s_wgu"], writes=[r_wgur[s_]] + alias("sp", *r_allw), key="ldgu%d" % s_)
                    pb = 2 * (f % 2)
                    for gu in range(2):
                        for k in range(8):
                            P.op("pe", lambda e, gu=gu, k=k: e.matmul(bank(pb + gu), wgu_ring[s_][:, k, gu, :], h3T_vs[par][:, k, :], start=(k == 0), stop=(k == 7)),
                                 reads=[r_wgur[s_]] + r_h3Ts[par], writes=[r_bank[pb + gu]])
                    tt = TC if f % 2 == 0 else TD
                    rtt = r_TC if f % 2 == 0 else r_TD
                    P.op("act", lambda e: e.activation(tt[:, 0:512], bank(pb), AF.Tanh, scale=0.5), reads=[r_bank[pb]], writes=[rtt])
                    P.op("dve", lambda e: e.scalar_tensor_tensor(tt[:, 512:1024], tt[:, 0:512], 1.0, bank(pb), ALU.add, ALU.mult), reads=[rtt, r_bank[pb]], writes=[rtt])
                    P.op("dve", lambda e: e.tensor_tensor(act_v[:, f, :], tt[:, 512:1024], bank(pb + 1), ALU.mult), reads=[rtt, r_bank[pb + 1]],
                         writes=[r_act[f]] + alias("dve", *r_allw))
                lists.append(P.record(body))
            return lists

        def f_down(u, bi):
            for fp in range(NF // 2):
                s_ = wdn_cnt[0] % 3
                wdn_cnt[0] += 1
                P.dma("sp", lambda e, fp=fp, s_=s_: e.dma_start(out=wdn_ring[s_], in_=s_wdn[fp * 256:(fp + 1) * 256, :].rearrange("(f p) c -> p f c", p=128)),
                      after=r_scr["s_wdn"], writes=[r_wdnr[s_]] + alias("sp", *r_allw), key="lddn%d" % s_)
                for fi in range(2):
                    f = fp * 2 + fi
                    for c in range(4):
                        for hf in range(2):
                            P.op("pe", lambda e, f=f, fi=fi, c=c, hf=hf, s_=s_: e.matmul(bank(2 * c + hf), act_v[:, f, c * 128:(c + 1) * 128], wdn_ring[s_][:, fi, hf * 512:(hf + 1) * 512],
                                                                               start=(f == 0), stop=(f == NF - 1)),
                                 reads=[r_act[f], r_wdnr[s_]], writes=[r_bank[2 * c + hf]])

        def f_epilogue(u, bi, par):
            t0 = bi * 512
            lists = []
            for c in range(4):
                def body(c=c):
                    i = c % 2
                    postnorm_residual(2 * c, NWp[2], r_NWp[2], xress[par][:, c, :], [r_xress[par][c]], 0, out_tile=xc[i][:], r_out=[r_xc[i]],
                                      junk=qx[:], r_junk=r_qx, smt=smY, r_smt=r_smY)
                    P.dma("pool", lambda e: e.dma_start(out=outp[u, t0 + c * 128:t0 + (c + 1) * 128, :], in_=xc[i][:]), reads=[r_xc[i]], key="sto%d" % i)
                lists.append(P.record(body))
            return lists

        def phase_f_all():
            blocks = [(u, bi) for u in range(NU) for bi in range(UT // 512)]
            P.play(f_prologue(blocks[0][0], blocks[0][1], 0))
            if FSTOP[0] == 0:
                return
            for n, (u, bi) in enumerate(blocks):
                par = n % 2
                if FSTOP[0] == 3 and n >= 1:
                    return
                gl = f_gateup(u, bi, par)
                epi = f_epilogue(*blocks[n - 1], 1 - par) if n > 0 else [[], [], [], []]
                pro = f_prologue(*blocks[n + 1], 1 - par) if n + 1 < len(blocks) else []
                P.play(epi[0][:ECUT[0]] if n == 1 else epi[0])
                if FSTOP[0] == 4 and n == 1:
                    return
                P.play(gl[0])
                if FSTOP[0] == 5 and n == 1:
                    return
                P.play(epi[1])
                if FSTOP[0] == 6 and n == 1:
                    return
                P.play(gl[1])
                if FSTOP[0] == 7 and n == 1:
                    return
                main = [t for l in gl[2:] for t in l]
                side = epi[2] + epi[3] + pro
                P.play(main, side)
                if FSTOP[0] == 1 or (FSTOP[0] == 8 and n == 1):
                    return
                P.play(P.record(lambda: f_down(u, bi)))
                if FSTOP[0] == 2 or (FSTOP[0] >= 10 and n == FSTOP[0] - 9):
                    return
            for li, l in enumerate(f_epilogue(*blocks[-1], (len(blocks) - 1) % 2)):
                if FSTOP[0] >= 30 and li >= FSTOP[0] - 30:
                    break
                P.play(l)

        STOP = STOPAT[0]
        if STOP >= 1:
            rope_tables(1)
            phase_a(1, True)
        if STOP >= 2:
            rope_tables(0)
            phase_a(0, False)
        if STOP == 3:
            mem_kv(0)
        if STOP >= 4:
            phase_m(0, False)
        if STOP >= 5:
            rope_tables(1)
            phase_m(1, True)
            rope_tables(0)
            phase_a(2, True)
            phase_m(2, False)
        if STOP >= 6:
            P.dma("sp", lambda e: e.dma_start(out=NWp[0][:], in_=norm_w[6].partition_broadcast(128)), writes=[r_NWp[0]], key="c_nwp0")
            phase_f_all()
        with nc.allow_non_contiguous_dma(reason="setup loads / weight scratch layout"):
            P.emit(nc)
    return nc


_NC_CACHE = {}
_DEBUG = [False]


def kernel(x_prompt, x_sample, mem_prompt, mem_sample, norm_w, w_in, ret_log_gamma, ret_gn_w,
           sgu_norm_w, sgu_w, sgu_b, w_out, xa_wq, xa_wkv, xa_wo, ffn_w_gu, ffn_w_down):
    f = lambda a: np.ascontiguousarray(np.asarray(a, dtype=np.float32))
    x_prompt, x_sample, mem_prompt, mem_sample = f(x_prompt), f(x_sample), f(mem_prompt), f(mem_sample)
    shared = {
        "norm_w": f(norm_w)[0], "w_in": f(w_in)[0], "ret_log_gamma": f(ret_log_gamma)[0].reshape(8),
        "ret_gn_w": f(ret_gn_w)[0], "sgu_norm_w": f(sgu_norm_w)[0], "sgu_w": f(sgu_w)[0], "sgu_b": f(sgu_b)[0],
        "w_out": f(w_out)[0], "xa_wq": f(xa_wq)[0], "xa_wkv": f(xa_wkv)[0], "xa_wo": f(xa_wo)[0],
        "ffn_w_gu": f(ffn_w_gu)[0], "ffn_w_down": f(ffn_w_down)[0],
    }
    in_maps = []
    for core in range(8):
        if core < 4:
            xs = np.stack([x_sample[core, :UT], x_sample[core, UT:], x_prompt[core]])
            mm = np.stack([mem_sample[core], mem_sample[core], mem_prompt[core]])
            link = 1.0
        else:
            p0 = 4 + 3 * (core - 4)
            xs = np.stack([x_prompt[p0], x_prompt[p0 + 1], x_prompt[p0 + 2]])
            mm = np.stack([mem_prompt[p0], mem_prompt[p0 + 1], mem_prompt[p0 + 2]])
            link = 0.0
        meta = np.zeros((128, 4), np.float32)
        meta[:, 0] = link
        meta[:, 2] = link * UT
        d = dict(shared)
        d.update({"xs": np.ascontiguousarray(xs), "mems": np.ascontiguousarray(mm), "meta": meta})
        in_maps.append(d)
    if "nc" not in _NC_CACHE:
        _NC_CACHE["nc"] = build_program(debug=_DEBUG[0])
    res = run_bass_kernel_spmd(_NC_CACHE["nc"], in_maps, core_ids=list(range(8)))
    if _DEBUG[0]:
        _DEBUG.append(res)
    y_prompt = np.empty_like(x_prompt)
    y_sample = np.empty_like(x_sample)
    for core in range(8):
        o = np.asarray(res.results[core]["out"], dtype=np.float32)
        if core < 4:
            y_sample[core, :UT] = o[0]
            y_sample[core, UT:] = o[1]
            y_prompt[core] = o[2]
        else:
            p0 = 4 + 3 * (core - 4)
            y_prompt[p0], y_prompt[p0 + 1], y_prompt[p0 + 2] = o[0], o[1], o[2]
    return (y_prompt, y_sample)
```

```python
import math
from contextlib import ExitStack

import numpy as np
import concourse.bass as bass
import concourse.mybir as mybir
from concourse.bass_utils import run_bass_kernel_spmd

F32 = mybir.dt.float32
BF16 = mybir.dt.bfloat16
I32 = mybir.dt.int32
AF = mybir.ActivationFunctionType
ALU = mybir.AluOpType
AX = mybir.AxisListType

D = 1024
NU = 3
UT = 4096
NCH = UT // 128
DFF = 2816
NF = DFF // 128
EPS = 1e-6
KSCALE = 128 ** -0.5
LN_KSCALE = math.log(KSCALE)
GELU_C = 0.7978845608028654
SEQ_PLAY = [False]
SKIPY = [False]
STOPAT = [9]
XCUT = [100000]
QKTR = [0]
STRICT = [1]
DUMMY = [2]
FSTOP = [9]
ECUT = [1000]


class Res:
    __slots__ = ("name", "last_w", "readers")

    def __init__(self, name):
        self.name = name
        self.last_w = None
        self.readers = {}


class Node:
    __slots__ = ("eng", "fn", "deps", "signal", "token", "dma_key", "idx")


class Prog:
    ENGS = ("pe", "act", "dve", "pool", "sp")

    def __init__(self):
        self.nodes = []
        self.rec = None

    def _grp(self, n):
        return n.dma_key if n.dma_key is not None else "E_" + n.eng

    def _add(self, eng, fn, reads, writes, dma_key=None, after=()):
        n = Node()
        n.eng = eng
        n.fn = fn
        n.signal = dma_key is not None
        n.token = None
        n.dma_key = dma_key
        n.idx = len(self.nodes)
        deps = {}

        def need(d, raw=False):
            dn = self.nodes[d]
            if dn.eng == eng and dn.dma_key is None and dma_key is None:
                if STRICT[0] == 0 and (not raw or eng == "pe"):
                    return
                if STRICT[0] == 1 and eng == "pe":
                    return
            g = self._grp(dn)
            if g not in deps or deps[g] < d:
                deps[g] = d

        for a in after:
            need(a.idx, True)
        for r in reads:
            if r.last_w is not None:
                need(r.last_w, True)
        for w in writes:
            if w.last_w is not None:
                need(w.last_w)
            for rd in w.readers.values():
                need(rd)
        n.deps = sorted(deps.values())
        for d in n.deps:
            self.nodes[d].signal = True
        self.nodes.append(n)
        g = self._grp(n)
        for r in reads:
            r.readers[g] = n.idx
        for w in writes:
            w.last_w = n.idx
            w.readers = {}
        return n

    def op(self, eng, fn, reads=(), writes=(), after=()):
        if self.rec is not None:
            self.rec.append((eng, fn, list(reads), list(writes), None, tuple(after)))
            return None
        return self._add(eng, fn, reads, writes, after=after)

    def dma(self, eng, fn, reads=(), writes=(), key=None, after=()):
        if self.rec is not None:
            self.rec.append((eng, fn, list(reads), list(writes), key, tuple(after)))
            return None
        return self._add(eng, fn, reads, writes, dma_key=key, after=after)

    def record(self, body):
        self.rec = []
        body()
        lst, self.rec = self.rec, None
        return lst

    def play(self, a, b=()):
        i = j = 0
        if SEQ_PLAY[0]:
            for t in list(a) + list(b):
                self._add(*t)
            return
        while i < len(a) or j < len(b):
            if j >= len(b) or (i < len(a) and i * len(b) <= j * len(a)):
                self._add(*a[i]); i += 1
            else:
                self._add(*b[j]); j += 1

    def emit(self, nc):
        cnt = {}
        keys = []
        for n in self.nodes:
            if not n.signal:
                continue
            k = self._grp(n)
            if k not in cnt:
                cnt[k] = 0
                keys.append(k)
            cnt[k] += 16 if n.dma_key is not None else 1
            n.token = (k, cnt[k])
        final = [(k, cnt[k]) for k in keys if not k.startswith("E_")]
        per_eng = {e: [n for n in self.nodes if n.eng == e] for e in self.ENGS}
        nodes = self.nodes
        with ExitStack() as st:
            sems = {k: st.enter_context(nc.semaphore("s_" + k)) for k in keys}
            block = st.enter_context(nc.Block())

            def run(lst, do_final):
                def body(eng):
                    waited = {}
                    for n in lst:
                        for d in n.deps:
                            k, v = nodes[d].token
                            if waited.get(k, 0) < v:
                                eng.wait_ge(sems[k], v)
                                waited[k] = v
                        ins = n.fn(eng)
                        if n.signal:
                            ins.then_inc(sems[n.token[0]], 16 if n.dma_key is not None else 1)
                    if do_final:
                        for k, v in final:
                            if waited.get(k, 0) < v:
                                eng.wait_ge(sems[k], v)
                return body

            block.tensor(run(per_eng["pe"], False))
            block.scalar(run(per_eng["act"], False))
            block.vector(run(per_eng["dve"], False))
            block.gpsimd(run(per_eng["pool"], False))
            block.sync(run(per_eng["sp"], True))


def build_program(debug=False):
    nc = bass.Bass("TRN2", target_bir_lowering=False)
    P = Prog()

    def din(name, shape, dt=F32):
        return nc.dram_tensor(name, shape, dt, kind="ExternalInput").ap()

    def dscr(name, shape, dt):
        return nc.dram_tensor(name, shape, dt, kind="ExternalOutput" if debug else "Internal").ap()

    xs = din("xs", [NU, UT, D])
    mems = din("mems", [NU, 256, D])
    meta = din("meta", [128, 4])
    norm_w = din("norm_w", [7, D])
    w_in = din("w_in", [D, 3072])
    lg_d = din("ret_log_gamma", [8])
    gnw_d = din("ret_gn_w", [512])
    sgunw_d = din("sgu_norm_w", [512])
    sguw_d = din("sgu_w", [4, 128, 128])
    sgub_d = din("sgu_b", [4, 128])
    w_out = din("w_out", [D, D])
    wq_d = din("xa_wq", [D, D])
    wkv_d = din("xa_wkv", [D, 2 * D])
    wo_d = din("xa_wo", [D, D])
    wgu_d = din("ffn_w_gu", [D, 2 * DFF])
    wdn_d = din("ffn_w_down", [DFF, D])
    outp = nc.dram_tensor("out", [NU, UT, D], F32, kind="ExternalOutput").ap()

    s_win = dscr("s_win", [D, 3072], BF16)
    s_wout = dscr("s_wout", [D, D], BF16)
    s_wq = dscr("s_wq", [D, D], BF16)
    s_wkv = dscr("s_wkv", [D, 2 * D], BF16)
    s_wo = dscr("s_wo", [D, D], BF16)
    s_wgu = dscr("s_wgu", [NF, 128, 8, 2, 128], BF16)
    s_wdn = dscr("s_wdn", [DFF, D], BF16)
    s_rb = dscr("s_rb", [NU, NCH, 128, 512], BF16)
    s_kv = dscr("s_kv", [NU, NCH, 128, 1024], BF16)
    s_hT = dscr("s_hT", [NU, NCH, 128, 1024], BF16)
    if debug:
        s_x2 = nc.dram_tensor("dbg_x2", [NU, UT, D], F32, kind="ExternalOutput").ap()
        dbg_x1 = nc.dram_tensor("dbg_x1", [NU, UT, D], F32, kind="ExternalOutput").ap()
    else:
        s_x2 = dscr("s_x2", [NU, UT, D], F32)

    st = ExitStack()
    with st:
        def sb(name, shape, dt):
            return st.enter_context(nc.sbuf_tensor(name, shape, dt))

        WA = sb("WA", [128, 49152], BF16)
        win_v = WA[:, 0:24576].rearrange("p (k c) -> p k c", k=8)
        wout_v = WA[:, 24576:32768].rearrange("p (k c) -> p k c", k=8)
        wq_v = WA[:, 32768:40960].rearrange("p (k c) -> p k c", k=8)
        wo_v = WA[:, 40960:49152].rearrange("p (k c) -> p k c", k=8)
        r_win = [Res("win%d" % i) for i in range(2)]
        r_wout, r_wq, r_wo = Res("wout"), Res("wq"), Res("wo")
        act_v = WA[:, 0:11264].rearrange("p (f t) -> p f t", f=NF)
        h3T_vs = [WA[:, 11264 + i * 4096:15360 + i * 4096].rearrange("p (k t) -> p k t", k=8) for i in range(2)]
        wgu_ring = [WA[:, 19456 + i * 2048: 19456 + (i + 1) * 2048].rearrange("p (k g c) -> p k g c", k=8, g=2) for i in range(3)]
        wdn_ring = [WA[:, 25600 + i * 2048: 25600 + (i + 1) * 2048].rearrange("p (f c) -> p f c", f=2) for i in range(3)]
        r_act = [Res("act%d" % f) for f in range(NF)]
        r_h3Ts = [[Res("h3T%d_%d" % (i, c)) for c in range(4)] for i in range(2)]
        r_wgur = [Res("wgur%d" % i) for i in range(3)]
        r_wdnr = [Res("wdnr%d" % i) for i in range(3)]

        xc = [sb("xc%d" % i, [128, D], F32) for i in range(3)]
        r_xc = [Res("xc0"), Res("xc1"), Res("xc2")]
        xress = [WA[:, 31744 + i * 8192:39936 + i * 8192].bitcast(F32).rearrange("p (c d) -> p c d", c=4) for i in range(2)]
        r_xress = [[Res("xres%d_%d" % (i, c)) for c in range(4)] for i in range(2)]
        TA = sb("TA", [128, D], F32); r_TA = Res("TA")
        TB = sb("TB", [128, D], F32); r_TB = Res("TB")
        TC = sb("TC", [128, D], F32); r_TC = Res("TC")
        TD = sb("TD", [128, D], F32); r_TD = Res("TD")
        hb = sb("hb", [128, D], BF16); r_hb = Res("hb")
        hT = sb("hT", [128, 8, 128], BF16); r_hT = Res("hT")
        qk = sb("qk", [128, D], BF16); r_qk = Res("qk")
        qkT = sb("qkT", [128, 8, 128], BF16); r_qkT = Res("qkT")
        qfT = sb("qfT", [128, 4, 128], BF16); r_qfT = Res("qfT")
        qbT = sb("qbT", [128, 4, 128], BF16); r_qbT = Res("qbT")
        vb = sb("vb", [128, 512], BF16); r_vb = Res("vb")
        kfb = sb("kfb", [128, 512], BF16); r_kfb = Res("kfb")
        sTm = sb("sTm", [128, 4, 128], BF16); r_sTm = Res("sTm")
        Rf = sb("Rf", [128, 512], BF16); r_Rf = Res("Rf")
        Sst = sb("Sst", [128, 512], F32); r_S = Res("S")
        Rb = [sb("Rb%d" % i, [128, 512], BF16) for i in range(2)]
        r_Rb = [Res("Rb0"), Res("Rb1")]
        sg = sb("sg", [128, 512], F32); r_sg = Res("sg")
        vsn = sb("vsn", [128, 512], BF16); r_vsn = Res("vsn")
        mixed = sb("mixed", [128, D], BF16); r_mixed = Res("mixed")
        mixedT = sb("mixedT", [128, 8, 128], BF16); r_mixedT = Res("mixedT")
        YB = sb("YB", [128, 2048], BF16)
        Pm = YB[:, 0:1024]; r_Pm = Res("Pm")
        pT = YB[:, 1024:2048].rearrange("p (k t) -> p k t", k=8); r_pT = Res("pT")
        memT = YB[:, :].rearrange("p (k m) -> p k m", k=8)
        hb2 = sb("hb2", [128, D], BF16); r_hb2 = Res("hb2")
        hT2 = sb("hT2", [128, 8, 128], BF16); r_hT2 = Res("hT2")
        qx = sb("qx", [128, D], BF16); r_qx = Res("qx")
        qxT = sb("qxT", [128, 8, 128], BF16); r_qxT = Res("qxT")
        smY = sb("smallY", [128, 64], F32); r_smY = Res("smallY")
        memkT = sb("memkT", [128, 8, 256], BF16); r_memkT = Res("memkT")
        memv = sb("memv", [128, 2, D], BF16); r_memv = Res("memv")
        cosT = sb("cosT", [128, NCH, 64], F32)
        sinT = sb("sinT", [128, NCH, 64], F32)
        r_rope = Res("rope")
        NWp = [sb("NWp%d" % i, [128, D], F32) for i in range(2)]
        NWp.append(NWp[0])
        r_NWp = [Res("NWp0"), Res("NWp1")]
        r_NWp.append(r_NWp[0])
        NWs = sb("NWs", [128, 512], F32)
        DT = sb("DT", [128, 4, 128], F32)
        Ab = sb("Ab", [128, 4, 128], F32)
        Bb = sb("Bb", [128, 4, 128], F32)
        KF = sb("KF", [128, 4, 128], F32)
        r_tab = Res("tab")
        sguwT = sb("sguwT", [128, 4, 128], BF16)
        ident = sb("ident", [128, 128], BF16)
        identf = sb("identf", [128, 128], F32)
        sm = sb("small", [128, 64], F32)
        r_sm = Res("small")
        cs = sb("const", [128, 96], F32)
        nwT = sb("nwT", [128, 64], F32)
        invf = sb("invf", [128, 64], F32)
        posb = sb("posb", [128, NCH], F32)

        PS = st.enter_context(nc.psum_tensor("PS", [128, 4096], F32))
        r_bank = [Res("bank%d" % b) for b in range(8)]

        def bank(b, n=1):
            return PS[:, b * 512:(b + n) * 512]

        def bankbf(b):
            return PS[:, b * 512:(b + 1) * 512].bitcast(BF16)

        C_PIDX, C_M05, C_LG, C_KFS, C_KBS, C_GF, C_GB, C_T, C_SGUB, C_META, C_HALF = 0, 1, 8, 16, 20, 24, 28, 32, 40, 48, 56

        ctr = [0]

        def key(prefix):
            ctr[0] += 1
            return "%s%d" % (prefix, ctr[0])

        def rstd_ops(ms_col, out_col, eps, smt=None, r_smt=None):
            smt = sm if smt is None else smt
            r_smt = r_sm if r_smt is None else r_smt
            P.op("pool", lambda e: e.tensor_scalar(smt[:, out_col:out_col + 1], smt[:, ms_col:ms_col + 1], eps, None, ALU.add),
                 reads=[r_smt], writes=[r_smt])
            P.op("pool", lambda e: e.tensor_tensor(smt[:, out_col:out_col + 1], smt[:, out_col:out_col + 1], cs[:, C_M05:C_M05 + 1], ALU.pow),
                 reads=[r_smt], writes=[r_smt])

        def transposes(src, r_src, dst_views, r_dst, tb, nblk=8, evac="act"):
            tr = bankbf(tb)
            for j in range(nblk):
                P.op("pe", lambda e, j=j: e.transpose(tr[:, j * 128:(j + 1) * 128], src[:, j * 128:(j + 1) * 128], ident[:]),
                     reads=[r_src], writes=[r_bank[tb]])
            for (dst, lo, hi, eng) in dst_views:
                if eng == "act":
                    P.op("act", lambda e, dst=dst, lo=lo, hi=hi: e.activation(dst, tr[:, lo * 128:hi * 128].rearrange("p (k t) -> p k t", t=128), AF.Copy),
                         reads=[r_bank[tb]], writes=r_dst)
                else:
                    P.op("dve", lambda e, dst=dst, lo=lo, hi=hi: e.tensor_copy(dst, tr[:, lo * 128:hi * 128].rearrange("p (k t) -> p k t", t=128)),
                         reads=[r_bank[tb]], writes=r_dst)

        def prenorm(src, r_src, dstb, r_dstb, slot, junk=None, r_junk=None, smt=None, r_smt=None):
            junk = TD[:] if junk is None else junk
            r_junk = r_TD if r_junk is None else r_junk
            smt_ = sm if smt is None else smt
            r_smt_ = r_sm if r_smt is None else r_smt
            P.op("act", lambda e: e.activation(junk, src, AF.Square, scale=1.0 / 32.0, accum_out=smt_[:, slot:slot + 1]),
                 reads=r_src, writes=[r_junk, r_smt_])
            rstd_ops(slot, slot + 1, EPS, smt_, r_smt_)
            P.op("act", lambda e: e.activation(dstb, src, AF.Copy, scale=smt_[:, slot + 1:slot + 2]),
                 reads=r_src + [r_smt_], writes=r_dstb)

        def postnorm_residual(b0, nwp, r_nwp, xtile, r_x, slot, out_tile=None, r_out=None, junk=None, r_junk=None,
                              smt=None, r_smt=None, tt=None, r_tt=None):
            junk = TD[:] if junk is None else junk
            r_junk = r_TD if r_junk is None else r_junk
            smt_ = sm if smt is None else smt
            r_smt_ = r_sm if r_smt is None else r_smt
            tt_ = TA if tt is None else tt
            r_tt_ = r_TA if r_tt is None else r_tt
            o = bank(b0, 2)
            P.op("act", lambda e: e.activation(junk, o, AF.Square, scale=1.0 / 32.0, accum_out=smt_[:, slot:slot + 1]),
                 reads=[r_bank[b0], r_bank[b0 + 1]], writes=[r_junk, r_smt_])
            rstd_ops(slot, slot + 1, EPS, smt_, r_smt_)
            P.op("dve", lambda e: e.scalar_tensor_tensor(tt_[:], o, smt_[:, slot + 1:slot + 2], nwp[:], ALU.mult, ALU.mult),
                 reads=[r_bank[b0], r_bank[b0 + 1], r_smt_, r_nwp], writes=[r_tt_])
            ot = xtile if out_tile is None else out_tile
            ro = r_x if r_out is None else r_out
            P.op("dve", lambda e: e.tensor_tensor(ot, tt_[:], xtile, ALU.add),
                 reads=[r_tt_] + r_x, writes=ro)

        with nc.allow_non_contiguous_dma(reason="tiny setup loads"):
            P.dma("sp", lambda e: e.dma_start(out=cs[:, C_LG:C_LG + 8], in_=lg_d.partition_broadcast(128)), writes=[r_tab], key="c_lg")
            P.dma("sp", lambda e: e.dma_start(out=cs[:, C_META:C_META + 4], in_=meta), writes=[r_tab], key="c_meta")
            P.dma("sp", lambda e: e.dma_start(out=cs[:, C_SGUB:C_SGUB + 4], in_=sgub_d.rearrange("g p -> p g")), writes=[r_tab], key="c_sgub")
            P.dma("sp", lambda e: e.dma_start(out=NWs[:], in_=sgunw_d.partition_broadcast(128)), writes=[r_tab], key="c_nws")
            for i, row in enumerate((1, 3)):
                P.dma("sp", lambda e, i=i, row=row: e.dma_start(out=NWp[i][:], in_=norm_w[row].partition_broadcast(128)), writes=[r_NWp[i]], key="c_nwp%d" % i)
            P.dma("sp", lambda e: e.dma_start(out=TA[0:56, 0:128], in_=norm_w.rearrange("n (k p) -> (n k) p", p=128)), writes=[r_TA], key="c_nw")
            P.dma("sp", lambda e: e.dma_start(out=TA[56:60, 0:128], in_=gnw_d.rearrange("(k p) -> k p", p=128)), writes=[r_TA], key="c_gnw")
            P.dma("sp", lambda e: e.dma_start(out=TB[:, 0:512].rearrange("p (g q) -> p g q", g=4), in_=sguw_d.rearrange("g p q -> p g q")), writes=[r_TB], key="c_sguw")

        P.op("pool", lambda e: e.iota(identf[:], pattern=[[1, 128]], base=0, channel_multiplier=-1, allow_small_or_imprecise_dtypes=True), writes=[r_tab])
        P.op("pool", lambda e: e.iota(cs[:, C_PIDX:C_PIDX + 1], pattern=[[0, 1]], base=0, channel_multiplier=1, allow_small_or_imprecise_dtypes=True), writes=[r_tab])
        P.op("pool", lambda e: e.iota(posb[:], pattern=[[128, NCH]], base=0, channel_multiplier=1, allow_small_or_imprecise_dtypes=True), writes=[r_tab])
        P.op("pool", lambda e: e.iota(TC[:, 0:128], pattern=[[1, 128]], base=0, channel_multiplier=0, allow_small_or_imprecise_dtypes=True), writes=[r_TC])
        P.op("dve", lambda e: e.memset(cs[:, C_M05:C_M05 + 1], -0.5), writes=[r_tab])
        P.op("dve", lambda e: e.memset(cs[:, C_HALF:C_HALF + 1], 0.5), writes=[r_tab])
        P.op("dve", lambda e: e.tensor_scalar(TC[:, 128:256], identf[:], 0.0, None, ALU.max), reads=[r_tab], writes=[r_TC])
        P.op("dve", lambda e: e.tensor_scalar(TC[:, 256:384], identf[:], -1.0, 0.0, ALU.mult, ALU.max), reads=[r_tab], writes=[r_TC])
        P.op("dve", lambda e: e.tensor_scalar(ident[:], identf[:], 0.0, None, ALU.is_equal), reads=[r_tab], writes=[r_tab])
        P.op("dve", lambda e: e.tensor_scalar(identf[:], identf[:], 0.0, None, ALU.is_equal), reads=[r_tab], writes=[r_tab])
        for h in range(4):
            lf = cs[:, C_LG + h:C_LG + h + 1]
            lb = cs[:, C_LG + 4 + h:C_LG + 5 + h]
            P.op("dve", lambda e, lf=lf: e.tensor_scalar(TD[:, 0:128], TC[:, 128:256], lf, None, ALU.mult), reads=[r_TC, r_tab], writes=[r_TD])
            P.op("dve", lambda e, lb=lb: e.scalar_tensor_tensor(TD[:, 0:128], TC[:, 256:384], lb, TD[:, 0:128], ALU.mult, ALU.add), reads=[r_TC, r_tab, r_TD], writes=[r_TD])
            P.op("act", lambda e, h=h: e.activation(DT[:, h, :], TD[:, 0:128], AF.Exp), reads=[r_TD], writes=[r_tab])
            P.op("dve", lambda e, h=h: e.tensor_scalar(DT[:, h, :], DT[:, h, :], KSCALE, None, ALU.mult), reads=[r_tab], writes=[r_tab])
            P.op("dve", lambda e, lf=lf: e.tensor_scalar(TD[:, 128:256], TC[:, 0:128], 1.0, lf, ALU.add, ALU.mult), reads=[r_TC, r_tab], writes=[r_TD])
            P.op("act", lambda e, h=h: e.activation(Ab[:, h, :], TD[:, 128:256], AF.Exp), reads=[r_TD], writes=[r_tab])
            P.op("dve", lambda e: e.tensor_scalar(TD[:, 256:384], TC[:, 0:128], -1.0, 128.0, ALU.mult, ALU.add), reads=[r_TC], writes=[r_TD])
            P.op("dve", lambda e, lb=lb: e.tensor_scalar(TD[:, 256:384], TD[:, 256:384], lb, None, ALU.mult), reads=[r_TD, r_tab], writes=[r_TD])
            P.op("act", lambda e, h=h: e.activation(Bb[:, h, :], TD[:, 256:384], AF.Exp), reads=[r_TD], writes=[r_tab])
        P.op("dve", lambda e: e.tensor_scalar(cs[:, C_T:C_T + 1], cs[:, C_PIDX:C_PIDX + 1], -1.0, 127.0, ALU.mult, ALU.add), reads=[r_tab], writes=[r_tab])
        P.op("dve", lambda e: e.tensor_scalar(cs[:, C_KFS:C_KFS + 4], cs[:, C_LG:C_LG + 4], cs[:, C_T:C_T + 1], None, ALU.mult), reads=[r_tab], writes=[r_tab])
        P.op("dve", lambda e: e.tensor_scalar(cs[:, C_KBS:C_KBS + 4], cs[:, C_LG + 4:C_LG + 8], cs[:, C_PIDX:C_PIDX + 1], None, ALU.mult), reads=[r_tab], writes=[r_tab])
        P.op("dve", lambda e: e.tensor_scalar(cs[:, C_GF:C_GF + 8], cs[:, C_LG:C_LG + 8], 128.0, None, ALU.mult), reads=[r_tab], writes=[r_tab])
        P.op("act", lambda e: e.activation(cs[:, C_KFS:C_KFS + 16], cs[:, C_KFS:C_KFS + 16], AF.Exp), reads=[r_tab], writes=[r_tab])
        P.op("dve", lambda e: e.tensor_scalar(cs[:, C_KFS:C_KFS + 8], cs[:, C_KFS:C_KFS + 8], KSCALE, None, ALU.mult), reads=[r_tab], writes=[r_tab])
        P.op("dve", lambda e: e.tensor_copy(KF[:], cs[:, C_KFS:C_KFS + 4].unsqueeze(2).to_broadcast([128, 4, 128])), reads=[r_tab], writes=[r_tab])
        P.op("dve", lambda e: e.memset(invf[:, 0:1], 1.0), writes=[r_tab])
        for kk in range(6):
            w = 1 << kk
            r = float(np.float32(10000.0 ** (-w / 64.0)))
            P.op("dve", lambda e, w=w, r=r: e.tensor_scalar(invf[:, w:2 * w], invf[:, 0:w], r, None, ALU.mult), reads=[r_tab], writes=[r_tab])
        P.op("pe", lambda e: e.transpose(bank(0)[:, 0:60], TA[0:60, 0:128], identf[0:60, 0:60]), reads=[r_TA, r_tab], writes=[r_bank[0]])
        P.op("dve", lambda e: e.tensor_copy(nwT[:, 0:60], bank(0)[:, 0:60]), reads=[r_bank[0]], writes=[r_tab])
        P.op("dve", lambda e: e.memset(nwT[:, 60:64], 0.5), reads=[r_tab], writes=[r_tab])
        P.op("dve", lambda e: e.tensor_scalar(nwT[:, 56:60], nwT[:, 56:60], 0.5, None, ALU.mult), reads=[r_tab], writes=[r_tab])
        P.op("dve", lambda e: e.tensor_copy(hb[:, 0:512], TB[:, 0:512]), reads=[r_TB], writes=[r_hb])
        for g in range(4):
            P.op("pe", lambda e, g=g: e.transpose(bankbf(1)[:, g * 128:(g + 1) * 128], hb[:, g * 128:(g + 1) * 128], ident[:]), reads=[r_hb, r_tab], writes=[r_bank[1]])
        P.op("dve", lambda e: e.tensor_copy(sguwT[:], bankbf(1)[:, 0:512].rearrange("p (g t) -> p g t", g=4)), reads=[r_bank[1]], writes=[r_tab])

        stage_f = [TA, TB, TC, TD]
        r_stage_f = [r_TA, r_TB, r_TC, r_TD]
        stage_b = [qk, mixed, hb, qx]
        r_stage_b = [r_qk, r_mixed, r_hb, r_qx]
        r_scr = {}
        last_store = {}
        pieces = []

        def nwcol(n):
            return lambda k: nwT[:, n * 8 + k:n * 8 + k + 1]

        def add_plain(src, scr, rows, cols, scale_col, rname):
            for k in range(rows // 128):
                for cb in range(cols // 1024):
                    last = (k == rows // 128 - 1) and (cb == cols // 1024 - 1)
                    pieces.append((src[k * 128:(k + 1) * 128, cb * 1024:(cb + 1) * 1024], 1024, scale_col(k),
                                   (lambda sbb, k=k, cb=cb, scr=scr: [(scr[k * 128:(k + 1) * 128, cb * 1024:(cb + 1) * 1024], sbb[:])]), rname, last))

        add_plain(w_in, s_win, D, 3072, nwcol(0), "s_win")
        add_plain(w_out, s_wout, D, D, lambda k: nwT[:, 56 + k:57 + k], "s_wout")
        add_plain(wq_d, s_wq, D, D, nwcol(2), "s_wq")
        add_plain(wo_d, s_wo, D, D, lambda k: 1.0, "s_wo")
        add_plain(wkv_d, s_wkv, D, 2 * D, nwcol(4), "s_wkv")
        for k in range(8):
            for cb in range(6):
                wcols = 1024 if cb < 5 else 512

                def st(sbb, k=k, cb=cb, wcols=wcols):
                    res = []
                    for j in range(wcols // 128):
                        col = cb * 1024 + j * 128
                        gu, f = (0, col // 128) if col < DFF else (1, (col - DFF) // 128)
                        res.append((s_wgu[f, :, k, gu, :], sbb[:, j * 128:(j + 1) * 128]))
                    return res
                pieces.append((wgu_d[k * 128:(k + 1) * 128, cb * 1024:cb * 1024 + wcols], wcols, nwT[:, 5 * 8 + k:5 * 8 + k + 1], st, "s_wgu", k == 7 and cb == 5))
        add_plain(wdn_d, s_wdn, DFF, D, lambda k: 0.5, "s_wdn")

        NSL, LA = 4, 3
        for t in range(len(pieces) + LA):
            if t < len(pieces):
                src, wcols, sc, stf, rname, last = pieces[t]
                i = t % NSL
                P.dma("sp", lambda e, src=src, wcols=wcols, i=i: e.dma_start(out=stage_f[i][:, 0:wcols], in_=src), writes=[r_stage_f[i]], key="pl%d" % i)
            t2 = t - LA
            if t2 >= 0:
                src, wcols, sc, stf, rname, last = pieces[t2]
                i = t2 % NSL
                sf, rf, sbb, rb = stage_f[i], r_stage_f[i], stage_b[i], r_stage_b[i]
                if t2 % 2 == 0:
                    P.op("act", lambda e, sf=sf, sbb=sbb, sc=sc, wcols=wcols: e.activation(sbb[:, 0:wcols], sf[:, 0:wcols], AF.Copy, scale=sc), reads=[rf, r_tab], writes=[rb])
                else:
                    P.op("dve", lambda e, sf=sf, sbb=sbb, sc=sc, wcols=wcols: e.tensor_scalar(sbb[:, 0:wcols], sf[:, 0:wcols], sc, None, ALU.mult), reads=[rf, r_tab], writes=[rb])
                for (dst, srcv) in stf(sbb):
                    last_store[i] = P.dma("pool", lambda e, dst=dst, srcv=srcv: e.dma_start(out=dst, in_=srcv), reads=[rb], key="ps%d" % i)
                if last:
                    r_scr[rname] = list(last_store.values())

        for i in range(2):
            P.dma("sp", lambda e, i=i: e.dma_start(out=win_v[:, i * 4:(i + 1) * 4, :], in_=s_win[i * 512:(i + 1) * 512, :].rearrange("(k p) c -> p k c", p=128)),
                  after=r_scr["s_win"], writes=[r_win[i]], key="ld_win%d" % i)
        P.dma("sp", lambda e: e.dma_start(out=wout_v, in_=s_wout.rearrange("(k p) c -> p k c", p=128)), after=r_scr["s_wout"], writes=[r_wout], key="ld_wout")
        P.dma("sp", lambda e: e.dma_start(out=wq_v, in_=s_wq.rearrange("(k p) c -> p k c", p=128)), after=r_scr["s_wq"], writes=[r_wq], key="ld_wq")
        P.dma("sp", lambda e: e.dma_start(out=wo_v, in_=s_wo.rearrange("(k p) c -> p k c", p=128)), after=r_scr["s_wo"], writes=[r_wo], key="ld_wo")

        def rope_tables(u):
            off = cs[:, C_META + 1 + u:C_META + 2 + u]
            assert NCH % 16 == 0
            for half in range(NCH // 16):
                c0 = half * 16
                P.op("dve", lambda e, c0=c0, off=off: e.tensor_scalar(sm[:, 32:48], posb[:, c0:c0 + 16], off, None, ALU.add), reads=[r_tab, r_sm], writes=[r_sm])
                angv = TC[:].rearrange("p (c j) -> p c j", j=64)
                P.op("dve", lambda e, angv=angv: e.tensor_tensor(angv, sm[:, 32:48].unsqueeze(2).to_broadcast([128, 16, 64]),
                                                                invf[:].unsqueeze(1).to_broadcast([128, 16, 64]), ALU.mult),
                     reads=[r_sm, r_tab], writes=[r_TC])
                for which, tbl in ((0, sinT), (1, cosT)):
                    shift = 0.0 if which == 0 else 0.25
                    P.op("dve", lambda e, shift=shift: e.tensor_scalar(TD[:], TC[:], 1.0 / (2.0 * math.pi), shift, ALU.mult, ALU.add), reads=[r_TC], writes=[r_TD])
                    P.op("dve", lambda e: e.tensor_copy(TA[:].bitcast(I32), TD[:]), reads=[r_TD], writes=[r_TA])
                    P.op("dve", lambda e: e.tensor_copy(TB[:], TA[:].bitcast(I32)), reads=[r_TA], writes=[r_TB])
                    P.op("dve", lambda e: e.tensor_tensor(TD[:], TD[:], TB[:], ALU.subtract), reads=[r_TD, r_TB], writes=[r_TD])
                    P.op("dve", lambda e: e.tensor_scalar(TB[:], TD[:], 0.5, None, ALU.is_gt), reads=[r_TD], writes=[r_TB])
                    P.op("dve", lambda e: e.tensor_tensor(TD[:], TD[:], TB[:], ALU.subtract), reads=[r_TD, r_TB], writes=[r_TD])
                    P.op("dve", lambda e: e.tensor_scalar(TB[:], TD[:], -0.5, None, ALU.is_lt), reads=[r_TD], writes=[r_TB])
                    P.op("dve", lambda e: e.tensor_tensor(TD[:], TD[:], TB[:], ALU.add), reads=[r_TD, r_TB], writes=[r_TD])
                    P.op("act", lambda e, tbl=tbl, c0=c0: e.activation(tbl[:, c0:c0 + 16, :], TD[:].rearrange("p (c j) -> p c j", j=64), AF.Sin, scale=6.283185),
                         reads=[r_TD], writes=[r_rope])

        def load_x(src_ap, i, q="pool"):
            P.dma(q, lambda e: e.dma_start(out=xc[i][:], in_=src_ap), writes=[r_xc[i]], key="ldx%d_%s" % (i, q))

        def zproj(ncols_blocks):
            for n in ncols_blocks:
                b = 2 + n
                for k in range(8):
                    P.op("pe", lambda e, n=n, k=k, b=b: e.matmul(bank(b), hT[:, k, :], win_v[:, k, n * 512:(n + 1) * 512], start=(k == 0), stop=(k == 7)),
                         reads=[r_hT, r_win[k // 4]], writes=[r_bank[b]])

        def rotary(b0, nblk, c, dst, tA=None, r_tA=None, tB=None, r_tB=None, r_dst=None):
            nh = nblk * 4
            w = nblk * 512
            tA = TA[:, 0:w] if tA is None else tA
            tB = TB[:, 0:w] if tB is None else tB
            r_tA = r_TA if r_tA is None else r_tA
            r_tB = r_TB if r_tB is None else r_tB
            r_dst = r_qk if r_dst is None else r_dst
            z3 = bank(b0, nblk).rearrange("p (g j) -> p g j", j=64)
            z4 = bank(b0, nblk).rearrange("p (h t j) -> p h t j", t=2, j=64)
            A3 = tA.rearrange("p (g j) -> p g j", j=64)
            A4 = tA.rearrange("p (h t j) -> p h t j", t=2, j=64)
            B4 = tB.rearrange("p (h t j) -> p h t j", t=2, j=64)
            d4 = dst.rearrange("p (h t j) -> p h t j", t=2, j=64)
            cosb = cosT[:, c, :].unsqueeze(1).to_broadcast([128, nh * 2, 64])
            sinb = sinT[:, c, :].unsqueeze(1).to_broadcast([128, nh, 64])
            rb = [r_bank[b0 + i] for i in range(nblk)]
            P.op("dve", lambda e: e.tensor_tensor(A3, z3, cosb, ALU.mult), reads=rb + [r_rope], writes=[r_tA])
            P.op("dve", lambda e: e.tensor_tensor(B4[:, :, 0, :], z4[:, :, 1, :], sinb, ALU.mult), reads=rb + [r_rope], writes=[r_tB])
            P.op("dve", lambda e: e.tensor_tensor(B4[:, :, 1, :], z4[:, :, 0, :], sinb, ALU.mult), reads=rb + [r_rope], writes=[r_tB])
            P.op("dve", lambda e: e.tensor_tensor(d4[:, :, 0, :], A4[:, :, 0, :], B4[:, :, 0, :], ALU.subtract), reads=[r_tA, r_tB], writes=[r_dst])
            P.op("dve", lambda e: e.tensor_tensor(d4[:, :, 1, :], A4[:, :, 1, :], B4[:, :, 1, :], ALU.add), reads=[r_tA, r_tB], writes=[r_dst])

        def phase_a(u, first_zero):
            if first_zero:
                P.op("dve", lambda e: e.memset(Sst[:], 0.0), writes=[r_S])
            else:
                P.op("dve", lambda e: e.tensor_scalar(Sst[:], Sst[:], cs[:, C_META:C_META + 1], None, ALU.mult), reads=[r_tab, r_S], writes=[r_S])
            SB = [dict(hb=hb, r_hb=r_hb, hT=hT, r_hT=r_hT, kr=qk[:, 512:1024], r_kr=r_qk, vb=vb[:], r_vb=r_vb, kf=kfb[:], r_kf=r_kfb,
                       tA=TA[:, 0:512], r_tA=r_TA, tB=TB[:, 0:512], r_tB=r_TB, junk=TD[:], r_junk=r_TD, sm=sm, r_sm=r_sm, tr=0, bk=3, bv=4, bw=5),
                  dict(hb=hb2, r_hb=r_hb2, hT=hT2, r_hT=r_hT2, kr=qx[:, 512:1024], r_kr=r_qx, vb=Pm[:, 0:512], r_vb=r_Pm, kf=mixed[:, 0:512], r_kf=r_mixed,
                       tA=TC[:, 0:512], r_tA=r_TC, tB=TC[:, 512:1024], r_tB=r_TC, junk=YB[:, 1024:2048], r_junk=r_pT, sm=smY, r_sm=r_smY, tr=1, bk=6, bv=7, bw=2)]
            load_x(xs[u, (NCH - 1) * 128:NCH * 128, :], 0)
            load_x(xs[u, (NCH - 2) * 128:(NCH - 1) * 128, :], 1)

            def front(c, st):
                B = SB[st]
                prenorm(xc[st][:], [r_xc[st]], B["hb"][:], [B["r_hb"]], 0, junk=B["junk"], r_junk=B["r_junk"], smt=B["sm"], r_smt=B["r_sm"])
                if c - 2 >= 0:
                    load_x(xs[u, (c - 2) * 128:(c - 1) * 128, :], st)
                transposes(B["hb"], B["r_hb"], [(B["hT"][:, 0:4, :], 0, 4, "act"), (B["hT"][:, 4:8, :], 4, 8, "dve")], [B["r_hT"]], B["tr"])
                P.dma("sp", lambda e: e.dma_start(out=s_hT[u, c].rearrange("p (k t) -> p k t", k=8), in_=B["hT"][:]), reads=[B["r_hT"]], writes=[r_scr_h[u][c]], key="sth%d" % st)
                for n, b in ((1, B["bk"]), (2, B["bv"])):
                    for k in range(8):
                        P.op("pe", lambda e, n=n, k=k, b=b: e.matmul(bank(b), B["hT"][:, k, :], win_v[:, k, n * 512:(n + 1) * 512], start=(k == 0), stop=(k == 7)),
                             reads=[B["r_hT"], r_win[k // 4]], writes=[r_bank[b]])
                rotary(B["bk"], 1, c, B["kr"], tA=B["tA"], r_tA=B["r_tA"], tB=B["tB"], r_tB=B["r_tB"], r_dst=B["r_kr"])
                P.op("act", lambda e: e.activation(B["vb"], bank(B["bv"]), AF.Copy), reads=[r_bank[B["bv"]]], writes=[B["r_vb"]])
                P.dma("sp", lambda e: e.dma_start(out=s_kv[u, c][:, 0:512], in_=B["kr"]), reads=[B["r_kr"]], writes=[r_scr_k[u][c]], key="stk%d" % st)
                P.dma("sp", lambda e: e.dma_start(out=s_kv[u, c][:, 512:1024], in_=B["vb"]), reads=[B["r_vb"]], writes=[r_scr_v[u][c]], key="stv%d" % st)
                P.op("dve", lambda e: e.tensor_tensor(B["kf"].rearrange("p (h t) -> p h t", t=128), B["kr"].rearrange("p (h t) -> p h t", t=128),
                                                      cs[:, C_KBS:C_KBS + 4].unsqueeze(2).to_broadcast([128, 4, 128]), ALU.mult),
                     reads=[B["r_kr"], r_tab], writes=[B["r_kf"]])
                for h in range(4):
                    hs = slice(h * 128, (h + 1) * 128)
                    P.op("pe", lambda e, hs=hs: e.matmul(bank(B["bw"])[:, hs], B["kf"][:, hs], B["vb"][:, hs], start=True, stop=True),
                         reads=[B["r_kf"], B["r_vb"]], writes=[r_bank[B["bw"]]])

            def tail(c, st):
                B = SB[st]
                P.op("act", lambda e: e.activation(Rb[st][:], Sst[:], AF.Copy), reads=[r_S], writes=[r_Rb[st]])
                P.dma("sp", lambda e: e.dma_start(out=s_rb[u, c], in_=Rb[st][:]), reads=[r_Rb[st]], writes=[r_scr_rb[u][c]], key="strb%d" % st)
                for h in range(4):
                    hs = slice(h * 128, (h + 1) * 128)
                    P.op("dve", lambda e, h=h, hs=hs: e.scalar_tensor_tensor(Sst[:, hs], Sst[:, hs], cs[:, C_GB + h:C_GB + h + 1], bank(B["bw"])[:, hs], ALU.mult, ALU.add),
                         reads=[r_S, r_tab, r_bank[B["bw"]]], writes=[r_S])

            for p in range(NCH // 2):
                c0 = NCH - 1 - 2 * p
                F0 = P.record(lambda: front(c0, 0))
                F1 = P.record(lambda: front(c0 - 1, 1))
                P.play(F0, F1)
                tail(c0, 0)
                tail(c0 - 1, 1)

        r_scr_rb = [[Res("s_rb%d_%d" % (u, c)) for c in range(NCH)] for u in range(NU)]
        r_scr_k = [[Res("s_k%d_%d" % (u, c)) for c in range(NCH)] for u in range(NU)]
        r_scr_h = [[Res("s_h%d_%d" % (u, c)) for c in range(NCH)] for u in range(NU)]
        r_scr_v = [[Res("s_v%d_%d" % (u, c)) for c in range(NCH)] for u in range(NU)]
        r_scr_x2 = [[Res("s_x2_%d_%d" % (u, c)) for c in range(NCH)] for u in range(NU)]

        def mem_kv(u):
            for mc in range(2):
                i = mc
                load_x(mems[u, mc * 128:(mc + 1) * 128, :], i)
                prenorm(xc[i][:], [r_xc[i]], hb[:], [r_hb], 0)
                tr = bankbf(0)
                for j in range(8):
                    P.op("pe", lambda e, j=j: e.transpose(tr[:, j * 128:(j + 1) * 128], hb[:, j * 128:(j + 1) * 128], ident[:]), reads=[r_hb], writes=[r_bank[0]])
                P.op("act", lambda e, mc=mc: e.activation(memT[:, :, mc * 128:(mc + 1) * 128], tr.rearrange("p (k t) -> p k t", t=128), AF.Copy),
                     reads=[r_bank[0]], writes=[r_Pm, r_pT])
            stg = [TA[:].bitcast(BF16), TB[:].bitcast(BF16)]
            rstg = [r_TA, r_TB]
            for rnd in range(2):
                for k in range(8):
                    i = k % 2
                    P.dma("sp", lambda e, k=k, i=i: e.dma_start(out=stg[i], in_=s_wkv[k * 128:(k + 1) * 128, :]), after=r_scr["s_wkv"], writes=[rstg[i]], key="ldkv%d" % i)
                    for dq in range(4):
                        dc = rnd * 4 + dq
                        P.op("pe", lambda e, k=k, i=i, dq=dq, dc=dc: e.matmul(bank(dq)[:, 0:256], stg[i][:, dc * 128:(dc + 1) * 128], memT[:, k, :], start=(k == 0), stop=(k == 7)),
                             reads=[rstg[i], r_Pm, r_pT], writes=[r_bank[dq]])
                    if rnd == 0:
                        for mc in range(2):
                            for hf in range(2):
                                b = 4 + mc * 2 + hf
                                P.op("pe", lambda e, k=k, i=i, mc=mc, hf=hf, b=b: e.matmul(bank(b), memT[:, k, mc * 128:(mc + 1) * 128], stg[i][:, 1024 + hf * 512:1024 + (hf + 1) * 512],
                                                                                   start=(k == 0), stop=(k == 7)),
                                     reads=[rstg[i], r_Pm, r_pT], writes=[r_bank[b]])
                for dq in range(4):
                    dc = rnd * 4 + dq
                    P.op("act", lambda e, dq=dq, dc=dc: e.activation(memkT[:, dc, :], bank(dq)[:, 0:256], AF.Copy), reads=[r_bank[dq]], writes=[r_memkT])
                if rnd == 0:
                    for mc in range(2):
                        P.op("dve", lambda e, mc=mc: e.tensor_copy(memv[:, mc, :], bank(4 + mc * 2, 2)), reads=[r_bank[4 + mc * 2], r_bank[5 + mc * 2]], writes=[r_memv])

        XB_Q, XB_K, XB_V, XB_G = 2, 3, 4, 5
        YB0 = 6

        def chunk_x(u, c):
            i = c % 3
            j = c % 2
            x = xc[i]
            rx = [r_xc[i]]
            if c + 1 < NCH:
                P.dma("sp", lambda e: e.dma_start(out=Rb[1 - j][:], in_=s_rb[u, c + 1]), reads=[r_scr_rb[u][c + 1]], writes=[r_Rb[1 - j]], key="ldrb%d" % (1 - j))

            def zp(n, b):
                for k in range(8):
                    P.op("pe", lambda e, k=k: e.matmul(bank(b), hT[:, k, :], win_v[:, k, n * 512:(n + 1) * 512], start=(k == 0), stop=(k == 7)),
                         reads=[r_hT, r_win[k // 4]], writes=[r_bank[b]])
            P.dma("sp", lambda e: e.dma_start(out=qk[:, 512:1024], in_=s_kv[u, c][:, 0:512]), reads=[r_scr_k[u][c]], writes=[r_qk], key="ldk")
            P.dma("sp", lambda e: e.dma_start(out=vb[:], in_=s_kv[u, c][:, 512:1024]), reads=[r_scr_v[u][c]], writes=[r_vb], key="ldv")
            zp(0, 2); zp(4, 3); zp(5, 4); zp(3, 5)
            if c + 1 < NCH:
                P.dma("sp", lambda e: e.dma_start(out=hT[:], in_=s_hT[u, c + 1].rearrange("p (k t) -> p k t", k=8)), reads=[r_scr_h[u][c + 1]], writes=[r_hT], key="ldh")
            rotary(2, 1, c, qk[:, 0:512])
            P.op("act", lambda e: e.activation(TB[:, 0:512], bank(5), AF.Tanh, scale=0.5), reads=[r_bank[5]], writes=[r_TB])
            P.op("dve", lambda e: e.scalar_tensor_tensor(sg[:], TB[:, 0:512], 1.0, bank(5), ALU.add, ALU.mult), reads=[r_TB, r_bank[5]], writes=[r_sg])
            P.op("dve", lambda e: e.tensor_tensor(kfb[:], qk[:, 512:1024], KF[:].rearrange("p h e -> p (h e)"), ALU.mult), reads=[r_qk, r_tab], writes=[r_kfb])
            tr = bankbf(QKTR[0])
            for jj in range(8):
                P.op("pe", lambda e, jj=jj: e.transpose(tr[:, jj * 128:(jj + 1) * 128], qk[:, jj * 128:(jj + 1) * 128], ident[:]), reads=[r_qk], writes=[r_bank[QKTR[0]]])
            tq = tr[:, 0:512].rearrange("p (h t) -> p h t", t=128)
            for hh in range(2):
                if DUMMY[0] == 2:
                    P.op("dve", lambda e, hh=hh: e.tensor_copy(qkT[:, hh * 4:(hh + 1) * 4, :], tr[:, hh * 512:(hh + 1) * 512].rearrange("p (k t) -> p k t", t=128)),
                         reads=[r_bank[QKTR[0]]], writes=[r_qkT])
                else:
                    P.op("act", lambda e, hh=hh: e.activation((hT2 if DUMMY[0] else qkT)[:, hh * 4:(hh + 1) * 4, :], tr[:, hh * 512:(hh + 1) * 512].rearrange("p (k t) -> p k t", t=128), AF.Copy),
                         reads=[r_bank[QKTR[0]]], writes=[r_qkT])
            P.op("dve", lambda e: e.tensor_tensor(qfT[:], tq, Ab[:], ALU.mult), reads=[r_bank[QKTR[0]], r_tab], writes=[r_qfT])
            P.op("dve", lambda e: e.tensor_tensor(qbT[:], tq, Bb[:], ALU.mult), reads=[r_bank[QKTR[0]], r_tab], writes=[r_qbT])
            for h in range(4):
                P.op("pe", lambda e, h=h: e.matmul(bank(2)[:, h * 128:(h + 1) * 128], qkT[:, 4 + h, :], qkT[:, h, :], start=True, stop=True),
                     reads=[r_qkT], writes=[r_bank[2]])
            P.op("dve", lambda e: e.tensor_tensor(sTm[:], bank(2).rearrange("p (h t) -> p h t", t=128), DT[:], ALU.mult), reads=[r_bank[2], r_tab], writes=[r_sTm])
            zuv = bank(3, 2)
            ruv = [r_bank[3], r_bank[4]]
            P.op("act", lambda e: e.activation(TB[:], zuv, AF.Square, scale=float(np.sqrt(0.044715))), reads=ruv, writes=[r_TB])
            P.op("dve", lambda e: e.scalar_tensor_tensor(TD[:], TB[:], 1.0, zuv, ALU.add, ALU.mult), reads=[r_TB] + ruv, writes=[r_TD])
            P.op("act", lambda e: e.activation(TB[:], TD[:], AF.Tanh, scale=GELU_C), reads=[r_TD], writes=[r_TB])
            P.op("dve", lambda e: e.scalar_tensor_tensor(TD[:], TB[:], 1.0, zuv, ALU.add, ALU.mult), reads=[r_TB] + ruv, writes=[r_TD])
            for h in range(4):
                hs = slice(h * 128, (h + 1) * 128)
                P.op("pe", lambda e, h=h, hs=hs: e.matmul(bank(5)[:, hs], sTm[:, h, :], vb[:, hs], start=True, stop=False), reads=[r_sTm, r_vb], writes=[r_bank[5]])
                P.op("pe", lambda e, h=h, hs=hs: e.matmul(bank(5)[:, hs], qfT[:, h, :], Rf[:, hs], start=False, stop=False), reads=[r_qfT, r_Rf], writes=[r_bank[5]])
                P.op("pe", lambda e, h=h, hs=hs: e.matmul(bank(5)[:, hs], qbT[:, h, :], Rb[j][:, hs], start=False, stop=True), reads=[r_qbT, r_Rb[j]], writes=[r_bank[5]])
            for h in range(4):
                hs = slice(h * 128, (h + 1) * 128)
                P.op("pe", lambda e, hs=hs: e.matmul(bank(2)[:, hs], kfb[:, hs], vb[:, hs], start=True, stop=True), reads=[r_kfb, r_vb], writes=[r_bank[2]])
            for h in range(4):
                hs = slice(h * 128, (h + 1) * 128)
                P.op("dve", lambda e, h=h, hs=hs: e.scalar_tensor_tensor(Sst[:, hs], Sst[:, hs], cs[:, C_GF + h:C_GF + h + 1], bank(2)[:, hs], ALU.mult, ALU.add),
                     reads=[r_S, r_tab, r_bank[2]], writes=[r_S])
            P.op("act", lambda e: e.activation(Rf[:], Sst[:], AF.Copy), reads=[r_S], writes=[r_Rf])
            P.op("dve", lambda e: e.tensor_reduce(sm[:, 2:3], TD[:, 512:1024], AX.X, ALU.add), reads=[r_TD, r_sm], writes=[r_sm])
            P.op("act", lambda e: e.activation(TB[:, 512:1024], TD[:, 512:1024], AF.Square, accum_out=sm[:, 3:4]), reads=[r_TD, r_sm], writes=[r_TB, r_sm])
            P.op("pool", lambda e: e.tensor_scalar(sm[:, 2:4], sm[:, 2:4], 1.0 / 512.0, None, ALU.mult), reads=[r_sm], writes=[r_sm])
            P.op("pool", lambda e: e.tensor_tensor(sm[:, 4:5], sm[:, 2:3], sm[:, 2:3], ALU.mult), reads=[r_sm], writes=[r_sm])
            P.op("pool", lambda e: e.tensor_tensor(sm[:, 5:6], sm[:, 3:4], sm[:, 4:5], ALU.subtract), reads=[r_sm], writes=[r_sm])
            rstd_ops(5, 6, 4.0 * EPS)
            P.op("dve", lambda e: e.tensor_scalar(TB[:, 512:1024], TD[:, 512:1024], sm[:, 2:3], sm[:, 6:7], ALU.subtract, ALU.mult), reads=[r_TD, r_sm], writes=[r_TB])
            P.op("dve", lambda e: e.tensor_tensor(vsn[:], TB[:, 512:1024], NWs[:], ALU.mult), reads=[r_TB, r_tab], writes=[r_vsn])
            y3 = bank(5).rearrange("p (h t) -> p h t", t=128)
            P.op("dve", lambda e: e.tensor_reduce(sm[:, 8:12], y3, AX.X, ALU.add), reads=[r_bank[5], r_sm], writes=[r_sm])
            P.op("act", lambda e: e.activation(TA[:, 512:1024], bank(5), AF.Square), reads=[r_bank[5]], writes=[r_TA])
            P.op("dve", lambda e: e.tensor_reduce(sm[:, 12:16], TA[:, 512:1024].rearrange("p (h t) -> p h t", t=128), AX.X, ALU.add), reads=[r_TA, r_sm], writes=[r_sm])
            P.op("pool", lambda e: e.tensor_scalar(sm[:, 8:16], sm[:, 8:16], 1.0 / 128.0, None, ALU.mult), reads=[r_sm], writes=[r_sm])
            P.op("pool", lambda e: e.tensor_tensor(sm[:, 16:20], sm[:, 8:12], sm[:, 8:12], ALU.mult), reads=[r_sm], writes=[r_sm])
            P.op("pool", lambda e: e.tensor_tensor(sm[:, 20:24], sm[:, 12:16], sm[:, 16:20], ALU.subtract), reads=[r_sm], writes=[r_sm])
            P.op("pool", lambda e: e.tensor_scalar(sm[:, 20:24], sm[:, 20:24], EPS, None, ALU.add), reads=[r_sm], writes=[r_sm])
            P.op("pool", lambda e: e.tensor_tensor(sm[:, 24:28], sm[:, 20:24], cs[:, C_M05:C_M05 + 1].to_broadcast([128, 4]), ALU.pow), reads=[r_sm, r_tab], writes=[r_sm])
            for h in range(4):
                hs = slice(h * 128, (h + 1) * 128)
                P.op("dve", lambda e, h=h, hs=hs: e.tensor_scalar(TA[:, hs], bank(5)[:, hs], sm[:, 8 + h:9 + h], sm[:, 24 + h:25 + h], ALU.subtract, ALU.mult),
                     reads=[r_bank[5], r_sm], writes=[r_TA])
            P.op("dve", lambda e: e.tensor_tensor(mixed[:, 0:512], TA[:, 0:512], sg[:], ALU.mult), reads=[r_TA, r_sg], writes=[r_mixed])
            for g in range(4):
                gs = slice(g * 128, (g + 1) * 128)
                P.op("pe", lambda e, g=g, gs=gs: e.matmul(bank(2)[:, gs], sguwT[:, g, :], vsn[:, gs], start=True, stop=True), reads=[r_vsn, r_tab], writes=[r_bank[2]])
            for g in range(4):
                gs = slice(g * 128, (g + 1) * 128)
                P.op("dve", lambda e, g=g, gs=gs: e.scalar_tensor_tensor(mixed[:, 512 + g * 128:640 + g * 128], bank(2)[:, gs], cs[:, C_SGUB + g:C_SGUB + g + 1],
                                                                        TD[:, gs], ALU.add, ALU.mult),
                     reads=[r_bank[2], r_tab, r_TD], writes=[r_mixed])

        def chunk_y(u, c):
            i = c % 3
            x = xc[i]
            rx = [r_xc[i]]
            ykw = dict(junk=Pm[:], r_junk=r_Pm, smt=smY, r_smt=r_smY)
            transposes(mixed, r_mixed, [(mixedT[:, 0:4, :], 0, 4, "act"), (mixedT[:, 4:8, :], 4, 8, "dve")], [r_mixedT], 1)
            for hf in range(2):
                for k in range(8):
                    P.op("pe", lambda e, hf=hf, k=k: e.matmul(bank(6 + hf), mixedT[:, k, :], wout_v[:, k, hf * 512:(hf + 1) * 512], start=(k == 0), stop=(k == 7)),
                         reads=[r_mixedT, r_wout], writes=[r_bank[6 + hf]])
            postnorm_residual(6, NWp[0], r_NWp[0], x[:], rx, 28, tt=TC, r_tt=r_TC, **ykw)
            if debug:
                P.dma("pool", lambda e: e.dma_start(out=dbg_x1[u, c * 128:(c + 1) * 128, :], in_=x[:]), reads=rx, key="dbgx1")
            prenorm(x[:], rx, hb2[:], [r_hb2], 30, **ykw)
            transposes(hb2, r_hb2, [(hT2[:, 0:4, :], 0, 4, "act"), (hT2[:, 4:8, :], 4, 8, "dve")], [r_hT2], 1)
            for hf in range(2):
                for k in range(8):
                    P.op("pe", lambda e, hf=hf, k=k: e.matmul(bank(6 + hf), hT2[:, k, :], wq_v[:, k, hf * 512:(hf + 1) * 512], start=(k == 0), stop=(k == 7)),
                         reads=[r_hT2, r_wq], writes=[r_bank[6 + hf]])
            P.op("act", lambda e: e.activation(qx[:], bank(6, 2), AF.Copy), reads=[r_bank[6], r_bank[7]], writes=[r_qx])
            transposes(qx, r_qx, [(qxT[:, 0:4, :], 0, 4, "act"), (qxT[:, 4:8, :], 4, 8, "dve")], [r_qxT], 1)
            for h in range(4):
                for dc in range(2):
                    P.op("pe", lambda e, h=h, dc=dc: e.matmul(bank(6 + h // 2)[:, (h % 2) * 256:(h % 2 + 1) * 256], qxT[:, 2 * h + dc, :], memkT[:, 2 * h + dc, :],
                                                         start=(dc == 0), stop=(dc == 1)),
                         reads=[r_qxT, r_memkT], writes=[r_bank[6 + h // 2]])
            sc4 = bank(6, 2).rearrange("p (h m) -> p h m", m=256)
            P.op("dve", lambda e: e.tensor_reduce(smY[:, 48:52], sc4, AX.X, ALU.max), reads=[r_bank[6], r_bank[7], r_smY], writes=[r_smY])
            P.op("pool", lambda e: e.tensor_scalar(smY[:, 52:56], smY[:, 48:52], -1.0 / 16.0, None, ALU.mult), reads=[r_smY], writes=[r_smY])
            for h in range(4):
                P.op("act", lambda e, h=h: e.activation(Pm[:, h * 256:(h + 1) * 256], bank(6, 2)[:, h * 256:(h + 1) * 256], AF.Exp, bias=smY[:, 52 + h:53 + h],
                                                       scale=1.0 / 16.0, accum_out=smY[:, 56 + h:57 + h]),
                     reads=[r_bank[6 + h // 2], r_smY], writes=[r_Pm, r_smY])
            P.op("dve", lambda e: e.reciprocal(smY[:, 60:64], smY[:, 56:60]), reads=[r_smY], writes=[r_smY])
            transposes(Pm, r_Pm, [(pT[:, 0:4, :], 0, 4, "act"), (pT[:, 4:8, :], 4, 8, "dve")], [r_pT], 1)
            for h in range(4):
                for mc in range(2):
                    P.op("pe", lambda e, h=h, mc=mc: e.matmul(bank(6 + h // 2)[:, (h % 2) * 256:(h % 2 + 1) * 256], pT[:, 2 * h + mc, :], memv[:, mc, h * 256:(h + 1) * 256],
                                                         start=(mc == 0), stop=(mc == 1)),
                         reads=[r_pT, r_memv], writes=[r_bank[6 + h // 2]])
            for h in range(4):
                P.op("dve", lambda e, h=h: e.tensor_scalar(qx[:, h * 256:(h + 1) * 256], bank(6, 2)[:, h * 256:(h + 1) * 256], smY[:, 60 + h:61 + h], None, ALU.mult),
                     reads=[r_bank[6 + h // 2], r_smY], writes=[r_qx])
            transposes(qx, r_qx, [(qxT[:, 0:4, :], 0, 4, "act"), (qxT[:, 4:8, :], 4, 8, "dve")], [r_qxT], 1)
            for hf in range(2):
                for k in range(8):
                    P.op("pe", lambda e, hf=hf, k=k: e.matmul(bank(6 + hf), qxT[:, k, :], wo_v[:, k, hf * 512:(hf + 1) * 512], start=(k == 0), stop=(k == 7)),
                         reads=[r_qxT, r_wo], writes=[r_bank[6 + hf]])
            postnorm_residual(6, NWp[1], r_NWp[1], x[:], rx, 34, tt=TC, r_tt=r_TC, **ykw)
            P.dma("sp", lambda e: e.dma_start(out=s_x2[u, c * 128:(c + 1) * 128, :], in_=x[:]), reads=rx, writes=[r_scr_x2[u][c]], key="stx%d" % i)

        def phase_m(u, carry):
            mem_kv(u)
            if carry:
                P.op("dve", lambda e: e.tensor_scalar(Sst[:], Sst[:], cs[:, C_META:C_META + 1], None, ALU.mult), reads=[r_tab, r_S], writes=[r_S])
            else:
                P.op("dve", lambda e: e.memset(Sst[:], 0.0), writes=[r_S])
            P.op("act", lambda e: e.activation(Rf[:], Sst[:], AF.Copy), reads=[r_S], writes=[r_Rf])
            load_x(xs[u, 0:128, :], 0, "sp")
            load_x(xs[u, 128:256, :], 1, "sp")
            P.dma("sp", lambda e: e.dma_start(out=Rb[0][:], in_=s_rb[u, 0]), reads=[r_scr_rb[u][0]], writes=[r_Rb[0]], key="ldrb0")
            P.dma("sp", lambda e: e.dma_start(out=hT[:], in_=s_hT[u, 0].rearrange("p (k t) -> p k t", k=8)), reads=[r_scr_h[u][0]], writes=[r_hT], key="ldh")
            XL = P.record(lambda: chunk_x(u, 0))[:XCUT[0]]
            P.play(XL)
            for c in range(NCH):
                if c + 2 < NCH:
                    load_x(xs[u, (c + 2) * 128:(c + 3) * 128, :], (c + 2) % 3, "sp")
                YL = P.record(lambda: chunk_y(u, c)) if not SKIPY[0] else P.record(lambda: P.dma("pool", lambda e, c=c: e.dma_start(out=s_x2[u, c * 128:(c + 1) * 128, :], in_=xc[c % 2][:]), reads=[r_xc[c % 2]], writes=[r_scr_x2[u][c]], key="stx%d" % (c % 2)))
                XL = []
                if c + 1 < NCH:
                    def bx(c=c):
                        if c + 2 < NCH:
                            pass
                        chunk_x(u, c + 1)
                    XL = P.record(bx)[:XCUT[0]]
                P.play(XL, YL)

        wgu_cnt = [0]
        wdn_cnt = [0]
        alias_done = set()

        def alias(eng, *rs):
            out = []
            for r in rs:
                if (eng, r.name) not in alias_done:
                    alias_done.add((eng, r.name))
                    out.append(r)
            return out

        r_allw = [r_win[0], r_win[1], r_wout, r_wq, r_wo]

        def f_prologue(u, bi, par):
            t0 = bi * 512
            res = []
            for c in range(4):
                def body(c=c):
                    xr = xress[par][:, c, :]
                    P.dma("pool", lambda e: e.dma_start(out=xr, in_=s_x2[u, t0 + c * 128:t0 + (c + 1) * 128, :]), reads=[r_scr_x2[u][bi * 4 + c]],
                          writes=[r_xress[par][c]] + alias("pool", *r_allw), key="ldx2_%d" % c)
                    prenorm(xr, [r_xress[par][c]], hb[:], [r_hb], 0, junk=mixed[:], r_junk=r_mixed)
                    tb = 6 + (c % 2)
                    tr = bankbf(tb)
                    for jj in range(8):
                        P.op("pe", lambda e, jj=jj: e.transpose(tr[:, jj * 128:(jj + 1) * 128], hb[:, jj * 128:(jj + 1) * 128], ident[:]), reads=[r_hb], writes=[r_bank[tb]])
                    P.op("dve", lambda e: e.tensor_copy(h3T_vs[par][:, :, c * 128:(c + 1) * 128], tr.rearrange("p (k t) -> p k t", t=128)),
                         reads=[r_bank[tb]], writes=[r_h3Ts[par][c]] + alias("dve", *r_allw))
                res += P.record(body)
            return res

        def f_gateup(u, bi, par):
            lists = []
            for f in range(NF):
                def body(f=f):
                    s_ = wgu_cnt[0] % 3
                    wgu_cnt[0] += 1
                    P.dma("sp", lambda e: e.dma_start(out=wgu_ring[s_], in_=s_wgu[f]), after=r_scr["s_wgu"], writes=[r_wgur[s_]] + alias("sp", *r_allw), key="ldgu%d" % s_)
                    pb = 2 * (f % 2)
                    for gu in range(2):
                        for k in range(8):
                            P.op("pe", lambda e, gu=gu, k=k: e.matmul(bank(pb + gu), wgu_ring[s_][:, k, gu, :], h3T_vs[par][:, k, :], start=(k == 0), stop=(k == 7)),
                                 reads=[r_wgur[s_]] + r_h3Ts[par], writes=[r_bank[pb + gu]])
                    tt = TC if f % 2 == 0 else TD
                    rtt = r_TC if f % 2 == 0 else r_TD
                    P.op("act", lambda e: e.activation(tt[:, 0:512], bank(pb), AF.Tanh, scale=0.5), reads=[r_bank[pb]], writes=[rtt])
                    P.op("dve", lambda e: e.scalar_tensor_tensor(tt[:, 512:1024], tt[:, 0:512], 1.0, bank(pb), ALU.add, ALU.mult), reads=[rtt, r_bank[pb]], writes=[rtt])
                    P.op("dve", lambda e: e.tensor_tensor(act_v[:, f, :], tt[:, 512:1024], bank(pb + 1), ALU.mult), reads=[rtt, r_bank[pb + 1]],
                         writes=[r_act[f]] + alias("dve", *r_allw))
                lists.append(P.record(body))
            return lists

        def f_down(u, bi):
            for fp in range(NF // 2):
                s_ = wdn_cnt[0] % 3
                wdn_cnt[0] += 1
                P.dma("sp", lambda e, fp=fp, s_=s_: e.dma_start(out=wdn_ring[s_], in_=s_wdn[fp * 256:(fp + 1) * 256, :].rearrange("(f p) c -> p f c", p=128)),
                      after=r_scr["s_wdn"], writes=[r_wdnr[s_]] + alias("sp", *r_allw), key="lddn%d" % s_)
                for fi in range(2):
                    f = fp * 2 + fi
                    for c in range(4):
                        for hf in range(2):
                            P.op("pe", lambda e, f=f, fi=fi, c=c, hf=hf, s_=s_: e.matmul(bank(2 * c + hf), act_v[:, f, c * 128:(c + 1) * 128], wdn_ring[s_][:, fi, hf * 512:(hf + 1) * 512],
                                                                               start=(f == 0), stop=(f == NF - 1)),
                                 reads=[r_act[f], r_wdnr[s_]], writes=[r_bank[2 * c + hf]])

        def f_epilogue(u, bi, par):
            t0 = bi * 512
            lists = []
            for c in range(4):
                def body(c=c):
                    i = c % 2
                    postnorm_residual(2 * c, NWp[2], r_NWp[2], xress[par][:, c, :], [r_xress[par][c]], 0, out_tile=xc[i][:], r_out=[r_xc[i]],
                                      junk=qx[:], r_junk=r_qx, smt=smY, r_smt=r_smY)
                    P.dma("pool", lambda e: e.dma_start(out=outp[u, t0 + c * 128:t0 + (c + 1) * 128, :], in_=xc[i][:]), reads=[r_xc[i]], key="sto%d" % i)
                lists.append(P.record(body))
            return lists

        def phase_f_all():
            blocks = [(u, bi) for u in range(NU) for bi in range(UT // 512)]
            P.play(f_prologue(blocks[0][0], blocks[0][1], 0))
            if FSTOP[0] == 0:
                return
            for n, (u, bi) in enumerate(blocks):
                par = n % 2
                if FSTOP[0] == 3 and n >= 1:
                    return
                gl = f_gateup(u, bi, par)
                epi = f_epilogue(*blocks[n - 1], 1 - par) if n > 0 else [[], [], [], []]
                pro = f_prologue(*blocks[n + 1], 1 - par) if n + 1 < len(blocks) else []
                P.play(epi[0][:ECUT[0]] if n == 1 else epi[0])
                if FSTOP[0] == 4 and n == 1:
                    return
                P.play(gl[0])
                if FSTOP[0] == 5 and n == 1:
                    return
                P.play(epi[1])
                if FSTOP[0] == 6 and n == 1:
                    return
                P.play(gl[1])
                if FSTOP[0] == 7 and n == 1:
                    return
                main = [t for l in gl[2:] for t in l]
                side = epi[2] + epi[3] + pro
                P.play(main, side)
                if FSTOP[0] == 1 or (FSTOP[0] == 8 and n == 1):
                    return
                P.play(P.record(lambda: f_down(u, bi)))
                if FSTOP[0] == 2 or (FSTOP[0] >= 10 and n == FSTOP[0] - 9):
                    return
            for li, l in enumerate(f_epilogue(*blocks[-1], (len(blocks) - 1) % 2)):
                if FSTOP[0] >= 30 and li >= FSTOP[0] - 30:
                    break
                P.play(l)

        STOP = STOPAT[0]
        if STOP >= 1:
            rope_tables(1)
            phase_a(1, True)
        if STOP >= 2:
            rope_tables(0)
            phase_a(0, False)
        if STOP == 3:
            mem_kv(0)
        if STOP >= 4:
            phase_m(0, False)
        if STOP >= 5:
            rope_tables(1)
            phase_m(1, True)
            rope_tables(0)
            phase_a(2, True)
            phase_m(2, False)
        if STOP >= 6:
            P.dma("sp", lambda e: e.dma_start(out=NWp[0][:], in_=norm_w[6].partition_broadcast(128)), writes=[r_NWp[0]], key="c_nwp0")
            phase_f_all()
        with nc.allow_non_contiguous_dma(reason="setup loads / weight scratch layout"):
            P.emit(nc)
    return nc


_NC_CACHE = {}
_DEBUG = [False]


def kernel(x_prompt, x_sample, mem_prompt, mem_sample, norm_w, w_in, ret_log_gamma, ret_gn_w,
           sgu_norm_w, sgu_w, sgu_b, w_out, xa_wq, xa_wkv, xa_wo, ffn_w_gu, ffn_w_down):
    f = lambda a: np.ascontiguousarray(np.asarray(a, dtype=np.float32))
    x_prompt, x_sample, mem_prompt, mem_sample = f(x_prompt), f(x_sample), f(mem_prompt), f(mem_sample)
    shared = {
        "norm_w": f(norm_w)[0], "w_in": f(w_in)[0], "ret_log_gamma": f(ret_log_gamma)[0].reshape(8),
        "ret_gn_w": f(ret_gn_w)[0], "sgu_norm_w": f(sgu_norm_w)[0], "sgu_w": f(sgu_w)[0], "sgu_b": f(sgu_b)[0],
        "w_out": f(w_out)[0], "xa_wq": f(xa_wq)[0], "xa_wkv": f(xa_wkv)[0], "xa_wo": f(xa_wo)[0],
        "ffn_w_gu": f(ffn_w_gu)[0], "ffn_w_down": f(ffn_w_down)[0],
    }
    in_maps = []
    for core in range(8):
        if core < 4:
            xs = np.stack([x_sample[core, :UT], x_sample[core, UT:], x_prompt[core]])
            mm = np.stack([mem_sample[core], mem_sample[core], mem_prompt[core]])
            link = 1.0
        else:
            p0 = 4 + 3 * (core - 4)
            xs = np.stack([x_prompt[p0], x_prompt[p0 + 1], x_prompt[p0 + 2]])
            mm = np.stack([mem_prompt[p0], mem_prompt[p0 + 1], mem_prompt[p0 + 2]])
            link = 0.0
        meta = np.zeros((128, 4), np.float32)
        meta[:, 0] = link
        meta[:, 2] = link * UT
        d = dict(shared)
        d.update({"xs": np.ascontiguousarray(xs), "mems": np.ascontiguousarray(mm), "meta": meta})
        in_maps.append(d)
    if "nc" not in _NC_CACHE:
        _NC_CACHE["nc"] = build_program(debug=_DEBUG[0])
    res = run_bass_kernel_spmd(_NC_CACHE["nc"], in_maps, core_ids=list(range(8)))
    if _DEBUG[0]:
        _DEBUG.append(res)
    y_prompt = np.empty_like(x_prompt)
    y_sample = np.empty_like(x_sample)
    for core in range(8):
        o = np.asarray(res.results[core]["out"], dtype=np.float32)
        if core < 4:
            y_sample[core, :UT] = o[0]
            y_sample[core, UT:] = o[1]
            y_prompt[core] = o[2]
        else:
            p0 = 4 + 3 * (core - 4)
            y_prompt[p0], y_prompt[p0 + 1], y_prompt[p0 + 2] = o[0], o[1], o[2]
    return (y_prompt, y_sample)
```
